# Optimizing a Trainium2 kernel written in Bass

```python
import jax, jax.numpy as jnp
from jax import lax
import numpy as np

D_MODEL = 2048
BATCH = 4
SEQ = 8192
DEPTH = 1
DEC_BATCH = 32
DEC_SEQ = 32
PAST_LEN = 4096

CHUNK = 64
EPS = 1e-6
RET_HEADS = 8
RET_DK = 256
RET_DV = 512
RET_QK = RET_HEADS * RET_DK
RET_V = RET_HEADS * RET_DV
ROPE_THETA = 10000.0
SSM_INNER = 2 * D_MODEL
SSM_HEADDIM = 64
SSM_HEADS = SSM_INNER // SSM_HEADDIM
SSM_GROUPS = 8
SSM_STATE = 128
SSM_CONV = 4
SSM_CONV_DIM = SSM_INNER + 2 * SSM_GROUPS * SSM_STATE
D_FF = 5632
PLE_DIM = 256
IN_SPLITS = (RET_QK, RET_QK, RET_V, RET_V, SSM_INNER, SSM_CONV_DIM, SSM_HEADS, D_MODEL, D_MODEL)
N_IN = sum(IN_SPLITS)

kernel_name = "hybrid_retention_ssd_macaron_stream_step"


def _rmsnorm(x, g):
    xf = x.astype(jnp.float32)
    y = xf * lax.rsqrt(jnp.mean(xf * xf, axis=-1, keepdims=True) + EPS)
    return (y * g.astype(jnp.float32)).astype(x.dtype)


def _swiglu(h, w_gu, w_down):
    gate, up = jnp.split(h @ w_gu, 2, axis=-1)
    return (jax.nn.silu(gate) * up) @ w_down


def _rope(x, pos):
    half = x.shape[-1] // 2
    inv = ROPE_THETA ** (-jnp.arange(half, dtype=jnp.float32) / half)
    ang = pos.astype(jnp.float32)[:, None] * inv[None, :]
    cos = jnp.cos(ang)[None, :, None, :]
    sin = jnp.sin(ang)[None, :, None, :]
    xf = x.astype(jnp.float32)
    x1, x2 = xf[..., :half], xf[..., half:]
    return jnp.concatenate([x1 * cos - x2 * sin, x2 * cos + x1 * sin], axis=-1)


def _to_chunks(a, c):
    b, L = a.shape[:2]
    a = a.reshape((b, L // c, c) + a.shape[2:])
    return jnp.moveaxis(a, 1, 0)


def _from_chunks(a):
    a = jnp.moveaxis(a, 0, 1)
    return a.reshape((a.shape[0], a.shape[1] * a.shape[2]) + a.shape[3:])


def _retention(q, k, v, s0):
    L = q.shape[1]
    c = min(CHUNK, L)
    log_g = jnp.log1p(-jnp.exp2(-5.0 - jnp.arange(RET_HEADS, dtype=jnp.float32)))
    idx = jnp.arange(c, dtype=jnp.float32)
    diff = idx[:, None] - idx[None, :]
    dmat = jnp.where(diff[None] >= 0, jnp.exp(jnp.maximum(diff, 0.0)[None] * log_g[:, None, None]), 0.0)
    q_dec = jnp.exp((idx[:, None] + 1.0) * log_g[None, :])
    k_dec = jnp.exp((c - 1.0 - idx[:, None]) * log_g[None, :])
    c_dec = jnp.exp(c * log_g)

    def step(S, inp):
        qc, kc, vc = inp
        att = jnp.einsum('bihd,bjhd->bhij', qc, kc) * dmat
        o = (jnp.einsum('bhij,bjhe->bihe', att, vc)
             + jnp.einsum('bihd,bhde->bihe', qc * q_dec[None, :, :, None], S))
        S = S * c_dec[None, :, None, None] + jnp.einsum('bjhd,bjhe->bhde', kc * k_dec[None, :, :, None], vc)
        return S, o

    S, o = lax.scan(step, s0, (_to_chunks(q, c), _to_chunks(k, c), _to_chunks(v, c)))
    return _from_chunks(o), S


def _ssd(x, dt, a, bm, cm, s0):
    b, L = x.shape[:2]
    c = min(CHUNK, L)
    r = SSM_HEADS // SSM_GROUPS
    x = x.reshape(b, L, SSM_GROUPS, r, SSM_HEADDIM)
    dt = dt.reshape(b, L, SSM_GROUPS, r)
    a = a.reshape(SSM_GROUPS, r)
    ar = jnp.arange(c)
    mask = (ar[:, None] >= ar[None, :])[None, :, :, None, None]

    def step(S, inp):
        xc, dtc, bc, cc = inp
        acum = jnp.cumsum(dtc * a, axis=1)
        seg = acum[:, :, None] - acum[:, None, :]
        lmat = jnp.where(mask, jnp.exp(jnp.where(mask, seg, 0.0)), 0.0)
        cb = jnp.einsum('bign,bjgn->bijg', cc, bc)
        w = lmat * cb[..., None] * dtc[:, None]
        y = jnp.einsum('bijgr,bjgrp->bigrp', w, xc)
        y = y + jnp.einsum('bign,bgrpn->bigrp', cc, S) * jnp.exp(acum)[..., None]
        wend = jnp.exp(acum[:, -1:] - acum) * dtc
        S = (S * jnp.exp(acum[:, -1])[..., None, None]
             + jnp.einsum('bjgn,bjgr,bjgrp->bgrpn', bc, wend, xc))
        return S, y

    S0 = s0.reshape(b, SSM_GROUPS, r, SSM_HEADDIM, SSM_STATE)
    S, y = lax.scan(step, S0, (_to_chunks(x, c), _to_chunks(dt, c), _to_chunks(bm, c), _to_chunks(cm, c)))
    y = _from_chunks(y).reshape(b, L, SSM_HEADS, SSM_HEADDIM)
    return y, S.reshape(b, SSM_HEADS, SSM_HEADDIM, SSM_STATE)


def _causal_dwconv(xpad, w, bias):
    y = lax.conv_general_dilated(xpad, w.astype(xpad.dtype)[:, None, :], window_strides=(1,), padding='VALID',
                                 dimension_numbers=('NWC', 'WIO', 'NWC'), feature_group_count=xpad.shape[-1])
    return y + bias.astype(xpad.dtype)


def _layer(x, p, pos, s_ret, s_ssm, s_conv,
           g_ffn1, w1_gu, w1_down, g_mix, w_in, ret_gn_g, ret_gn_b, conv_w, conv_b, dt_bias, a_log, d_skip,
           ssm_norm_g, w_br_ret, w_br_ssm, w_out, g_ffn2, w2_gu, w2_down, g_ple, w_ple, w_ple_gate):
    f32 = jnp.float32
    b, L = x.shape[:2]
    x = x + 0.5 * _swiglu(_rmsnorm(x, g_ffn1), w1_gu, w1_down)

    h = _rmsnorm(x, g_mix)
    split_idx = np.cumsum(IN_SPLITS)[:-1].tolist()
    q, k, v, rg, z, xbc, dt_raw, gate_ret, gate_ssm = jnp.split(h @ w_in, split_idx, axis=-1)

    q = _rope(q.reshape(b, L, RET_HEADS, RET_DK), pos)
    k = _rope(k.reshape(b, L, RET_HEADS, RET_DK), pos) * (RET_DK ** -0.5)
    v = v.reshape(b, L, RET_HEADS, RET_DV).astype(f32)
    o, s_ret_new = _retention(q, k, v, s_ret.astype(f32))
    mu = jnp.mean(o, axis=-1, keepdims=True)
    var = jnp.mean(jnp.square(o - mu), axis=-1, keepdims=True)
    on = ((o - mu) * lax.rsqrt(var + EPS)).reshape(b, L, RET_V)
    ret = (jax.nn.silu(rg.astype(f32)) * (on * ret_gn_g.astype(f32) + ret_gn_b.astype(f32))).astype(x.dtype)

    xpad = jnp.concatenate([s_conv.astype(xbc.dtype), xbc], axis=1)
    s_conv_new = xpad[:, -(SSM_CONV - 1):]
    xbc = jax.nn.silu(_causal_dwconv(xpad, conv_w, conv_b))
    xs, bm, cm = jnp.split(xbc, [SSM_INNER, SSM_INNER + SSM_GROUPS * SSM_STATE], axis=-1)
    dt = jax.nn.softplus(dt_raw.astype(f32) + dt_bias.astype(f32))
    a = -jnp.exp(a_log.astype(f32))
    xs4 = xs.reshape(b, L, SSM_HEADS, SSM_HEADDIM).astype(f32)
    y, s_ssm_new = _ssd(xs4, dt, a,
                        bm.reshape(b, L, SSM_GROUPS, SSM_STATE).astype(f32),
                        cm.reshape(b, L, SSM_GROUPS, SSM_STATE).astype(f32),
                        s_ssm.astype(f32))
    y = (y + d_skip.astype(f32)[:, None] * xs4).reshape(b, L, SSM_INNER) * jax.nn.silu(z.astype(f32))
    yg = y.reshape(b, L, SSM_GROUPS, SSM_INNER // SSM_GROUPS)
    yg = yg * lax.rsqrt(jnp.mean(yg * yg, axis=-1, keepdims=True) + EPS)
    ssm = (yg.reshape(b, L, SSM_INNER) * ssm_norm_g.astype(f32)).astype(x.dtype)

    merged = jax.nn.sigmoid(gate_ret) * (ret @ w_br_ret) + jax.nn.sigmoid(gate_ssm) * (ssm @ w_br_ssm)
    x = x + merged @ w_out

    x = x + 0.5 * _swiglu(_rmsnorm(x, g_ffn2), w2_gu, w2_down)

    x = x + (p.astype(x.dtype) @ w_ple) * jax.nn.sigmoid(_rmsnorm(x, g_ple) @ w_ple_gate)
    return x, s_ret_new, s_ssm_new, s_conv_new


def _trunk(x, p, pos, s_ret, s_ssm, s_conv, layer_params, g_final):
    rets, ssms, convs = [], [], []
    for i in range(DEPTH):
        lw = [w[i] for w in layer_params]
        x, r_new, s_new, c_new = _layer(x, p[i], pos, s_ret[i], s_ssm[i], s_conv[i], *lw)
        rets.append(r_new)
        ssms.append(s_new)
        convs.append(c_new)
    return _rmsnorm(x, g_final), jnp.stack(rets), jnp.stack(ssms), jnp.stack(convs)


def setup_inputs(seed: int = 0) -> dict:
    key = jax.random.key(seed)
    ks = iter(jax.random.split(key, 40))
    f32 = jnp.float32

    def nrm(shape, scale):
        return jax.random.normal(next(ks), shape, f32) * scale

    def gain(shape):
        return 1.0 + nrm(shape, 0.02)

    dt0 = jnp.exp(jax.random.uniform(next(ks), (DEPTH, SSM_HEADS), f32, np.log(1e-3), np.log(1e-1)))
    return {
        "x_prompt": nrm((BATCH, SEQ, D_MODEL), 1.0),
        "x_sample": nrm((DEC_BATCH, DEC_SEQ, D_MODEL), 1.0),
        "state_ret": nrm((DEPTH, DEC_BATCH, RET_HEADS, RET_DK, RET_DV), 0.1),
        "state_ssm": nrm((DEPTH, DEC_BATCH, SSM_HEADS, SSM_HEADDIM, SSM_STATE), 0.1),
        "state_conv": nrm((DEPTH, DEC_BATCH, SSM_CONV - 1, SSM_CONV_DIM), 1.0),
        "p_prompt": nrm((DEPTH, BATCH, SEQ, PLE_DIM), 1.0),
        "p_sample": nrm((DEPTH, DEC_BATCH, DEC_SEQ, PLE_DIM), 1.0),
        "g_ffn1": gain((DEPTH, D_MODEL)),
        "w1_gu": nrm((DEPTH, D_MODEL, 2 * D_FF), D_MODEL ** -0.5),
        "w1_down": nrm((DEPTH, D_FF, D_MODEL), D_FF ** -0.5),
        "g_mix": gain((DEPTH, D_MODEL)),
        "w_in": nrm((DEPTH, D_MODEL, N_IN), D_MODEL ** -0.5),
        "ret_gn_g": gain((DEPTH, RET_V)),
        "ret_gn_b": nrm((DEPTH, RET_V), 0.02),
        "conv_w": nrm((DEPTH, SSM_CONV, SSM_CONV_DIM), SSM_CONV ** -0.5),
        "conv_b": nrm((DEPTH, SSM_CONV_DIM), 0.02),
        "dt_bias": dt0 + jnp.log(-jnp.expm1(-dt0)),
        "a_log": jnp.log(jax.random.uniform(next(ks), (DEPTH, SSM_HEADS), f32, 1.0, 16.0)),
        "d_skip": 1.0 + nrm((DEPTH, SSM_HEADS), 0.1),
        "ssm_norm_g": gain((DEPTH, SSM_INNER)),
        "w_br_ret": nrm((DEPTH, RET_V, D_MODEL), RET_V ** -0.5),
        "w_br_ssm": nrm((DEPTH, SSM_INNER, D_MODEL), SSM_INNER ** -0.5),
        "w_out": nrm((DEPTH, D_MODEL, D_MODEL), D_MODEL ** -0.5),
        "g_ffn2": gain((DEPTH, D_MODEL)),
        "w2_gu": nrm((DEPTH, D_MODEL, 2 * D_FF), D_MODEL ** -0.5),
        "w2_down": nrm((DEPTH, D_FF, D_MODEL), D_FF ** -0.5),
        "g_ple": gain((DEPTH, D_MODEL)),
        "w_ple": nrm((DEPTH, PLE_DIM, D_MODEL), PLE_DIM ** -0.5),
        "w_ple_gate": nrm((DEPTH, D_MODEL, D_MODEL), D_MODEL ** -0.5),
        "g_final": gain((D_MODEL,)),
    }


def reference(x_prompt, x_sample, state_ret, state_ssm, state_conv, p_prompt, p_sample,
              g_ffn1, w1_gu, w1_down, g_mix, w_in, ret_gn_g, ret_gn_b, conv_w, conv_b, dt_bias, a_log, d_skip,
              ssm_norm_g, w_br_ret, w_br_ssm, w_out, g_ffn2, w2_gu, w2_down, g_ple, w_ple, w_ple_gate, g_final):
    layer_params = (g_ffn1, w1_gu, w1_down, g_mix, w_in, ret_gn_g, ret_gn_b, conv_w, conv_b, dt_bias, a_log,
                    d_skip, ssm_norm_g, w_br_ret, w_br_ssm, w_out, g_ffn2, w2_gu, w2_down, g_ple, w_ple, w_ple_gate)
    bp, lp = x_prompt.shape[:2]
    bs, ls = x_sample.shape[:2]
    pos_p = jnp.arange(lp, dtype=jnp.int32)
    zr = jnp.zeros((DEPTH, bp, RET_HEADS, RET_DK, RET_DV), jnp.float32)
    zs = jnp.zeros((DEPTH, bp, SSM_HEADS, SSM_HEADDIM, SSM_STATE), jnp.float32)
    zc = jnp.zeros((DEPTH, bp, SSM_CONV - 1, SSM_CONV_DIM), x_prompt.dtype)
    y_prompt, ret_p, ssm_p, conv_p = _trunk(x_prompt, p_prompt, pos_p, zr, zs, zc, layer_params, g_final)
    pos_s = PAST_LEN + jnp.arange(ls, dtype=jnp.int32)
    y_sample, ret_s, ssm_s, conv_s = _trunk(x_sample, p_sample, pos_s, state_ret, state_ssm, state_conv,
                                            layer_params, g_final)
    return (y_prompt, y_sample, ret_p, ssm_p, conv_p, ret_s, ssm_s, conv_s)
```

```python
import numpy as np
from contextlib import ExitStack
import concourse.bass as bass
import concourse.mybir as mybir
from concourse.bass_utils import run_bass_kernel_spmd

F32 = mybir.dt.float32
BF16 = mybir.dt.bfloat16
AF = mybir.ActivationFunctionType
ALU = mybir.AluOpType

D = 2048
DC = 16
FF = 5632
FC = 44
NIN = 26688
RH = 8
SG = 8
EPS = 1e-6
PAST = 4096
OFF_Q, OFF_K, OFF_V, OFF_RG, OFF_Z, OFF_X, OFF_B, OFF_C, OFF_DT, OFF_GR, OFF_GS = (
    0, 2048, 4096, 8192, 12288, 16384, 20480, 21504, 22528, 22592, 24640)
SAME_SYNC = True
WSLOTS = 3
WCOLS = 256


class Buf:
    __slots__ = ("name", "w", "r", "region", "lo", "hi", "dkey", "dcnt")

    def __init__(self, name, region=None, lo=0, hi=1):
        self.name = name
        self.w = None
        self.r = {}
        self.region = region
        self.lo = lo
        self.hi = hi
        self.dkey = None
        self.dcnt = 0
        if region is not None:
            region.append(self)

    def overl(self):
        if self.region is None:
            return (self,)
        return [b for b in self.region if b.lo < self.hi and self.lo < b.hi]


class K:
    def __init__(self, nc, es, dry):
        self.nc = nc
        self.es = es
        self.dry = dry
        self.eng = {"pe": nc.tensor, "act": nc.scalar, "dve": nc.vector, "pool": nc.gpsimd, "sp": nc.sync}
        self.sems = {}
        self.cnt = {}
        self.waited = {e: {} for e in self.eng}
        self.nsem = 0
        self.wreq = []
        self.wpos = 0
        self.wissued = 0
        self.out_events = []
        if not dry:
            for e in ("pe", "act", "dve", "pool"):
                self.sems[e] = es.enter_context(nc.semaphore("s_" + e))
                self.cnt[e] = 0
                self.nsem += 1

    def _wait(self, eng, key, val):
        if key == eng and (eng == "pe" or not SAME_SYNC):
            return
        wd = self.waited[eng]
        if wd.get(key, 0) >= val:
            return
        self.eng[eng].wait_ge(self.sems[key], val)
        wd[key] = val

    def _deps(self, eng, R, W):
        need = {}

        def add(ev):
            if ev is not None and need.get(ev[0], 0) < ev[1]:
                need[ev[0]] = ev[1]
        for b in R:
            for o in b.overl():
                add(o.w)
        for b in W:
            for o in b.overl():
                add(o.w)
                for k_, v_ in o.r.items():
                    add((k_, v_))
        for k_, v_ in need.items():
            self._wait(eng, k_, v_)

    def _record(self, ev, R, W):
        for b in R:
            for o in b.overl():
                if o.r.get(ev[0], 0) < ev[1]:
                    o.r[ev[0]] = ev[1]
        for b in W:
            for o in b.overl():
                o.w = ev
                o.r = {}

    def op(self, eng, fn, R=(), W=()):
        if self.dry:
            return
        self._deps(eng, R, W)
        ins = fn(self.eng[eng])
        self.cnt[eng] += 1
        ins.then_inc(self.sems[eng], 1)
        self._record((eng, self.cnt[eng]), R, W)

    def _dsem(self, b):
        if b.dkey is None:
            b.dkey = "d_" + b.name
            self.sems[b.dkey] = self.es.enter_context(self.nc.semaphore(b.dkey))
            self.nsem += 1
        return b.dkey

    def dma(self, pieces, R=(), W=(), q="sp", is_out=False):
        if self.dry:
            return
        self._deps(q, R, W)
        b = W[0] if W else R[0]
        key = self._dsem(b)
        for (o, i) in pieces:
            self.eng[q].dma_start(out=o, in_=i).then_inc(self.sems[key], 16)
            b.dcnt += 16
        ev = (key, b.dcnt)
        self._record(ev, R, W)
        if is_out:
            self.out_events.append(ev)

    def finish(self):
        if self.dry:
            return
        for ev in self.out_events:
            self._wait("sp", ev[0], ev[1])


def _gammas():
    return 1.0 - np.exp2(-5.0 - np.arange(RH, dtype=np.float64))


def make_consts(kind):
    if kind == "P":
        nq, blk, pos, clen = 1, np.zeros(128, int), np.arange(128), 128
    else:
        nq, blk, pos, clen = 4, np.arange(128) // 32, np.arange(128) % 32, 32
    g = _gammas()
    same = blk[:, None] == blk[None, :]
    ar = np.arange(128)
    triT = ((ar[:, None] <= ar[None, :]) & same).astype(np.float64)
    SL = ((ar[:, None] > ar[None, :]) & same).astype(np.float64)
    maskT = ((ar[None, :] >= ar[:, None]) & same).astype(np.float64)
    dif = np.maximum(ar[None, :] - ar[:, None], 0)
    dmatT = np.stack([np.power(g[h], dif) * maskT * (256.0 ** -0.5) for h in range(RH)], axis=1)
    qdec = np.stack([np.power(g[h], pos + 1.0) for h in range(RH)], axis=0)
    qdec_bc = np.broadcast_to(qdec[None], (128, RH, 128))
    kdec = np.stack([np.power(g[h], clen - 1.0 - pos) * (256.0 ** -0.5) for h in range(RH)], axis=1)
    kdecq = np.stack([kdec * (blk == q)[:, None] for q in range(nq)], axis=1)
    colmask = np.stack([np.broadcast_to((blk == q)[None, :], (128, 128)) for q in range(nq)], axis=1)
    rowmask = np.stack([(blk == q) for q in range(nq)], axis=1).astype(np.float64)
    seqones = np.stack([np.broadcast_to((blk == q)[:, None], (128, 128)) for q in range(nq)], axis=1)
    parts = [("triT", triT), ("SL", SL), ("maskT", maskT), ("dmatT", dmatT.reshape(128, -1)),
             ("qdec", qdec_bc.reshape(128, -1)), ("kdecq", kdecq.reshape(128, -1)),
             ("colmask", colmask.reshape(128, -1)), ("rowmask", rowmask), ("seqones", seqones.reshape(128, -1))]
    offs = {}
    o = 0
    for n, a in parts:
        offs[n] = (o, a.shape[1])
        o += a.shape[1]
    arr = np.concatenate([np.asarray(a, np.float64) for _, a in parts], axis=1).astype(np.float32)
    cdec = [float(np.power(g[h], float(clen))) for h in range(RH)]
    return arr, offs, cdec, nq


CONST_W = 3500


def rope_tables(pos):
    half = 128
    inv = (10000.0 ** (-np.arange(half, dtype=np.float32) / np.float32(half))).astype(np.float32)
    ang = pos.astype(np.float32)[None, :] * inv[:, None]
    return np.cos(ang).astype(np.float32), np.sin(ang).astype(np.float32)


DEBUG_TAPS = []


def build(NPRE, NMAIN, T=256, with_smp=True):
    NS = T // 128
    nc = bass.Bass("TRN2", target_bir_lowering=False)
    npre_tok, nmain_tok = NPRE * T, NMAIN * T

    def din(name, shape, dt=F32):
        return nc.dram_tensor(name, list(shape), dt, kind="ExternalInput").ap()

    def dout(name, shape, dt=F32):
        return nc.dram_tensor(name, list(shape), dt, kind="ExternalOutput").ap()

    def dint(name, shape, dt=BF16):
        return nc.dram_tensor(name, list(shape), dt, kind="Internal").ap()

    I = {}
    I["x_pre"] = din("x_pre", [max(npre_tok, 1), D])
    I["x_main"] = din("x_main", [nmain_tok, D])
    I["p_main"] = din("p_main", [nmain_tok, 256])
    I["x_smp"] = din("x_smp", [128, D])
    I["p_smp"] = din("p_smp", [128, 256])
    I["sret_in"] = din("sret_in", [4, RH, 256, 512])
    I["sssm_in"] = din("sssm_in", [4, 64, 64, 128])
    I["sconv_in"] = din("sconv_in", [12, 6144])
    I["flag"] = din("flag", [128, 1])
    for r, n in (("pre", max(npre_tok, 1)), ("main", nmain_tok), ("smp", 128)):
        I["cos_" + r] = din("cos_" + r, [128, n])
        I["sin_" + r] = din("sin_" + r, [128, n])
    I["constP"] = din("constP", [128, CONST_W])
    I["constS"] = din("constS", [128, CONST_W])
    I["ident"] = din("ident", [128, 128])
    wshapes = {"w1_gu": (D, 2 * FF), "w1_down": (FF, D), "w_in": (D, NIN), "w_br_ret": (4096, D),
               "w_br_ssm": (4096, D), "w_out": (D, D), "w2_gu": (D, 2 * FF), "w2_down": (FF, D),
               "w_ple": (256, D), "w_ple_gate": (D, D)}
    WF = {n: din(n, s) for n, s in wshapes.items()}
    WB = {n: dint(n + "_bf", s) for n, s in wshapes.items()}
    I["gcols"] = din("gcols", [128, 5, DC])
    I["gncols"] = din("gncols", [128, 3, 32])
    I["cwcols"] = din("cwcols", [128, 5, 48])
    I["hvin"] = din("hvin", [128, 3, 64])
    O = {}
    O["y_main"] = dout("y_main", [nmain_tok, D])
    O["y_smp"] = dout("y_smp", [128, D])
    O["ret_main"] = dout("ret_main", [RH, 256, 512])
    O["ssm_main"] = dout("ssm_main", [64 * 64, 128])
    O["conv_main"] = dout("conv_main", [3, 6144])
    O["ret_smp"] = dout("ret_smp", [4, RH, 256, 512])
    O["ssm_smp"] = dout("ssm_smp", [4, 64 * 64, 128])
    O["conv_smp"] = dout("conv_smp", [12, 6144])

    for tn in DEBUG_TAPS:
        rr = tn in ("retT", "ssmT", "merged", "hidden")
        O["dbg_" + tn] = dout("dbg_" + tn, [128, 48 if rr else DC, T], BF16 if rr else F32)
    constP, offP, cdecP, _ = make_consts("P")
    constS, offS, cdecS, _ = make_consts("S")

    wreq_holder = []
    for dry in (True, False):
        es = ExitStack()
        k = K(nc, es, dry)
        if not dry:
            k.wreq = wreq_holder[0]
        _emit(nc, es, k, I, O, WF, WB, wshapes, NPRE, NMAIN, T, NS, with_smp, offP, cdecP, offS, cdecS)
        if dry:
            wreq_holder.append(k.wreq)
            es.close()
        else:
            assert k.wpos == len(k.wreq), (k.wpos, len(k.wreq))
            k.finish()
            es.close()
    return nc


def _emit(nc, es, k, I, O, WF, WB, wshapes, NPRE, NMAIN, T, NS, with_smp, offP, cdecP, offS, cdecS):
    dry = k.dry
    TP = T + 3

    class Dummy:
        def __getitem__(self, _):
            return self

        def __getattr__(self, _):
            return lambda *a, **kw: self

    def sb(name, shape, dt=F32):
        if dry:
            return Dummy()
        return es.enter_context(nc.sbuf_tensor("sb_" + name, list(shape), dt))

    def B(name):
        return Buf(name)

    xT = sb("xT", [128, DC, T])
    b_xT = [B("xT%d" % c) for c in range(DC)]
    hT = sb("hT", [128, DC, T], BF16)
    b_hT = B("hT")
    wring = sb("wring", [128, WSLOTS, 16 * WCOLS], BF16)
    b_w = [B("w%d" % i) for i in range(WSLOTS)]
    S_ret = sb("S_ret", [128, RH, 2, 512])
    Sreg = []
    b_Sret = [[Buf("Sret%d_%d" % (h, d), Sreg, (h * 2 + d) * 2048, (h * 2 + d + 1) * 2048) for d in range(2)]
              for h in range(RH)]
    S_ssm = sb("S_ssm", [128, SG, 512])
    b_Sssm = [B("Sssm%d" % g) for g in range(SG)]
    RR = sb("RR", [128, 48, T], BF16)
    b_RR = [B("RR%d" % j) for j in range(48)]
    conv_hist = sb("conv_hist", [128, 48, 4, 3])
    b_chist = [B("chist%d" % c) for c in range(48)]
    if dry:
        conv_new = Dummy()
    else:
        conv_new = S_ret[:, 5, :, :].rearrange("p a b -> p (a b)")[:, 0:576].rearrange("p (c q t) -> p c q t", c=48, q=4)
    b_cnew = [Buf("cnew%d" % c, Sreg, 5 * 4096 + c * 48, 5 * 4096 + (c + 1) * 48) for c in range(48)]
    USZ = 6144
    U = [sb("U%d" % u, [128, USZ], BF16) for u in range(2)]
    Ureg = [[], []]

    def uview(u, name, lo_b, shape, dt):
        n = int(np.prod(shape))
        nb = n * (4 if dt == F32 else 2)
        buf = Buf("U%d_%s" % (u, name), Ureg[u], lo_b, lo_b + nb)
        if dry:
            return Dummy(), buf
        ap = U[u][:, lo_b // 2:(lo_b + nb) // 2]
        if dt == F32:
            ap = ap.bitcast(F32)
        if len(shape) == 2:
            ap = ap.rearrange("p (a b) -> p a b", a=shape[0])
        elif len(shape) == 3:
            ap = ap.rearrange("p (a b c) -> p a b c", a=shape[0], b=shape[1])
        return ap, buf

    RV = []
    SV = []
    for u in range(2):
        o = 0
        d = {}
        for name, shape, dt in (("qT", (2, T), BF16), ("kT", (2, T), BF16), ("qdT", (2, T), F32),
                                ("kdtm", (4, 256), BF16), ("vtm", (NS, 512), BF16),
                                ("grgs", (4, T), BF16), ("brgs", (4, T), BF16)):
            d[name] = uview(u, "r_" + name, o, shape, dt)
            o += int(np.prod(shape)) * (4 if dt == F32 else 2)
        assert o <= USZ * 2, o
        RV.append(d)
        d = {}
        lay = (("zs", (NS, 512), BF16, 0), ("xpad", (4, TP + 9), BF16, 2048), ("xtm", (NS, 512), BF16, 2048),
               ("bcpad", (2, TP + 9), BF16, 4224), ("Btm", (NS, 128), BF16, 4224),
               ("xcT", (4, T), BF16, 5312), ("xdt", (NS, 512), BF16, 5312),
               ("BT", (1, T), F32, 7360), ("CT", (1, T), F32, 8384))
        for name, shape, dt, o in lay:
            d[name] = uview(u, "s_" + name, o, shape, dt)
            assert o + int(np.prod(shape)) * (4 if dt == F32 else 2) <= USZ * 2
        SV.append(d)

    xstage = sb("xstage", [128, 1024])
    b_xstage = B("xstage")
    ystage = sb("ystage", [128, 1024])
    b_ystage = B("ystage")
    pstage = sb("pstage", [128, 256])
    b_pstage = B("pstage")
    pT = sb("pT", [128, 2, T], BF16)
    b_pT = B("pT")
    ropec = sb("ropec", [128, T])
    ropes = sb("ropes", [128, T])
    b_rope = B("rope")
    consts = sb("consts", [128, CONST_W])
    b_consts = B("consts")
    ident = sb("ident_sb", [128, 128])
    identb = sb("identb", [128, 128], BF16)
    onesf = sb("onesf", [128, 128])
    onesb = sb("onesb", [128, 128], BF16)
    b_setup = B("setup")
    gcols = sb("gcols", [128, 5, DC])
    gncols = sb("gncols", [128, 3, 32])
    cwcols = sb("cwcols", [128, 5, 48])
    hv = sb("hv", [128, 4, 64])
    flagt = sb("flagt", [128, 1])
    sqt = [sb("sqt%d" % i, [128, T], BF16) for i in range(2)]
    b_sqt = [B("sqt%d" % i) for i in range(2)]
    rstd = sb("rstd", [128, T])
    b_rstd = B("rstd")
    tmpA = [sb("tmpA%d" % i, [128, 512]) for i in range(3)]
    b_tmpA = [B("tmpA%d" % i) for i in range(3)]
    tmpB = [sb("tmpB%d" % i, [128, 512], BF16) for i in range(3)]
    b_tmpB = [B("tmpB%d" % i) for i in range(3)]
    dtt = sb("dtt", [128, NS, 6, 64])
    b_dtt = [B("dtt%d" % s) for s in range(NS)]
    decb = sb("decb", [128, NS, 4, 64])
    b_decb = [B("decb%d" % s) for s in range(NS)]
    wendq = sb("wendq", [128, 4, 64])
    b_wendq = B("wendq")
    attm = [sb("attm%d" % i, [128, 128], BF16) for i in range(2)]
    b_attm = [B("attm%d" % i) for i in range(2)]
    cbm = [sb("cbm%d" % i, [128, 128], BF16) for i in range(2)]
    b_cbm = [B("cbm%d" % i) for i in range(2)]
    rhsR_ = sb("rhsR", [128, 8, 128])
    rhsR = [rhsR_, rhsR_]
    b_rhsR_ = B("rhsR")
    b_rhsR = [b_rhsR_, b_rhsR_]
    wTt_ = sb("wTt", [128, 8, 128], BF16)
    wTt = [wTt_, wTt_]
    b_wTt_ = B("wTt")
    b_wTt = [b_wTt_, b_wTt_]
    xwt = [sb("xwt%d" % i, [128, 512], BF16) for i in range(2)]
    b_xwt = [B("xwt%d" % i) for i in range(2)]
    onh = [sb("onh%d" % i, [128, 512], BF16) for i in range(2)]
    b_onh = [B("onh%d" % i) for i in range(2)]
    stat = [sb("stat%d" % i, [128, 16]) for i in range(2)]
    b_stat = [B("stat%d" % i) for i in range(2)]
    if dry:
        s0buf = [Dummy(), Dummy()]
        s1buf = [Dummy(), Dummy()]
        qmk = [Dummy(), Dummy()]
    else:
        s0buf = [S_ret[:, j, :, :] for j in range(2)]
        s1buf = [S_ret[:, 2 + j, :, :] for j in range(2)]
        qmk = [S_ret[:, 4, j, 0:256].rearrange("p (a b) -> p a b", a=2) for j in range(2)]
    b_s0buf = [Buf("s0buf%d" % j, Sreg, j * 4096, (j + 1) * 4096) for j in range(2)]
    b_s1buf = [Buf("s1buf%d" % j, Sreg, (2 + j) * 4096, (3 + j) * 4096) for j in range(2)]
    b_qmk = [Buf("qmk%d" % j, Sreg, 4 * 4096 + j * 2048, 4 * 4096 + j * 2048 + 1024) for j in range(2)]

    psum = []
    b_ps = []
    for i in range(8):
        if dry:
            psum.append(Dummy())
        else:
            psum.append(es.enter_context(nc.psum_tensor("ps%d" % i, [128, 512], F32)))
        b_ps.append(B("ps%d" % i))
    pstate = {"i": 0, "a": 0, "b": 0, "sq": 0}

    pinned = set()

    def PS(pin=False):
        i = pstate["i"]
        while i in pinned:
            i = (i + 1) % 8
        pstate["i"] = (i + 1) % 8
        if pin:
            pinned.add(i)
        return psum[i], b_ps[i]

    def unpin(bps):
        pinned.discard(b_ps.index(bps))

    def TA():
        i = pstate["a"]
        pstate["a"] = (i + 1) % 3
        return tmpA[i], b_tmpA[i]

    def TB():
        i = pstate["b"]
        pstate["b"] = (i + 1) % 3
        return tmpB[i], b_tmpB[i]

    b_cast = {n: B("cast_" + n) for n in WB}

    def cast_all():
        order = ["w1_gu", "w1_down", "w_in", "w_br_ret", "w_br_ssm", "w_out", "w2_gu", "w2_down", "w_ple",
                 "w_ple_gate"]
        for n in order:
            rows, cols = wshapes[n]
            pieces = []
            rb = 128
            for r0 in range(0, rows, rb):
                pieces.append((WB[n][r0:r0 + rb, :], WF[n][r0:r0 + rb, :]))
            k.dma(pieces, W=[b_cast[n]], q="pool")

    def _issue_next():
        if k.dry or k.wissued >= len(k.wreq):
            return
        i = k.wissued
        slot = i % WSLOTS
        pieces = []
        names = set()
        for (n, r0, nkc, c0, ncols, dcol) in k.wreq[i]:
            src = WB[n][r0:r0 + nkc * 128, c0:c0 + ncols].rearrange("(kc p) n -> p kc n", p=128)
            dst = wring[:, slot, :].rearrange("p (kc n) -> p kc n", kc=16)[:, 0:nkc, dcol:dcol + ncols]
            pieces.append((dst, src))
            names.add(n)
        k.dma(pieces, R=[b_cast[n] for n in names], W=[b_w[slot]])
        k.wissued += 1

    def wtile(pieces):
        if k.dry:
            k.wreq.append(tuple(pieces))
            return Dummy(), b_w[0]
        assert tuple(pieces) == k.wreq[k.wpos], (pieces, k.wreq[k.wpos])
        while k.wissued < min(k.wpos + WSLOTS, len(k.wreq)):
            _issue_next()
        slot = k.wpos % WSLOTS
        k.wpos += 1
        return wring[:, slot, :].rearrange("p (kc n) -> p kc n", kc=16), b_w[slot]

    def wrel():
        if k.dry:
            return
        while k.wissued < min(k.wpos + WSLOTS, len(k.wreq)):
            _issue_next()

    def setup():
        k.dma([(ident[:], I["ident"])], W=[b_setup])
        k.op("dve", lambda e: e.tensor_copy(identb[:], ident[:]), R=[b_setup], W=[b_setup])
        k.op("pool", lambda e: e.memset(onesf[:], 1.0), W=[b_setup])
        k.op("pool", lambda e: e.memset(onesb[:], 1.0), W=[b_setup])
        k.dma([(gcols[:], I["gcols"]), (gncols[:], I["gncols"]), (cwcols[:], I["cwcols"]),
               (hv[:, 0:3, :], I["hvin"]), (flagt[:], I["flag"])], W=[b_setup], q="sp")
        k.op("act", lambda e: e.activation(out=hv[:, 3, :], in_=hv[:, 1, :], func=AF.Exp), R=[b_setup], W=[b_setup])
        k.op("dve", lambda e: e.tensor_scalar(out=hv[:, 1, :], in0=hv[:, 3, :], scalar1=-1.0, scalar2=None,
                                              op0=ALU.mult), R=[b_setup], W=[b_setup])
        for h in range(RH):
            for d in range(2):
                k.op("pool", lambda e, h=h, d=d: e.memset(S_ret[:, h, d, :], 0.0), W=[b_Sret[h][d]])
        for g in range(SG):
            k.op("pool", lambda e, g=g: e.memset(S_ssm[:, g, :], 0.0), W=[b_Sssm[g]])
        for c in range(48):
            k.op("pool", lambda e, c=c: e.memset(conv_hist[:, c, :, :], 0.0), W=[b_chist[c]])

    def load_consts(which):
        k.dma([(consts[:], I["constP" if which == "P" else "constS"])], W=[b_consts])

    def cv(off, name, n=None):
        o, w = off[name]
        return consts[:, o:o + w]

    def load_x(src_ap, tok0, Tt, NSt):
        for s in range(NSt):
            for hf in range(2):
                k.dma([(xstage[:], src_ap[tok0 + s * 128: tok0 + (s + 1) * 128, hf * 1024:(hf + 1) * 1024])],
                      W=[b_xstage])
                for cb2 in range(2):
                    cb = hf * 2 + cb2
                    ps, bps = PS()
                    for j in range(4):
                        cl = cb2 * 4 + j
                        k.op("pe", lambda e, cl=cl, j=j, ps=ps: e.transpose(ps[:, j * 128:(j + 1) * 128],
                                                                            xstage[:, cl * 128:(cl + 1) * 128], ident[:]),
                             R=[b_xstage, b_setup], W=[bps])
                    if cb % 2 == 0:
                        k.op("act", lambda e, cb=cb, ps=ps, s=s: e.copy(
                            out=xT[:, cb * 4:cb * 4 + 4, s * 128:(s + 1) * 128],
                            in_=ps[:].rearrange("p (a b) -> p a b", a=4)), R=[bps], W=b_xT[cb * 4:cb * 4 + 4])
                    else:
                        k.op("dve", lambda e, cb=cb, ps=ps, s=s: e.tensor_copy(
                            xT[:, cb * 4:cb * 4 + 4, s * 128:(s + 1) * 128],
                            ps[:].rearrange("p (a b) -> p a b", a=4)), R=[bps], W=b_xT[cb * 4:cb * 4 + 4])

    def rms_stats(Tt):
        ps, bps = PS()
        for c in range(DC):
            i = pstate["sq"]
            pstate["sq"] = (i + 1) % 2
            k.op("act", lambda e, c=c, i=i: e.activation(out=sqt[i][:, 0:Tt], in_=xT[:, c, 0:Tt], func=AF.Square),
                 R=[b_xT[c]], W=[b_sqt[i]])
            k.op("pe", lambda e, c=c, i=i, ps=ps: e.matmul(ps[:, 0:Tt], onesb[:], sqt[i][:, 0:Tt], start=(c == 0),
                                                           stop=(c == DC - 1)), R=[b_sqt[i], b_setup], W=[bps])
        k.op("act", lambda e, ps=ps: e.activation(out=rstd[:, 0:Tt], in_=ps[:, 0:Tt], func=AF.Sqrt, bias=EPS,
                                                  scale=1.0 / D), R=[bps], W=[b_rstd])
        k.op("dve", lambda e: e.reciprocal(out=rstd[:, 0:Tt], in_=rstd[:, 0:Tt]), R=[b_rstd], W=[b_rstd])

    def norm_to_hT(gi, Tt):
        rms_stats(Tt)
        for c in range(DC):
            k.op("dve", lambda e, c=c: e.scalar_tensor_tensor(out=hT[:, c, 0:Tt], in0=xT[:, c, 0:Tt],
                                                              scalar=gcols[:, gi, c:c + 1], in1=rstd[:, 0:Tt],
                                                              op0=ALU.mult, op1=ALU.mult),
                 R=[b_xT[c], b_rstd, b_setup], W=[b_hT])

    def proj_fm(wname, col0, nch, rhs_fn, nkc, rbufs, row0=0):
        res = []
        for t0 in range(0, nch, 2):
            n = min(2, nch - t0)
            wt, bw = wtile([(wname, row0, nkc, col0 + t0 * 128, n * 128, 0)])
            for j in range(n):
                ps, bps = PS()
                for kc in range(nkc):
                    k.op("pe", lambda e, kc=kc, j=j, ps=ps, wt=wt: e.matmul(
                        ps[:, 0:rhs_fn(kc).shape[-1]], wt[:, kc, j * 128:(j + 1) * 128], rhs_fn(kc),
                        start=(kc == 0), stop=(kc == nkc - 1)), R=[bw] + rbufs, W=[bps])
                res.append((t0 + j, ps, bps))
            wrel()
        return res

    def ffn(wgu, wdown, gi, Tt):
        norm_to_hT(gi, Tt)
        for jp in range(FC // 2):
            gl = proj_fm(wgu, jp * 256, 2, lambda kc: hT[:, kc, 0:Tt], DC, [b_hT])
            sgs = []
            for (j, ps, bps) in gl:
                ta, bta = TA()
                k.op("act", lambda e, ps=ps, ta=ta: e.activation(out=ta[:, 0:Tt], in_=ps[:, 0:Tt], func=AF.Silu),
                     R=[bps], W=[bta])
                sgs.append((ta, bta))
            ul = proj_fm(wgu, FF + jp * 256, 2, lambda kc: hT[:, kc, 0:Tt], DC, [b_hT])
            for (j, ps, bps), (ta, bta) in zip(ul, sgs):
                hc = jp * 2 + j
                k.op("dve", lambda e, ps=ps, ta=ta, hc=hc: e.tensor_tensor(out=RR[:, hc, 0:Tt], in0=ps[:, 0:Tt],
                                                                          in1=ta[:, 0:Tt], op=ALU.mult),
                     R=[bps, bta], W=[b_RR[hc]])
        kps = [(0, 16), (16, 16), (32, 12)]
        for op_ in range(DC // 2):
            banks = [PS(), PS()]
            for pi, (k0, nk) in enumerate(kps):
                wt, bw = wtile([(wdown, k0 * 128, nk, op_ * 256, 256, 0)])
                for j in range(2):
                    ps, bps = banks[j]
                    for kc in range(nk):
                        k.op("pe", lambda e, kc=kc, j=j, ps=ps, wt=wt, k0=k0, pi=pi, nk=nk: e.matmul(
                            ps[:, 0:Tt], wt[:, kc, j * 128:(j + 1) * 128], RR[:, k0 + kc, 0:Tt],
                            start=(pi == 0 and kc == 0), stop=(pi == 2 and kc == nk - 1)),
                             R=[bw, b_RR[k0 + kc]], W=[bps])
                wrel()
            for j in range(2):
                ps, bps = banks[j]
                c = op_ * 2 + j
                k.op("dve", lambda e, ps=ps, c=c: e.scalar_tensor_tensor(out=xT[:, c, 0:Tt], in0=ps[:, 0:Tt],
                                                                         scalar=0.5, in1=xT[:, c, 0:Tt],
                                                                         op0=ALU.mult, op1=ALU.add),
                     R=[bps, b_xT[c]], W=[b_xT[c]])

    def dt_path(Tt, NSt, off, nq, pre):
        wt, bw = wtile([("w_in", 0, DC, OFF_DT, 64, 0)])
        pss = []
        for s in range(NSt):
            ps, bps = PS()
            for kc in range(DC):
                k.op("pe", lambda e, kc=kc, ps=ps, s=s, wt=wt: e.matmul(ps[:, 0:64], hT[:, kc, s * 128:(s + 1) * 128],
                                                                        wt[:, kc, 0:64], start=(kc == 0),
                                                                        stop=(kc == DC - 1)), R=[bw, b_hT], W=[bps])
            pss.append((ps, bps))
        wrel()
        for s in range(NSt):
            ps, bps = pss[s]
            d_ = dtt[:, s]
            bd = b_dtt[s]
            k.op("dve", lambda e, ps=ps, d_=d_: e.tensor_tensor(out=d_[:, 4, :], in0=ps[:, 0:64], in1=hv[:, 0, :],
                                                                op=ALU.add), R=[bps, b_setup], W=[bd])
            k.op("dve", lambda e, d_=d_: e.scalar_tensor_tensor(out=d_[:, 5, :], in0=d_[:, 4, :], scalar=-1.0,
                                                                in1=d_[:, 4, :], op0=ALU.mult, op1=ALU.max),
                 R=[bd], W=[bd])
            k.op("act", lambda e, d_=d_: e.activation(out=d_[:, 5, :], in_=d_[:, 5, :], func=AF.Exp, scale=-1.0),
                 R=[bd], W=[bd])
            k.op("act", lambda e, d_=d_: e.activation(out=d_[:, 5, :], in_=d_[:, 5, :], func=AF.Ln, bias=1.0,
                                                      scale=1.0), R=[bd], W=[bd])
            k.op("dve", lambda e, d_=d_: e.scalar_tensor_tensor(out=d_[:, 0, :], in0=d_[:, 4, :], scalar=0.0,
                                                                in1=d_[:, 5, :], op0=ALU.max, op1=ALU.add),
                 R=[bd], W=[bd])
            k.op("dve", lambda e, d_=d_: e.tensor_tensor(out=d_[:, 1, :], in0=d_[:, 0, :], in1=hv[:, 1, :],
                                                         op=ALU.mult), R=[bd, b_setup], W=[bd])
            if not pre:
                ps2, bps2 = PS()
                k.op("pe", lambda e, ps2=ps2, d_=d_: e.matmul(ps2[:, 0:64], cv(off, "triT"), d_[:, 1, :], start=True,
                                                              stop=True), R=[bd, b_consts], W=[bps2])
                k.op("act", lambda e, ps2=ps2, d_=d_: e.activation(out=d_[:, 2, :], in_=ps2[:, 0:64], func=AF.Exp),
                     R=[bps2], W=[bd])
            ps3, bps3 = PS()
            k.op("pe", lambda e, ps3=ps3, d_=d_: e.matmul(ps3[:, 0:64], cv(off, "SL"), d_[:, 1, :], start=True,
                                                          stop=True), R=[bd, b_consts], W=[bps3])
            k.op("act", lambda e, ps3=ps3, d_=d_: e.activation(out=d_[:, 3, :], in_=ps3[:, 0:64], func=AF.Exp),
                 R=[bps3], W=[bd])
            ps4, bps4 = PS()
            for q in range(nq):
                so = cv(off, "seqones")[:, q * 128:(q + 1) * 128]
                k.op("pe", lambda e, ps4=ps4, d_=d_, q=q, so=so: e.matmul(ps4[:, q * 64:(q + 1) * 64], so,
                                                                           d_[:, 1, :], start=True, stop=True),
                     R=[bd, b_consts], W=[bps4])
            k.op("act", lambda e, ps4=ps4, s=s: e.activation(
                out=decb[:, s, 0:nq, :], in_=ps4[:, 0:nq * 64].rearrange("p (q h) -> p q h", q=nq), func=AF.Exp),
                 R=[bps4], W=[b_decb[s]])

    def rope(pa, ba, pb, bb, Tt, out1, out2, wbufs):
        t1, b1 = TA()
        t2, b2 = TA()
        k.op("dve", lambda e: e.tensor_tensor(out=t1[:, 0:Tt], in0=pa[:, 0:Tt], in1=ropec[:, 0:Tt], op=ALU.mult),
             R=[ba, b_rope], W=[b1])
        k.op("dve", lambda e: e.tensor_tensor(out=t2[:, 0:Tt], in0=pb[:, 0:Tt], in1=ropes[:, 0:Tt], op=ALU.mult),
             R=[bb, b_rope], W=[b2])
        k.op("pool", lambda e: e.tensor_tensor(out=out1, in0=t1[:, 0:Tt], in1=t2[:, 0:Tt], op=ALU.subtract),
             R=[b1, b2], W=wbufs)
        t3, b3 = TA()
        k.op("dve", lambda e: e.tensor_tensor(out=t3[:, 0:Tt], in0=pb[:, 0:Tt], in1=ropec[:, 0:Tt], op=ALU.mult),
             R=[bb, b_rope], W=[b3])
        k.op("dve", lambda e: e.tensor_tensor(out=t1[:, 0:Tt], in0=pa[:, 0:Tt], in1=ropes[:, 0:Tt], op=ALU.mult),
             R=[ba, b_rope], W=[b1])
        k.op("pool", lambda e: e.tensor_tensor(out=out2, in0=t3[:, 0:Tt], in1=t1[:, 0:Tt], op=ALU.add),
             R=[b3, b1], W=wbufs)

    def ret_w(h, u, Tt, NSt, off, nq, pre):
        V = RV[u]
        qT, bq = V["qT"]
        kT, bk = V["kT"]
        qdT, bqd = V["qdT"]
        kdtm, bkd = V["kdtm"]
        vtm, bv = V["vtm"]
        grgs, bgr = V["grgs"]
        brgs, bbr = V["brgs"]
        rh = lambda kc: hT[:, kc, 0:Tt]
        names = ["k"] if pre else ["q", "k"]
        for nm in names:
            wt, bw = wtile([("w_in", 0, DC, (OFF_Q if nm == "q" else OFF_K) + h * 256, 256, 0)])
            pp = []
            for dh in range(2):
                ps, bps = PS()
                for kc in range(DC):
                    k.op("pe", lambda e, kc=kc, dh=dh, ps=ps, wt=wt: e.matmul(ps[:, 0:Tt], wt[:, kc, dh * 128:(dh + 1) * 128],
                                                                              rh(kc), start=(kc == 0),
                                                                              stop=(kc == DC - 1)),
                         R=[bw, b_hT], W=[bps])
                pp.append((ps, bps))
            if nm == "q":
                rope(pp[0][0], pp[0][1], pp[1][0], pp[1][1], Tt, qdT[:, 0, 0:Tt], qdT[:, 1, 0:Tt], [bqd])
                k.op("act", lambda e: e.copy(out=qT[:, :, 0:Tt], in_=qdT[:, :, 0:Tt]), R=[bqd], W=[bq])
                if nq == 1:
                    for dh in range(2):
                        k.op("pool", lambda e, dh=dh: e.tensor_tensor(
                            out=qdT[:, dh, 0:Tt].rearrange("p (s i) -> p s i", s=NSt),
                            in0=qdT[:, dh, 0:Tt].rearrange("p (s i) -> p s i", s=NSt),
                            in1=cv(off, "qdec")[:, h * 128:(h + 1) * 128].unsqueeze(1).to_broadcast([128, NSt, 128]),
                            op=ALU.mult), R=[bqd, b_consts, bq], W=[bqd])
                else:
                    k.op("pool", lambda e: e.tensor_tensor(
                        out=qdT[:, :, 0:128], in0=qdT[:, :, 0:128],
                        in1=cv(off, "qdec")[:, h * 128:(h + 1) * 128].unsqueeze(1).to_broadcast([128, 2, 128]),
                        op=ALU.mult), R=[bqd, b_consts, bq], W=[bqd])
            else:
                rope(pp[0][0], pp[0][1], pp[1][0], pp[1][1], Tt, kT[:, 0, 0:Tt], kT[:, 1, 0:Tt], [bk])
            wrel()
            yield
        vps = [PS() for s in range(NSt)]
        for half in range(2):
            wt, bw = wtile([("w_in", 0, DC, OFF_V + h * 512 + half * 256, 256, 0)])
            for s in range(NSt):
                ps, bps = vps[s]
                for kc in range(DC):
                    k.op("pe", lambda e, kc=kc, s=s, ps=ps, wt=wt, half=half: e.matmul(
                        ps[:, half * 256:(half + 1) * 256], hT[:, kc, s * 128:(s + 1) * 128], wt[:, kc, 0:256],
                        start=(kc == 0), stop=(kc == DC - 1)), R=[bw, b_hT], W=[bps])
            wrel()
        for s in range(NSt):
            ps, bps = vps[s]
            k.op("act", lambda e, s=s, ps=ps: e.copy(out=vtm[:, s, :], in_=ps[:]), R=[bps], W=[bv])
        yield
        for t0 in ((0, 2) if not pre else ()):
            for (j, ps, bps) in proj_fm("w_in", OFF_RG + h * 512 + t0 * 128, 2, rh, DC, [b_hT]):
                ec = t0 + j
                tb, btb = TB()
                k.op("act", lambda e, ps=ps, tb=tb: e.activation(out=tb[:, 0:Tt], in_=ps[:, 0:Tt], func=AF.Silu),
                     R=[bps], W=[btb])
                k.op("pool", lambda e, ec=ec, tb=tb: e.tensor_scalar(out=grgs[:, ec, 0:Tt], in0=tb[:, 0:Tt],
                                                                     scalar1=gncols[:, 0, h * 4 + ec:h * 4 + ec + 1],
                                                                     scalar2=None, op0=ALU.mult),
                     R=[btb, b_setup], W=[bgr])
                k.op("pool", lambda e, ec=ec, tb=tb: e.tensor_scalar(out=brgs[:, ec, 0:Tt], in0=tb[:, 0:Tt],
                                                                     scalar1=gncols[:, 1, h * 4 + ec:h * 4 + ec + 1],
                                                                     scalar2=None, op0=ALU.mult),
                     R=[btb, b_setup], W=[bbr])
            yield
        for s in range(NSt):
            ps, bps = PS()
            psb = ps[:].bitcast(BF16)
            for dh in range(2):
                k.op("pe", lambda e, dh=dh, s=s, psb=psb: e.transpose(psb[:, dh * 128:(dh + 1) * 128],
                                                                      kT[:, dh, s * 128:(s + 1) * 128], identb[:]),
                     R=[bk, b_setup], W=[bps])
            for q in range(nq):
                k.op("dve", lambda e, s=s, q=q, psb=psb: e.tensor_scalar(
                    out=kdtm[:, s + q, :], in0=psb[:, 0:256], scalar1=cv(off, "kdecq")[:, q * 8 + h:q * 8 + h + 1],
                    scalar2=None, op0=ALU.mult), R=[bps, b_consts], W=[bkd])
        yield

    def ret_chunk(h, u, s, Tt, off, cdec, nq, pre, smp):
        V = RV[u]
        qT, bq = V["qT"]
        kT, bk = V["kT"]
        qdT, bqd = V["qdT"]
        kdtm, bkd = V["kdtm"]
        vtm, bv = V["vtm"]
        grgs, bgr = V["grgs"]
        brgs, bbr = V["brgs"]
        sl = slice(s * 128, (s + 1) * 128)
        i = (h * 2 + s) % 2
        if not pre:
            ps, bps = PS()
            for dh in range(2):
                k.op("pe", lambda e, dh=dh, ps=ps: e.matmul(ps[:, 0:128], kT[:, dh, sl], qT[:, dh, sl],
                                                            start=(dh == 0), stop=(dh == 1)), R=[bk, bq], W=[bps])
            k.op("dve", lambda e, ps=ps: e.tensor_tensor(out=attm[i][:], in0=ps[:, 0:128],
                                                         in1=cv(off, "dmatT")[:, h * 128:(h + 1) * 128], op=ALU.mult),
                 R=[bps, b_consts], W=[b_attm[i]])
            yield
        if smp:
            _ret_chunk_smp(h, u, s, off, cdec, nq, V, i)
            yield
            return
        if not pre:
            po, bpo = PS()
            k.op("pe", lambda e: e.matmul(po[:], attm[i][:], vtm[:, s, :], start=True, stop=False),
                 R=[b_attm[i], bv], W=[bpo])
            for dh in range(2):
                k.op("pe", lambda e, dh=dh: e.matmul(po[:], qdT[:, dh, sl], S_ret[:, h, dh, :], start=False,
                                                     stop=(dh == 1)), R=[bqd, b_Sret[h][dh]], W=[bpo])
        for dh in range(2):
            ps, bps = PS()
            k.op("pe", lambda e, dh=dh, ps=ps: e.matmul(ps[:], kdtm[:, s, dh * 128:(dh + 1) * 128], vtm[:, s, :],
                                                        start=True, stop=True), R=[bkd, bv], W=[bps])
            k.op("dve", lambda e, dh=dh, ps=ps: e.scalar_tensor_tensor(out=S_ret[:, h, dh, :], in0=S_ret[:, h, dh, :],
                                                                       scalar=cdec[h], in1=ps[:], op0=ALU.mult,
                                                                       op1=ALU.add),
                 R=[bps, b_Sret[h][dh]], W=[b_Sret[h][dh]])
        if not pre:
            ret_post1(h, s, po, bpo)
            yield
            ret_post2(h, s, grgs, bgr, brgs, bbr)
        yield

    def ret_post(h, s, po, bpo, grgs, bgr, brgs, bbr):
        ret_post1(h, s, po, bpo)
        ret_post2(h, s, grgs, bgr, brgs, bbr)

    def ret_post1(h, s, po, bpo):
        i = (h * 2 + s) % 2
        st, bst = stat[i], b_stat[i]
        k.op("dve", lambda e: e.bn_stats(out=st[:, 0:6], in_=po[:]), R=[bpo], W=[bst])
        k.op("dve", lambda e: e.bn_aggr(out=st[:, 6:8], in_=st[:, 0:6]), R=[bst], W=[bst])
        k.op("act", lambda e: e.activation(out=st[:, 8:9], in_=st[:, 7:8], func=AF.Sqrt, bias=EPS, scale=1.0),
             R=[bst], W=[bst])
        k.op("dve", lambda e: e.reciprocal(out=st[:, 8:9], in_=st[:, 8:9]), R=[bst], W=[bst])
        tb, btb = onh[i], b_onh[i]
        k.op("dve", lambda e: e.tensor_scalar(out=tb[:], in0=po[:], scalar1=st[:, 6:7], scalar2=st[:, 8:9],
                                              op0=ALU.subtract, op1=ALU.mult), R=[bpo, bst], W=[btb])

    def ret_post2(h, s, grgs, bgr, brgs, bbr):
        sl = slice(s * 128, (s + 1) * 128)
        i = (h * 2 + s) % 2
        tb, btb = onh[i], b_onh[i]
        pt, bpt = PS()
        ptb = pt[:].bitcast(BF16)
        for ec in range(4):
            k.op("pe", lambda e, ec=ec: e.transpose(ptb[:, ec * 128:(ec + 1) * 128], tb[:, ec * 128:(ec + 1) * 128],
                                                    identb[:]), R=[btb, b_setup], W=[bpt])
        tb2, btb2 = TB()
        k.op("dve", lambda e: e.tensor_tensor(out=tb2[:].rearrange("p (a b) -> p a b", a=4),
                                              in0=ptb[:, 0:512].rearrange("p (a b) -> p a b", a=4),
                                              in1=grgs[:, :, sl], op=ALU.mult), R=[bpt, bgr], W=[btb2])
        k.op("pool", lambda e: e.tensor_tensor(out=RR[:, h * 4:h * 4 + 4, sl],
                                               in0=tb2[:].rearrange("p (a b) -> p a b", a=4), in1=brgs[:, :, sl],
                                               op=ALU.add), R=[btb2, bbr], W=b_RR[h * 4:h * 4 + 4])

    def _ret_chunk_smp(h, u, s, off, cdec, nq, V, i):
        qT, bq = V["qT"]
        qdT, bqd = V["qdT"]
        kdtm, bkd = V["kdtm"]
        vtm, bv = V["vtm"]
        grgs, bgr = V["grgs"]
        brgs, bbr = V["brgs"]
        po, bpo = PS(pin=True)
        k.op("pe", lambda e: e.matmul(po[:], attm[i][:], vtm[:, 0, :], start=True, stop=False),
             R=[b_attm[i], bv], W=[bpo])
        for q in range(nq):
            j = q % 2
            k.dma([(s0buf[j][:], I["sret_in"][q, h].rearrange("(dh p) e -> p dh e", p=128))], W=[b_s0buf[j]])
            k.op("pool", lambda e, q=q, j=j: e.tensor_tensor(
                out=qmk[j][:], in0=qdT[:, :, 0:128],
                in1=cv(off, "colmask")[:, q * 128:(q + 1) * 128].unsqueeze(1).to_broadcast([128, 2, 128]),
                op=ALU.mult), R=[bqd, b_consts], W=[b_qmk[j]])
            for dh in range(2):
                k.op("pe", lambda e, dh=dh, j=j, q=q: e.matmul(po[:], qmk[j][:, dh, :], s0buf[j][:, dh, :], start=False,
                                                               stop=(q == nq - 1 and dh == 1)),
                     R=[b_qmk[j], b_s0buf[j]], W=[bpo])
            for dh in range(2):
                ps, bps = PS()
                k.op("pe", lambda e, dh=dh, ps=ps, q=q: e.matmul(ps[:], kdtm[:, q, dh * 128:(dh + 1) * 128],
                                                                 vtm[:, 0, :], start=True, stop=True),
                     R=[bkd, bv], W=[bps])
                k.op("dve", lambda e, dh=dh, ps=ps, j=j: e.scalar_tensor_tensor(
                    out=s1buf[j][:, dh, :], in0=s0buf[j][:, dh, :], scalar=cdec[h], in1=ps[:], op0=ALU.mult,
                    op1=ALU.add), R=[bps, b_s0buf[j]], W=[b_s1buf[j]])
            k.dma([(O["ret_smp"][q, h].rearrange("(dh p) e -> p dh e", p=128), s1buf[j][:])], R=[b_s1buf[j]],
                  is_out=True)
        ret_post(h, s, po, bpo, grgs, bgr, brgs, bbr)
        unpin(bpo)

    def ssm_w(g, u, Tt, NSt, off, nq, pre, L, nseg):
        V = SV[u]
        zs, bzs = V["zs"]
        xpad, bxp = V["xpad"]
        bcpad, bbc = V["bcpad"]
        xcT, bxc = V["xcT"]
        BT, bBT = V["BT"]
        CT, bCT = V["CT"]
        xtm, bxt = V["xtm"]
        Btm, bBt = V["Btm"]
        xdt, bxd = V["xdt"]
        rh = lambda kc: hT[:, kc, 0:Tt]
        LP = L + 3
        if not pre:
            zps = [PS() for s in range(NSt)]
            for half in range(2):
                wt, bw = wtile([("w_in", 0, DC, OFF_Z + g * 512 + half * 256, 256, 0)])
                for s in range(NSt):
                    ps, bps = zps[s]
                    for kc in range(DC):
                        k.op("pe", lambda e, kc=kc, s=s, ps=ps, wt=wt, half=half: e.matmul(
                            ps[:, half * 256:(half + 1) * 256], hT[:, kc, s * 128:(s + 1) * 128], wt[:, kc, 0:256],
                            start=(kc == 0), stop=(kc == DC - 1)), R=[bw, b_hT], W=[bps])
                wrel()
            for s in range(NSt):
                ps, bps = zps[s]
                k.op("act", lambda e, s=s, ps=ps: e.activation(out=zs[:, s, :], in_=ps[:], func=AF.Silu),
                     R=[bps], W=[bzs])
            yield

        def preconv(ps, bps, cch, padv, bpad):
            pv = padv.rearrange("p (q l) -> p q l", q=nseg)
            k.op("act", lambda e: e.copy(out=pv[:, :, 0:3], in_=conv_hist[:, cch, 0:nseg, :]),
                 R=[b_chist[cch]], W=[bpad])
            k.op("act", lambda e: e.copy(out=pv[:, :, 3:LP], in_=ps[:, 0:Tt].rearrange("p (q l) -> p q l", q=nseg)),
                 R=[bps], W=[bpad])
            dst = conv_new if nseg > 1 else conv_hist
            bdst = b_cnew if nseg > 1 else b_chist
            k.op("dve", lambda e: e.tensor_copy(dst[:, cch, 0:nseg, :],
                                                ps[:, 0:Tt].rearrange("p (q l) -> p q l", q=nseg)[:, :, L - 3:L]),
                 R=[bps, bpad], W=[bdst[cch]])

        def conv(cch, padv, bpad, out_ap, bout):
            pv = padv.rearrange("p (q l) -> p q l", q=nseg)
            ta, bta = TA()
            tv = ta[:, 0:Tt].rearrange("p (q l) -> p q l", q=nseg)
            k.op("pool", lambda e: e.tensor_scalar(out=tv, in0=pv[:, :, 0:L], scalar1=cwcols[:, 0, cch:cch + 1],
                                                   scalar2=None, op0=ALU.mult), R=[bpad, b_setup], W=[bta])
            for w in range(1, 4):
                k.op("dve", lambda e, w=w: e.scalar_tensor_tensor(out=tv, in0=pv[:, :, w:w + L],
                                                                  scalar=cwcols[:, w, cch:cch + 1], in1=tv,
                                                                  op0=ALU.mult, op1=ALU.add),
                     R=[bpad, b_setup, bta], W=[bta])
            k.op("act", lambda e: e.activation(out=out_ap, in_=ta[:, 0:Tt], func=AF.Silu,
                                               bias=cwcols[:, 4, cch:cch + 1], scale=1.0),
                 R=[bta, b_setup], W=[bout])

        for t0 in (0, 2):
            for (j, ps, bps) in proj_fm("w_in", OFF_X + g * 512 + t0 * 128, 2, rh, DC, [b_hT]):
                ci = t0 + j
                cch = g * 4 + ci
                preconv(ps, bps, cch, xpad[:, ci, 0:nseg * LP], bxp)
                conv(cch, xpad[:, ci, 0:nseg * LP], bxp, xcT[:, ci, 0:Tt], bxc)
            yield
        for (bi, offc, cch) in [(0, OFF_B, 32 + g), (1, OFF_C, 40 + g)]:
            (_, ps, bps), = proj_fm("w_in", offc + g * 128, 1, rh, DC, [b_hT])
            preconv(ps, bps, cch, bcpad[:, bi, 0:nseg * LP], bbc)
            if bi == 0:
                conv(cch, bcpad[:, bi, 0:nseg * LP], bbc, BT[:, 0, 0:Tt], bBT)
            elif not pre:
                conv(cch, bcpad[:, bi, 0:nseg * LP], bbc, CT[:, 0, 0:Tt], bCT)
            yield
        for s in range(NSt):
            ps, bps = PS()
            psb = ps[:].bitcast(BF16)
            for ci in range(4):
                k.op("pe", lambda e, ci=ci, s=s, psb=psb: e.transpose(psb[:, ci * 128:(ci + 1) * 128],
                                                                      xcT[:, ci, s * 128:(s + 1) * 128], identb[:]),
                     R=[bxc, b_setup], W=[bps])
            k.op("act", lambda e, s=s, psb=psb: e.copy(out=xtm[:, s, :], in_=psb[:, 0:512]), R=[bps], W=[bxt])
            ps2, bps2 = PS()
            k.op("pe", lambda e, s=s, ps2=ps2: e.transpose(ps2[:, 0:128], BT[:, 0, s * 128:(s + 1) * 128], ident[:]),
                 R=[bBT, b_setup], W=[bps2])
            k.op("act", lambda e, s=s, ps2=ps2: e.copy(out=Btm[:, s, :], in_=ps2[:, 0:128]), R=[bps2], W=[bBt])
            yield
        for s in range(NSt):
            k.op("pool", lambda e, s=s: e.tensor_tensor(
                out=xdt[:, s, :].rearrange("p (h q) -> p h q", h=8), in0=xtm[:, s, :].rearrange("p (h q) -> p h q", h=8),
                in1=dtt[:, s, 0, g * 8:(g + 1) * 8].unsqueeze(2).to_broadcast([128, 8, 64]), op=ALU.mult),
                 R=[bxt, b_dtt[s]], W=[bxd])
        yield

    def ssm_chunk(g, u, s, Tt, off, nq, pre, smp):
        V = SV[u]
        zs, bzs = V["zs"]
        BT, bBT = V["BT"]
        CT, bCT = V["CT"]
        xtm, bxt = V["xtm"]
        Btm, bBt = V["Btm"]
        xdt, bxd = V["xdt"]
        sl = slice(s * 128, (s + 1) * 128)
        i = (g * 2 + s) % 2
        gs = slice(g * 8, (g + 1) * 8)
        if not pre:
            ps, bps = PS()
            k.op("pe", lambda e: e.matmul(ps[:, 0:128], BT[:, 0, sl], CT[:, 0, sl], start=True, stop=True),
                 R=[bBT, bCT], W=[bps])
            k.op("dve", lambda e: e.tensor_tensor(out=cbm[i][:], in0=ps[:, 0:128], in1=cv(off, "maskT"), op=ALU.mult),
                 R=[bps, b_consts], W=[b_cbm[i]])
            k.op("pool", lambda e: e.tensor_tensor(
                out=rhsR[i][:], in0=dtt[:, s, 1, gs].unsqueeze(2).to_broadcast([128, 8, 128]),
                in1=cv(off, "triT").unsqueeze(1).to_broadcast([128, 8, 128]), op=ALU.mult),
                 R=[b_dtt[s], b_consts], W=[b_rhsR[i]])
            for hh in range(2):
                pg, bpg = PS()
                k.op("pe", lambda e, hh=hh, pg=pg: e.matmul(
                    pg[:], cv(off, "SL"), rhsR[i][:, hh * 4:(hh + 1) * 4, :].rearrange("p a b -> p (a b)"),
                    start=True, stop=True), R=[b_rhsR[i], b_consts], W=[bpg])
                k.op("act", lambda e, hh=hh, pg=pg: e.activation(
                    out=wTt[i][:, hh * 4:(hh + 1) * 4, :].rearrange("p a b -> p (a b)"), in_=pg[:], func=AF.Exp),
                     R=[bpg], W=[b_wTt[i]])
            k.op("dve", lambda e: e.tensor_tensor(out=wTt[i][:], in0=wTt[i][:],
                                                  in1=cbm[i][:].unsqueeze(1).to_broadcast([128, 8, 128]), op=ALU.mult),
                 R=[b_wTt[i], b_cbm[i]], W=[b_wTt[i]])
            yield
            py, bpy = PS(pin=True)
            for hd in range(8):
                k.op("pe", lambda e, hd=hd: e.matmul(py[:, hd * 64:(hd + 1) * 64], wTt[i][:, hd, :],
                                                     xdt[:, s, hd * 64:(hd + 1) * 64], start=True, stop=True),
                     R=[b_wTt[i], bxd], W=[bpy])
        if smp:
            pin, bpin = PS(pin=True)
            for q in range(nq):
                j = q % 2
                k.dma([(s1buf[j][:].rearrange("p a b -> p (a b)")[:, 0:512].rearrange("p (a b) -> p a b", a=4),
                        I["sssm_in"][q, g * 8:(g + 1) * 8].rearrange("h p n -> (h p) n").rearrange(
                            "(a r) n -> r a n", r=128))], W=[b_s1buf[j]])
                pt, bpt = PS()
                for a in range(4):
                    k.op("pe", lambda e, a=a, j=j, pt=pt: e.transpose(
                        pt[:, a * 128:(a + 1) * 128],
                        s1buf[j][:].rearrange("p a b -> p (a b)")[:, a * 128:(a + 1) * 128], ident[:]),
                         R=[b_s1buf[j], b_setup], W=[bpt])
                k.op("act", lambda e, j=j, pt=pt: e.copy(out=s0buf[j][:, 0, :], in_=pt[:]), R=[bpt], W=[b_s0buf[j]])
                k.op("pool", lambda e, j=j, q=q: e.tensor_tensor(out=qmk[j][:, 0, :], in0=CT[:, 0, 0:128],
                                                                in1=cv(off, "colmask")[:, q * 128:(q + 1) * 128],
                                                                op=ALU.mult), R=[bCT, b_consts], W=[b_qmk[j]])
                k.op("pe", lambda e, j=j, q=q: e.matmul(pin[:], qmk[j][:, 0, :], s0buf[j][:, 0, :], start=(q == 0),
                                                        stop=(q == nq - 1)), R=[b_qmk[j], b_s0buf[j]], W=[bpin])
                k.op("pool", lambda e, q=q: e.tensor_scalar(out=wendq[:, q, :], in0=dtt[:, 0, 3, :],
                                                            scalar1=cv(off, "rowmask")[:, q:q + 1], scalar2=None,
                                                            op0=ALU.mult), R=[b_dtt[0], b_consts], W=[b_wendq])
                k.op("pool", lambda e, j=j, q=q: e.tensor_tensor(
                    out=xwt[j][:].rearrange("p (h q) -> p h q", h=8), in0=xdt[:, 0, :].rearrange("p (h q) -> p h q", h=8),
                    in1=wendq[:, q, gs].unsqueeze(2).to_broadcast([128, 8, 64]), op=ALU.mult),
                     R=[bxd, b_wendq], W=[b_xwt[j]])
                pst, bpst = PS()
                k.op("pe", lambda e, j=j, pst=pst: e.matmul(pst[:], Btm[:, 0, :], xwt[j][:], start=True, stop=True),
                     R=[bBt, b_xwt[j]], W=[bpst])
                k.op("dve", lambda e, j=j, q=q: e.tensor_tensor(
                    out=s0buf[j][:, 1, :].rearrange("p (h q) -> p h q", h=8),
                    in0=s0buf[j][:, 0, :].rearrange("p (h q) -> p h q", h=8),
                    in1=decb[:, 0, q, gs].unsqueeze(2).to_broadcast([128, 8, 64]), op=ALU.mult),
                     R=[b_s0buf[j], b_decb[0]], W=[b_s0buf[j]])
                k.op("dve", lambda e, j=j, pst=pst: e.tensor_tensor(out=s0buf[j][:, 1, :], in0=s0buf[j][:, 1, :],
                                                                    in1=pst[:], op=ALU.add),
                     R=[bpst, b_s0buf[j]], W=[b_s0buf[j]])
                pt2, bpt2 = PS()
                for a in range(4):
                    k.op("pe", lambda e, a=a, j=j, pt2=pt2: e.transpose(pt2[:, a * 128:(a + 1) * 128],
                                                                        s0buf[j][:, 1, a * 128:(a + 1) * 128],
                                                                        ident[:]),
                         R=[b_s0buf[j], b_setup], W=[bpt2])
                k.op("act", lambda e, j=j, pt2=pt2: e.copy(
                    out=s1buf[j][:].rearrange("p a b -> p (a b)")[:, 512:1024], in_=pt2[:]), R=[bpt2], W=[b_s1buf[j]])
                k.dma([(O["ssm_smp"][q, g * 512:(g + 1) * 512, :].rearrange("(a r) n -> r a n", r=128),
                        s1buf[j][:].rearrange("p a b -> p (a b)")[:, 512:1024].rearrange("p (a b) -> p a b", a=4))],
                      R=[b_s1buf[j]], is_out=True)
        else:
            if not pre:
                pin, bpin = PS(pin=True)
                k.op("pe", lambda e: e.matmul(pin[:], CT[:, 0, sl], S_ssm[:, g, :], start=True, stop=True),
                     R=[bCT, b_Sssm[g]], W=[bpin])
            k.op("pool", lambda e: e.tensor_tensor(
                out=xwt[i][:].rearrange("p (h q) -> p h q", h=8), in0=xdt[:, s, :].rearrange("p (h q) -> p h q", h=8),
                in1=dtt[:, s, 3, gs].unsqueeze(2).to_broadcast([128, 8, 64]), op=ALU.mult),
                 R=[bxd, b_dtt[s]], W=[b_xwt[i]])
            pst, bpst = PS()
            k.op("pe", lambda e: e.matmul(pst[:], Btm[:, s, :], xwt[i][:], start=True, stop=True),
                 R=[bBt, b_xwt[i]], W=[bpst])
            k.op("dve", lambda e: e.tensor_tensor(
                out=S_ssm[:, g, :].rearrange("p (h q) -> p h q", h=8), in0=S_ssm[:, g, :].rearrange("p (h q) -> p h q", h=8),
                in1=decb[:, s, 0, gs].unsqueeze(2).to_broadcast([128, 8, 64]), op=ALU.mult),
                 R=[b_Sssm[g], b_decb[s]], W=[b_Sssm[g]])
            k.op("dve", lambda e: e.tensor_tensor(out=S_ssm[:, g, :], in0=S_ssm[:, g, :], in1=pst[:], op=ALU.add),
                 R=[bpst, b_Sssm[g]], W=[b_Sssm[g]])
        if pre:
            yield
            return
        ta, bta = TA()
        k.op("dve", lambda e: e.tensor_tensor(
            out=ta[:].rearrange("p (h q) -> p h q", h=8), in0=pin[:].rearrange("p (h q) -> p h q", h=8),
            in1=dtt[:, s, 2, gs].unsqueeze(2).to_broadcast([128, 8, 64]), op=ALU.mult),
             R=[bpin, b_dtt[s]], W=[bta])
        k.op("dve", lambda e: e.tensor_tensor(out=ta[:], in0=ta[:], in1=py[:], op=ALU.add), R=[bta, bpy], W=[bta])
        unpin(bpy)
        unpin(bpin)
        ta2, bta2 = TA()
        k.op("pool", lambda e: e.tensor_tensor(
            out=ta2[:].rearrange("p (h q) -> p h q", h=8), in0=xtm[:, s, :].rearrange("p (h q) -> p h q", h=8),
            in1=hv[:, 2, gs].unsqueeze(2).to_broadcast([128, 8, 64]), op=ALU.mult), R=[bxt, b_setup], W=[bta2])
        k.op("pool", lambda e: e.tensor_tensor(out=ta[:], in0=ta[:], in1=ta2[:], op=ALU.add),
             R=[bta, bta2], W=[bta])
        k.op("pool", lambda e: e.tensor_tensor(out=ta[:], in0=ta[:], in1=zs[:, s, :], op=ALU.mult),
             R=[bta, bzs], W=[bta])
        st, bst = stat[i], b_stat[i]
        k.op("dve", lambda e: e.memset(st[:, 10:11], 0.0), W=[bst])
        k.op("act", lambda e: e.activation(out=ta2[:], in_=ta[:], func=AF.Square, accum_out=st[:, 10:11]),
             R=[bta], W=[bta2, bst])
        k.op("act", lambda e: e.activation(out=st[:, 11:12], in_=st[:, 10:11], func=AF.Sqrt, bias=EPS,
                                           scale=1.0 / 512.0), R=[bst], W=[bst])
        k.op("dve", lambda e: e.reciprocal(out=st[:, 11:12], in_=st[:, 11:12]), R=[bst], W=[bst])
        tb, btb = onh[i], b_onh[i]
        k.op("dve", lambda e: e.tensor_scalar(out=tb[:], in0=ta[:], scalar1=st[:, 11:12], scalar2=None,
                                              op0=ALU.mult), R=[bta, bst], W=[btb])
        yield
        pt, bpt = PS()
        ptb = pt[:].bitcast(BF16)
        for ci in range(4):
            k.op("pe", lambda e, ci=ci: e.transpose(ptb[:, ci * 128:(ci + 1) * 128], tb[:, ci * 128:(ci + 1) * 128],
                                                    identb[:]), R=[btb, b_setup], W=[bpt])
        for ci in range(4):
            c = g * 4 + ci
            k.op("dve", lambda e, ci=ci, c=c: e.tensor_scalar(out=RR[:, c, sl], in0=ptb[:, ci * 128:(ci + 1) * 128],
                                                              scalar1=gncols[:, 2, c:c + 1], scalar2=None,
                                                              op0=ALU.mult), R=[bpt, b_setup], W=[b_RR[c]])
        yield

    def merge(which, Tt):
        wbr, offg = ("w_br_ret", OFF_GR) if which == 0 else ("w_br_ssm", OFF_GS)
        for cp in range(DC // 2):
            banks = [PS(), PS()]
            for kp in range(2):
                wt, bw = wtile([(wbr, kp * 2048, 16, cp * 256, 256, 0)])
                for j in range(2):
                    ps, bps = banks[j]
                    for kc in range(16):
                        k.op("pe", lambda e, kc=kc, j=j, ps=ps, wt=wt, kp=kp: e.matmul(
                            ps[:, 0:Tt], wt[:, kc, j * 128:(j + 1) * 128], RR[:, kp * 16 + kc, 0:Tt],
                            start=(kp == 0 and kc == 0), stop=(kp == 1 and kc == 15)),
                             R=[bw, b_RR[kp * 16 + kc]], W=[bps])
                wrel()
            gl = proj_fm("w_in", offg + cp * 256, 2, lambda kc: hT[:, kc, 0:Tt], DC, [b_hT])
            for j in range(2):
                ps, bps = banks[j]
                (_, pg, bpg) = gl[j]
                c = cp * 2 + j
                ta, bta = TA()
                k.op("act", lambda e, pg=pg, ta=ta: e.activation(out=ta[:, 0:Tt], in_=pg[:, 0:Tt], func=AF.Sigmoid),
                     R=[bpg], W=[bta])
                if which == 0:
                    k.op("dve", lambda e, ps=ps, ta=ta, c=c: e.tensor_tensor(out=RR[:, 32 + c, 0:Tt], in0=ps[:, 0:Tt],
                                                                            in1=ta[:, 0:Tt], op=ALU.mult),
                         R=[bps, bta], W=[b_RR[32 + c]])
                else:
                    k.op("dve", lambda e, ps=ps, ta=ta: e.tensor_tensor(out=ta[:, 0:Tt], in0=ps[:, 0:Tt],
                                                                       in1=ta[:, 0:Tt], op=ALU.mult),
                         R=[bps, bta], W=[bta])
                    k.op("pool", lambda e, ta=ta, c=c: e.tensor_tensor(out=RR[:, 32 + c, 0:Tt], in0=ta[:, 0:Tt],
                                                                      in1=RR[:, 32 + c, 0:Tt], op=ALU.add),
                         R=[bta, b_RR[32 + c]], W=[b_RR[32 + c]])

    def out_proj(Tt):
        for cp in range(DC // 2):
            for (j, ps, bps) in proj_fm("w_out", cp * 256, 2, lambda kc: RR[:, 32 + kc, 0:Tt], DC, b_RR[32:48]):
                c = cp * 2 + j
                k.op("dve", lambda e, ps=ps, c=c: e.tensor_tensor(out=xT[:, c, 0:Tt], in0=ps[:, 0:Tt],
                                                                  in1=xT[:, c, 0:Tt], op=ALU.add),
                     R=[bps, b_xT[c]], W=[b_xT[c]])

    def ple(p_ap, tok0, Tt, NSt):
        norm_to_hT(3, Tt)
        for s in range(NSt):
            k.dma([(pstage[:], p_ap[tok0 + s * 128: tok0 + (s + 1) * 128, :])], W=[b_pstage])
            ps, bps = PS()
            for j in range(2):
                k.op("pe", lambda e, j=j, ps=ps: e.transpose(ps[:, j * 128:(j + 1) * 128],
                                                             pstage[:, j * 128:(j + 1) * 128], ident[:]),
                     R=[b_pstage, b_setup], W=[bps])
            k.op("act", lambda e, s=s, ps=ps: e.copy(out=pT[:, :, s * 128:(s + 1) * 128],
                                                     in_=ps[:, 0:256].rearrange("p (a b) -> p a b", a=2)),
                 R=[bps], W=[b_pT])
        for cp in range(DC // 2):
            gl = proj_fm("w_ple_gate", cp * 256, 2, lambda kc: hT[:, kc, 0:Tt], DC, [b_hT])
            pl = proj_fm("w_ple", cp * 256, 2, lambda kc: pT[:, kc, 0:Tt], 2, [b_pT])
            for j in range(2):
                (_, pg, bpg) = gl[j]
                (_, pp, bpp) = pl[j]
                c = cp * 2 + j
                ta, bta = TA()
                k.op("act", lambda e, pg=pg, ta=ta: e.activation(out=ta[:, 0:Tt], in_=pg[:, 0:Tt], func=AF.Sigmoid),
                     R=[bpg], W=[bta])
                k.op("dve", lambda e, pp=pp, ta=ta: e.tensor_tensor(out=ta[:, 0:Tt], in0=pp[:, 0:Tt], in1=ta[:, 0:Tt],
                                                                   op=ALU.mult), R=[bpp, bta], W=[bta])
                k.op("pool", lambda e, ta=ta, c=c: e.tensor_tensor(out=xT[:, c, 0:Tt], in0=ta[:, 0:Tt],
                                                                  in1=xT[:, c, 0:Tt], op=ALU.add),
                     R=[bta, b_xT[c]], W=[b_xT[c]])

    def final_store(y_ap, tok0, Tt, NSt):
        rms_stats(Tt)
        for s in range(NSt):
            for hf in range(2):
                for cb2 in range(2):
                    cb = hf * 2 + cb2
                    ps, bps = PS()
                    for j in range(4):
                        c = cb * 4 + j
                        ta, bta = TA()
                        k.op("dve", lambda e, c=c, ta=ta, s=s: e.scalar_tensor_tensor(
                            out=ta[:, 0:128], in0=xT[:, c, s * 128:(s + 1) * 128], scalar=gcols[:, 4, c:c + 1],
                            in1=rstd[:, s * 128:(s + 1) * 128], op0=ALU.mult, op1=ALU.mult),
                             R=[b_xT[c], b_rstd, b_setup], W=[bta])
                        k.op("pe", lambda e, j=j, ta=ta, ps=ps: e.transpose(ps[:, j * 128:(j + 1) * 128], ta[:, 0:128],
                                                                            ident[:]), R=[bta, b_setup], W=[bps])
                    k.op("act", lambda e, cb2=cb2, ps=ps: e.copy(out=ystage[:, cb2 * 512:(cb2 + 1) * 512], in_=ps[:]),
                         R=[bps], W=[b_ystage])
                k.dma([(y_ap[tok0 + s * 128: tok0 + (s + 1) * 128, hf * 1024:(hf + 1) * 1024], ystage[:])],
                      R=[b_ystage], is_out=True)

    def load_rope(region, tok0, Tt):
        k.dma([(ropec[:, 0:Tt], I["cos_" + region][:, tok0:tok0 + Tt]),
               (ropes[:, 0:Tt], I["sin_" + region][:, tok0:tok0 + Tt])], W=[b_rope])

    def drain(g):
        for _ in g:
            pass

    def interleave(cg, wg):
        cdone = cg is None
        wdone = wg is None
        while not (cdone and wdone):
            if not wdone:
                try:
                    next(wg)
                except StopIteration:
                    wdone = True
            if not cdone:
                try:
                    next(cg)
                except StopIteration:
                    cdone = True

    def chain(gens):
        for g in gens:
            for _ in g:
                yield

    def mixer(Tt, NSt, off, cdec, nq, pre, smp, L, nseg):
        norm_to_hT(1, Tt)
        dt_path(Tt, NSt, off, nq, pre)
        drain(ret_w(0, 0, Tt, NSt, off, nq, pre))
        for h in range(RH):
            cg = chain([ret_chunk(h, h % 2, s, Tt, off, cdec, nq, pre, smp) for s in range(NSt)])
            if h + 1 < RH:
                wg = ret_w(h + 1, (h + 1) % 2, Tt, NSt, off, nq, pre)
            else:
                wg = ssm_w(0, 0, Tt, NSt, off, nq, pre, L, nseg)
            interleave(cg, wg)
        if not pre:
            tap("retT")
            merge(0, Tt)
        for g in range(SG):
            cg = chain([ssm_chunk(g, g % 2, s, Tt, off, nq, pre, smp) for s in range(NSt)])
            wg = ssm_w(g + 1, (g + 1) % 2, Tt, NSt, off, nq, pre, L, nseg) if g + 1 < SG else None
            interleave(cg, wg)
        if not pre:
            tap("ssmT")
            merge(1, Tt)

    def apply_flag():
        for h in range(RH):
            for d in range(2):
                k.op("pool", lambda e, h=h, d=d: e.tensor_scalar(out=S_ret[:, h, d, :], in0=S_ret[:, h, d, :],
                                                                 scalar1=flagt[:, 0:1], scalar2=None, op0=ALU.mult),
                     R=[b_Sret[h][d], b_setup], W=[b_Sret[h][d]])
        for g in range(SG):
            k.op("pool", lambda e, g=g: e.tensor_scalar(out=S_ssm[:, g, :], in0=S_ssm[:, g, :], scalar1=flagt[:, 0:1],
                                                        scalar2=None, op0=ALU.mult),
                 R=[b_Sssm[g], b_setup], W=[b_Sssm[g]])
        for c in range(48):
            k.op("pool", lambda e, c=c: e.tensor_scalar(out=conv_hist[:, c, 0, :], in0=conv_hist[:, c, 0, :],
                                                        scalar1=flagt[:, 0:1], scalar2=None, op0=ALU.mult),
                 R=[b_chist[c], b_setup], W=[b_chist[c]])

    def store_states_main():
        for h in range(RH):
            k.dma([(O["ret_main"][h].rearrange("(dh p) e -> p dh e", p=128), S_ret[:, h, :, :])],
                  R=[b_Sret[h][0], b_Sret[h][1]], is_out=True)
        for g in range(SG):
            j = g % 2
            pt, bpt = PS()
            for a in range(4):
                k.op("pe", lambda e, a=a, pt=pt, g=g: e.transpose(pt[:, a * 128:(a + 1) * 128],
                                                                  S_ssm[:, g, a * 128:(a + 1) * 128], ident[:]),
                     R=[b_Sssm[g], b_setup], W=[bpt])
            k.op("act", lambda e, j=j, pt=pt: e.copy(out=s1buf[j][:, 0, :], in_=pt[:]), R=[bpt], W=[b_s1buf[j]])
            k.dma([(O["ssm_main"][g * 512:(g + 1) * 512, :].rearrange("(a r) n -> r a n", r=128),
                    s1buf[j][:, 0, :].rearrange("p (a b) -> p a b", a=4))], R=[b_s1buf[j]], is_out=True)
        conv_out(conv_hist, b_chist, 1, O["conv_main"])

    def conv_out(src, bsrc, nseg, out_ap):
        n = nseg * 3
        for pc in range(6):
            for cb2 in range(2):
                ps, bps = PS()
                for j in range(4):
                    c = pc * 8 + cb2 * 4 + j
                    k.op("pe", lambda e, j=j, c=c, ps=ps: e.transpose(
                        ps[0:n, j * 128:(j + 1) * 128], src[:, c, 0:nseg, :].rearrange("p q t -> p (q t)"), ident[:]),
                         R=[bsrc[c], b_setup], W=[bps])
                k.op("act", lambda e, cb2=cb2, ps=ps: e.copy(out=ystage[0:n, cb2 * 512:(cb2 + 1) * 512],
                                                             in_=ps[0:n, :]), R=[bps], W=[b_ystage])
            k.dma([(out_ap[:, pc * 1024:(pc + 1) * 1024], ystage[0:n, :])], R=[b_ystage], is_out=True)

    def conv_in_smp():
        for pc in range(6):
            k.dma([(ystage[0:12, :], I["sconv_in"][:, pc * 1024:(pc + 1) * 1024])], W=[b_ystage])
            for cb2 in range(2):
                ps, bps = PS()
                for j in range(4):
                    cl = cb2 * 4 + j
                    k.op("pe", lambda e, j=j, cl=cl, ps=ps: e.transpose(
                        ps[:, j * 12:(j + 1) * 12], ystage[0:12, cl * 128:(cl + 1) * 128], ident[0:12, 0:12]),
                         R=[b_ystage, b_setup], W=[bps])
                for j in range(4):
                    c = pc * 8 + cb2 * 4 + j
                    k.op("act", lambda e, j=j, c=c, ps=ps: e.copy(
                        out=conv_hist[:, c, :, :].rearrange("p q t -> p (q t)"), in_=ps[:, j * 12:(j + 1) * 12]),
                         R=[bps], W=[b_chist[c]])

    tapped = set()

    def tap(name):
        if name not in DEBUG_TAPS or name in tapped:
            return
        tapped.add(name)
        if name in ("retT", "ssmT", "merged", "hidden"):
            k.dma([(O["dbg_" + name], RR[:, :, :])], R=b_RR, q="pool", is_out=True)
        else:
            k.dma([(O["dbg_" + name], xT[:, :, :])], R=b_xT, q="pool", is_out=True)

    cast_all()
    setup()
    load_consts("P")
    for t in range(NPRE):
        load_x(I["x_pre"], t * T, T, NS)
        load_rope("pre", t * T, T)
        ffn("w1_gu", "w1_down", 0, T)
        mixer(T, NS, offP, cdecP, 1, True, False, T, 1)
    apply_flag()
    for t in range(NMAIN):
        load_x(I["x_main"], t * T, T, NS)
        load_rope("main", t * T, T)
        ffn("w1_gu", "w1_down", 0, T)
        tap("x1")
        mixer(T, NS, offP, cdecP, 1, False, False, T, 1)
        tap("merged")
        out_proj(T)
        tap("x2")
        ffn("w2_gu", "w2_down", 2, T)
        tap("x3")
        ple(I["p_main"], t * T, T, NS)
        tap("x4")
        final_store(O["y_main"], t * T, T, NS)
    store_states_main()
    if with_smp:
        load_consts("S")
        conv_in_smp()
        load_x(I["x_smp"], 0, 128, 1)
        load_rope("smp", 0, 128)
        ffn("w1_gu", "w1_down", 0, 128)
        mixer(128, 1, offS, cdecS, 4, False, True, 32, 4)
        out_proj(128)
        ffn("w2_gu", "w2_down", 2, 128)
        ple(I["p_smp"], 0, 128, 1)
        final_store(O["y_smp"], 0, 128, 1)
        conv_out(conv_new, b_cnew, 4, O["conv_smp"])


_NC_CACHE = {}
LAST_RESULTS = None


def run(inputs, NPRE, NMAIN, T=256, with_smp=True, ncores=8):
    f32 = np.float32
    xp = np.asarray(inputs["x_prompt"], f32)
    seq = xp.shape[1]
    half = seq // 2
    assert half == NMAIN * T and (NPRE == NMAIN)
    key = (NPRE, NMAIN, T, with_smp)
    if key not in _NC_CACHE:
        import time as _t
        _t0 = _t.time()
        _NC_CACHE[key] = build(NPRE, NMAIN, T, with_smp)
        print("[kernel] build %.1fs" % (_t.time() - _t0), flush=True)
    nc = _NC_CACHE[key]
    constP = make_consts("P")[0]
    constS = make_consts("S")[0]
    cP = np.zeros((128, CONST_W), f32)
    cP[:, :constP.shape[1]] = constP
    cS = np.zeros((128, CONST_W), f32)
    cS[:, :constS.shape[1]] = constS
    ident = np.eye(128, dtype=f32)
    cos_a, sin_a = rope_tables(np.arange(0, half))
    cos_b, sin_b = rope_tables(np.arange(half, seq))
    cos_s, sin_s = rope_tables(np.tile(PAST + np.arange(32), 4))
    shared = {}
    for n in ("w1_gu", "w1_down", "w_in", "w_br_ret", "w_br_ssm", "w_out", "w2_gu", "w2_down", "w_ple", "w_ple_gate"):
        shared[n] = np.ascontiguousarray(np.asarray(inputs[n], f32)[0])
    def colv(v, nchunk):
        return np.ascontiguousarray(np.asarray(v, f32).reshape(nchunk, 128).T)
    gc = np.stack([colv(inputs[n][0], DC) for n in ("g_ffn1", "g_mix", "g_ffn2", "g_ple")] +
                  [colv(inputs["g_final"], DC)], axis=1)
    shared["gcols"] = np.ascontiguousarray(gc)
    shared["gncols"] = np.ascontiguousarray(np.stack([colv(inputs[n][0], 32) for n in
                                                      ("ret_gn_g", "ret_gn_b", "ssm_norm_g")], axis=1))
    cwv = np.asarray(inputs["conv_w"], f32)[0]
    shared["cwcols"] = np.ascontiguousarray(np.stack([colv(cwv[w], 48) for w in range(4)] +
                                                     [colv(inputs["conv_b"][0], 48)], axis=1))
    shared["hvin"] = np.ascontiguousarray(np.broadcast_to(
        np.stack([np.asarray(inputs[n], f32)[0] for n in ("dt_bias", "a_log", "d_skip")], axis=0)[None], (128, 3, 64)))
    shared["constP"] = cP
    shared["constS"] = cS
    shared["ident"] = ident
    in_maps = []
    for c in range(ncores):
        b, hf = c // 2, c % 2
        m = dict(shared)
        m["x_pre"] = np.ascontiguousarray(xp[b, 0:half])
        m["x_main"] = np.ascontiguousarray(xp[b, hf * half:(hf + 1) * half])
        m["p_main"] = np.ascontiguousarray(np.asarray(inputs["p_prompt"], f32)[0, b, hf * half:(hf + 1) * half])
        m["x_smp"] = np.ascontiguousarray(np.asarray(inputs["x_sample"], f32)[4 * c:4 * c + 4].reshape(128, D))
        m["p_smp"] = np.ascontiguousarray(np.asarray(inputs["p_sample"], f32)[0, 4 * c:4 * c + 4].reshape(128, 256))
        m["sret_in"] = np.ascontiguousarray(np.asarray(inputs["state_ret"], f32)[0, 4 * c:4 * c + 4])
        m["sssm_in"] = np.ascontiguousarray(np.asarray(inputs["state_ssm"], f32)[0, 4 * c:4 * c + 4])
        m["sconv_in"] = np.ascontiguousarray(np.asarray(inputs["state_conv"], f32)[0, 4 * c:4 * c + 4].reshape(12, 6144))
        m["flag"] = np.full((128, 1), float(hf), f32)
        m["cos_pre"], m["sin_pre"] = cos_a, sin_a
        m["cos_main"], m["sin_main"] = (cos_a, sin_a) if hf == 0 else (cos_b, sin_b)
        m["cos_smp"], m["sin_smp"] = cos_s, sin_s
        in_maps.append(m)
    import time as _t
    _t0 = _t.time()
    res = run_bass_kernel_spmd(nc, in_maps, core_ids=list(range(ncores)))
    print("[kernel] spmd launch wall %.1fs" % (_t.time() - _t0), flush=True)
    R = res.results
    global LAST_RESULTS
    LAST_RESULTS = R
    nb = xp.shape[0]
    y_prompt = np.zeros((nb, seq, D), f32)
    ret_p = np.zeros((1, nb, RH, 256, 512), f32)
    ssm_p = np.zeros((1, nb, 64, 64, 128), f32)
    conv_p = np.zeros((1, nb, 3, 6144), f32)
    nsb = np.asarray(inputs["x_sample"]).shape[0]
    y_sample = np.zeros((nsb, 32, D), f32)
    ret_s = np.zeros((1, nsb, RH, 256, 512), f32)
    ssm_s = np.zeros((1, nsb, 64, 64, 128), f32)
    conv_s = np.zeros((1, nsb, 3, 6144), f32)
    for c in range(ncores):
        b, hf = c // 2, c % 2
        y_prompt[b, hf * half:(hf + 1) * half] = R[c]["y_main"]
        if hf == 1:
            ret_p[0, b] = R[c]["ret_main"]
            ssm_p[0, b] = R[c]["ssm_main"].reshape(64, 64, 128)
            conv_p[0, b] = R[c]["conv_main"]
        if with_smp:
            y_sample[4 * c:4 * c + 4] = R[c]["y_smp"].reshape(4, 32, D)
            ret_s[0, 4 * c:4 * c + 4] = R[c]["ret_smp"]
            ssm_s[0, 4 * c:4 * c + 4] = R[c]["ssm_smp"].reshape(4, 64, 64, 128)
            conv_s[0, 4 * c:4 * c + 4] = R[c]["conv_smp"].reshape(4, 3, 6144)
    return (y_prompt, y_sample, ret_p, ssm_p, conv_p, ret_s, ssm_s, conv_s)


def kernel(**inputs):
    return run(inputs, 16, 16, 256, True, 8)
```

```python
import numpy as np
from contextlib import ExitStack
import concourse.bass as bass
import concourse.mybir as mybir
from concourse.bass_utils import run_bass_kernel_spmd

F32 = mybir.dt.float32
BF16 = mybir.dt.bfloat16
AF = mybir.ActivationFunctionType
ALU = mybir.AluOpType

D = 2048
DC = 16
FF = 5632
FC = 44
NIN = 26688
RH = 8
SG = 8
EPS = 1e-6
PAST = 4096
OFF_Q, OFF_K, OFF_V, OFF_RG, OFF_Z, OFF_X, OFF_B, OFF_C, OFF_DT, OFF_GR, OFF_GS = (
    0, 2048, 4096, 8192, 12288, 16384, 20480, 21504, 22528, 22592, 24640)
SAME_SYNC = True
WSLOTS = 3
WCOLS = 256


class Buf:
    __slots__ = ("name", "w", "r", "region", "lo", "hi", "dkey", "dcnt")

    def __init__(self, name, region=None, lo=0, hi=1):
        self.name = name
        self.w = None
        self.r = {}
        self.region = region
        self.lo = lo
        self.hi = hi
        self.dkey = None
        self.dcnt = 0
        if region is not None:
            region.append(self)

    def overl(self):
        if self.region is None:
            return (self,)
        return [b for b in self.region if b.lo < self.hi and self.lo < b.hi]


class K:
    def __init__(self, nc, es, dry):
        self.nc = nc
        self.es = es
        self.dry = dry
        self.eng = {"pe": nc.tensor, "act": nc.scalar, "dve": nc.vector, "pool": nc.gpsimd, "sp": nc.sync}
        self.sems = {}
        self.cnt = {}
        self.waited = {e: {} for e in self.eng}
        self.nsem = 0
        self.wreq = []
        self.wpos = 0
        self.wissued = 0
        self.out_events = []
        if not dry:
            for e in ("pe", "act", "dve", "pool"):
                self.sems[e] = es.enter_context(nc.semaphore("s_" + e))
                self.cnt[e] = 0
                self.nsem += 1

    def _wait(self, eng, key, val):
        if key == eng and (eng == "pe" or not SAME_SYNC):
            return
        wd = self.waited[eng]
        if wd.get(key, 0) >= val:
            return
        self.eng[eng].wait_ge(self.sems[key], val)
        wd[key] = val

    def _deps(self, eng, R, W):
        need = {}

        def add(ev):
            if ev is not None and need.get(ev[0], 0) < ev[1]:
                need[ev[0]] = ev[1]
        for b in R:
            for o in b.overl():
                add(o.w)
        for b in W:
            for o in b.overl():
                add(o.w)
                for k_, v_ in o.r.items():
                    add((k_, v_))
        for k_, v_ in need.items():
            self._wait(eng, k_, v_)

    def _record(self, ev, R, W):
        for b in R:
            for o in b.overl():
                if o.r.get(ev[0], 0) < ev[1]:
                    o.r[ev[0]] = ev[1]
        for b in W:
            for o in b.overl():
                o.w = ev
                o.r = {}

    def op(self, eng, fn, R=(), W=()):
        if self.dry:
            return
        self._deps(eng, R, W)
        ins = fn(self.eng[eng])
        self.cnt[eng] += 1
        ins.then_inc(self.sems[eng], 1)
        self._record((eng, self.cnt[eng]), R, W)

    def _dsem(self, b):
        if b.dkey is None:
            b.dkey = "d_" + b.name
            self.sems[b.dkey] = self.es.enter_context(self.nc.semaphore(b.dkey))
            self.nsem += 1
        return b.dkey

    def dma(self, pieces, R=(), W=(), q="sp", is_out=False):
        if self.dry:
            return
        self._deps(q, R, W)
        b = W[0] if W else R[0]
        key = self._dsem(b)
        for (o, i) in pieces:
            self.eng[q].dma_start(out=o, in_=i).then_inc(self.sems[key], 16)
            b.dcnt += 16
        ev = (key, b.dcnt)
        self._record(ev, R, W)
        if is_out:
            self.out_events.append(ev)

    def finish(self):
        if self.dry:
            return
        for ev in self.out_events:
            self._wait("sp", ev[0], ev[1])


def _gammas():
    return 1.0 - np.exp2(-5.0 - np.arange(RH, dtype=np.float64))


def make_consts(kind):
    if kind == "P":
        nq, blk, pos, clen = 1, np.zeros(128, int), np.arange(128), 128
    else:
        nq, blk, pos, clen = 4, np.arange(128) // 32, np.arange(128) % 32, 32
    g = _gammas()
    same = blk[:, None] == blk[None, :]
    ar = np.arange(128)
    triT = ((ar[:, None] <= ar[None, :]) & same).astype(np.float64)
    SL = ((ar[:, None] > ar[None, :]) & same).astype(np.float64)
    maskT = ((ar[None, :] >= ar[:, None]) & same).astype(np.float64)
    dif = np.maximum(ar[None, :] - ar[:, None], 0)
    dmatT = np.stack([np.power(g[h], dif) * maskT * (256.0 ** -0.5) for h in range(RH)], axis=1)
    qdec = np.stack([np.power(g[h], pos + 1.0) for h in range(RH)], axis=0)
    qdec_bc = np.broadcast_to(qdec[None], (128, RH, 128))
    kdec = np.stack([np.power(g[h], clen - 1.0 - pos) * (256.0 ** -0.5) for h in range(RH)], axis=1)
    kdecq = np.stack([kdec * (blk == q)[:, None] for q in range(nq)], axis=1)
    colmask = np.stack([np.broadcast_to((blk == q)[None, :], (128, 128)) for q in range(nq)], axis=1)
    rowmask = np.stack([(blk == q) for q in range(nq)], axis=1).astype(np.float64)
    seqones = np.stack([np.broadcast_to((blk == q)[:, None], (128, 128)) for q in range(nq)], axis=1)
    parts = [("triT", triT), ("SL", SL), ("maskT", maskT), ("dmatT", dmatT.reshape(128, -1)),
             ("qdec", qdec_bc.reshape(128, -1)), ("kdecq", kdecq.reshape(128, -1)),
             ("colmask", colmask.reshape(128, -1)), ("rowmask", rowmask), ("seqones", seqones.reshape(128, -1))]
    offs = {}
    o = 0
    for n, a in parts:
        offs[n] = (o, a.shape[1])
        o += a.shape[1]
    arr = np.concatenate([np.asarray(a, np.float64) for _, a in parts], axis=1).astype(np.float32)
    cdec = [float(np.power(g[h], float(clen))) for h in range(RH)]
    return arr, offs, cdec, nq


CONST_W = 3500


def rope_tables(pos):
    half = 128
    inv = (10000.0 ** (-np.arange(half, dtype=np.float32) / np.float32(half))).astype(np.float32)
    ang = pos.astype(np.float32)[None, :] * inv[:, None]
    return np.cos(ang).astype(np.float32), np.sin(ang).astype(np.float32)


DEBUG_TAPS = []


def build(NPRE, NMAIN, T=256, with_smp=True):
    NS = T // 128
    nc = bass.Bass("TRN2", target_bir_lowering=False)
    npre_tok, nmain_tok = NPRE * T, NMAIN * T

    def din(name, shape, dt=F32):
        return nc.dram_tensor(name, list(shape), dt, kind="ExternalInput").ap()

    def dout(name, shape, dt=F32):
        return nc.dram_tensor(name, list(shape), dt, kind="ExternalOutput").ap()

    def dint(name, shape, dt=BF16):
        return nc.dram_tensor(name, list(shape), dt, kind="Internal").ap()

    I = {}
    I["x_pre"] = din("x_pre", [max(npre_tok, 1), D])
    I["x_main"] = din("x_main", [nmain_tok, D])
    I["p_main"] = din("p_main", [nmain_tok, 256])
    I["x_smp"] = din("x_smp", [128, D])
    I["p_smp"] = din("p_smp", [128, 256])
    I["sret_in"] = din("sret_in", [4, RH, 256, 512])
    I["sssm_in"] = din("sssm_in", [4, 64, 64, 128])
    I["sconv_in"] = din("sconv_in", [12, 6144])
    I["flag"] = din("flag", [128, 1])
    for r, n in (("pre", max(npre_tok, 1)), ("main", nmain_tok), ("smp", 128)):
        I["cos_" + r] = din("cos_" + r, [128, n])
        I["sin_" + r] = din("sin_" + r, [128, n])
    I["constP"] = din("constP", [128, CONST_W])
    I["constS"] = din("constS", [128, CONST_W])
    I["ident"] = din("ident", [128, 128])
    wshapes = {"w1_gu": (D, 2 * FF), "w1_down": (FF, D), "w_in": (D, NIN), "w_br_ret": (4096, D),
               "w_br_ssm": (4096, D), "w_out": (D, D), "w2_gu": (D, 2 * FF), "w2_down": (FF, D),
               "w_ple": (256, D), "w_ple_gate": (D, D)}
    WF = {n: din(n, s) for n, s in wshapes.items()}
    WB = {n: dint(n + "_bf", s) for n, s in wshapes.items()}
    I["gcols"] = din("gcols", [128, 5, DC])
    I["gncols"] = din("gncols", [128, 3, 32])
    I["cwcols"] = din("cwcols", [128, 5, 48])
    I["hvin"] = din("hvin", [128, 3, 64])
    O = {}
    O["y_main"] = dout("y_main", [nmain_tok, D])
    O["y_smp"] = dout("y_smp", [128, D])
    O["ret_main"] = dout("ret_main", [RH, 256, 512])
    O["ssm_main"] = dout("ssm_main", [64 * 64, 128])
    O["conv_main"] = dout("conv_main", [3, 6144])
    O["ret_smp"] = dout("ret_smp", [4, RH, 256, 512])
    O["ssm_smp"] = dout("ssm_smp", [4, 64 * 64, 128])
    O["conv_smp"] = dout("conv_smp", [12, 6144])

    for tn in DEBUG_TAPS:
        rr = tn in ("retT", "ssmT", "merged", "hidden")
        O["dbg_" + tn] = dout("dbg_" + tn, [128, 48 if rr else DC, T], BF16 if rr else F32)
    constP, offP, cdecP, _ = make_consts("P")
    constS, offS, cdecS, _ = make_consts("S")

    wreq_holder = []
    for dry in (True, False):
        es = ExitStack()
        k = K(nc, es, dry)
        if not dry:
            k.wreq = wreq_holder[0]
        _emit(nc, es, k, I, O, WF, WB, wshapes, NPRE, NMAIN, T, NS, with_smp, offP, cdecP, offS, cdecS)
        if dry:
            wreq_holder.append(k.wreq)
            es.close()
        else:
            assert k.wpos == len(k.wreq), (k.wpos, len(k.wreq))
            k.finish()
            es.close()
    return nc


def _emit(nc, es, k, I, O, WF, WB, wshapes, NPRE, NMAIN, T, NS, with_smp, offP, cdecP, offS, cdecS):
    dry = k.dry
    TP = T + 3

    class Dummy:
        def __getitem__(self, _):
            return self

        def __getattr__(self, _):
            return lambda *a, **kw: self

    def sb(name, shape, dt=F32):
        if dry:
            return Dummy()
        return es.enter_context(nc.sbuf_tensor("sb_" + name, list(shape), dt))

    def B(name):
        return Buf(name)

    xT = sb("xT", [128, DC, T])
    b_xT = [B("xT%d" % c) for c in range(DC)]
    hT = sb("hT", [128, DC, T], BF16)
    b_hT = B("hT")
    wring = sb("wring", [128, WSLOTS, 16 * WCOLS], BF16)
    b_w = [B("w%d" % i) for i in range(WSLOTS)]
    S_ret = sb("S_ret", [128, RH, 2, 512])
    Sreg = []
    b_Sret = [[Buf("Sret%d_%d" % (h, d), Sreg, (h * 2 + d) * 2048, (h * 2 + d + 1) * 2048) for d in range(2)]
              for h in range(RH)]
    S_ssm = sb("S_ssm", [128, SG, 512])
    b_Sssm = [B("Sssm%d" % g) for g in range(SG)]
    RR = sb("RR", [128, 48, T], BF16)
    b_RR = [B("RR%d" % j) for j in range(48)]
    conv_hist = sb("conv_hist", [128, 48, 4, 3])
    b_chist = [B("chist%d" % c) for c in range(48)]
    if dry:
        conv_new = Dummy()
    else:
        conv_new = S_ret[:, 5, :, :].rearrange("p a b -> p (a b)")[:, 0:576].rearrange("p (c q t) -> p c q t", c=48, q=4)
    b_cnew = [Buf("cnew%d" % c, Sreg, 5 * 4096 + c * 48, 5 * 4096 + (c + 1) * 48) for c in range(48)]
    USZ = 6144
    U = [sb("U%d" % u, [128, USZ], BF16) for u in range(2)]
    Ureg = [[], []]

    def uview(u, name, lo_b, shape, dt):
        n = int(np.prod(shape))
        nb = n * (4 if dt == F32 else 2)
        buf = Buf("U%d_%s" % (u, name), Ureg[u], lo_b, lo_b + nb)
        if dry:
            return Dummy(), buf
        ap = U[u][:, lo_b // 2:(lo_b + nb) // 2]
        if dt == F32:
            ap = ap.bitcast(F32)
        if len(shape) == 2:
            ap = ap.rearrange("p (a b) -> p a b", a=shape[0])
        elif len(shape) == 3:
            ap = ap.rearrange("p (a b c) -> p a b c", a=shape[0], b=shape[1])
        return ap, buf

    RV = []
    SV = []
    for u in range(2):
        o = 0
        d = {}
        for name, shape, dt in (("qT", (2, T), BF16), ("kT", (2, T), BF16), ("qdT", (2, T), F32),
                                ("kdtm", (4, 256), BF16), ("vtm", (NS, 512), BF16),
                                ("grgs", (4, T), BF16), ("brgs", (4, T), BF16)):
            d[name] = uview(u, "r_" + name, o, shape, dt)
            o += int(np.prod(shape)) * (4 if dt == F32 else 2)
        assert o <= USZ * 2, o
        RV.append(d)
        d = {}
        lay = (("zs", (NS, 512), BF16, 0), ("xpad", (4, TP + 9), BF16, 2048), ("xtm", (NS, 512), BF16, 2048),
               ("bcpad", (2, TP + 9), BF16, 4224), ("Btm", (NS, 128), BF16, 4224),
               ("xcT", (4, T), BF16, 5312), ("xdt", (NS, 512), BF16, 5312),
               ("BT", (1, T), F32, 7360), ("CT", (1, T), F32, 8384))
        for name, shape, dt, o in lay:
            d[name] = uview(u, "s_" + name, o, shape, dt)
            assert o + int(np.prod(shape)) * (4 if dt == F32 else 2) <= USZ * 2
        SV.append(d)

    xstage = sb("xstage", [128, 1024])
    b_xstage = B("xstage")
    ystage = sb("ystage", [128, 1024])
    b_ystage = B("ystage")
    pstage = sb("pstage", [128, 256])
    b_pstage = B("pstage")
    pT = sb("pT", [128, 2, T], BF16)
    b_pT = B("pT")
    ropec = sb("ropec", [128, T])
    ropes = sb("ropes", [128, T])
    b_rope = B("rope")
    consts = sb("consts", [128, CONST_W])
    b_consts = B("consts")
    ident = sb("ident_sb", [128, 128])
    identb = sb("identb", [128, 128], BF16)
    onesf = sb("onesf", [128, 128])
    onesb = sb("onesb", [128, 128], BF16)
    b_setup = B("setup")
    gcols = sb("gcols", [128, 5, DC])
    gncols = sb("gncols", [128, 3, 32])
    cwcols = sb("cwcols", [128, 6, 48])
    hv = sb("hv", [128, 4, 64])
    flagt = sb("flagt", [128, 1])
    sqt = [sb("sqt%d" % i, [128, T], BF16) for i in range(2)]
    b_sqt = [B("sqt%d" % i) for i in range(2)]
    rstd = sb("rstd", [128, T])
    b_rstd = B("rstd")
    tmpA = [sb("tmpA%d" % i, [128, 512]) for i in range(3)]
    b_tmpA = [B("tmpA%d" % i) for i in range(3)]
    tmpB = [sb("tmpB%d" % i, [128, 512], BF16) for i in range(3)]
    b_tmpB = [B("tmpB%d" % i) for i in range(3)]
    dtt = sb("dtt", [128, NS, 6, 64])
    b_dtt = [B("dtt%d" % s) for s in range(NS)]
    decb = sb("decb", [128, NS, 4, 64])
    b_decb = [B("decb%d" % s) for s in range(NS)]
    wendq = sb("wendq", [128, 4, 64])
    b_wendq = B("wendq")
    attm = [sb("attm%d" % i, [128, 128], BF16) for i in range(2)]
    b_attm = [B("attm%d" % i) for i in range(2)]
    cbm = [sb("cbm%d" % i, [128, 128], BF16) for i in range(2)]
    b_cbm = [B("cbm%d" % i) for i in range(2)]
    rhsR_ = sb("rhsR", [128, 8, 128])
    rhsR = [rhsR_, rhsR_]
    b_rhsR_ = B("rhsR")
    b_rhsR = [b_rhsR_, b_rhsR_]
    wTt_ = sb("wTt", [128, 8, 128], BF16)
    wTt = [wTt_, wTt_]
    b_wTt_ = B("wTt")
    b_wTt = [b_wTt_, b_wTt_]
    xwt = [sb("xwt%d" % i, [128, 512], BF16) for i in range(2)]
    b_xwt = [B("xwt%d" % i) for i in range(2)]
    onh = [sb("onh%d" % i, [128, 512], BF16) for i in range(2)]
    b_onh = [B("onh%d" % i) for i in range(2)]
    stat = [sb("stat%d" % i, [128, 16]) for i in range(2)]
    b_stat = [B("stat%d" % i) for i in range(2)]
    if dry:
        s0buf = [Dummy(), Dummy()]
        s1buf = [Dummy(), Dummy()]
        qmk = [Dummy(), Dummy()]
    else:
        s0buf = [S_ret[:, j, :, :] for j in range(2)]
        s1buf = [S_ret[:, 2 + j, :, :] for j in range(2)]
        qmk = [S_ret[:, 4, j, 0:256].rearrange("p (a b) -> p a b", a=2) for j in range(2)]
    b_s0buf = [Buf("s0buf%d" % j, Sreg, j * 4096, (j + 1) * 4096) for j in range(2)]
    b_s1buf = [Buf("s1buf%d" % j, Sreg, (2 + j) * 4096, (3 + j) * 4096) for j in range(2)]
    b_qmk = [Buf("qmk%d" % j, Sreg, 4 * 4096 + j * 2048, 4 * 4096 + j * 2048 + 1024) for j in range(2)]

    psum = []
    b_ps = []
    for i in range(8):
        if dry:
            psum.append(Dummy())
        else:
            psum.append(es.enter_context(nc.psum_tensor("ps%d" % i, [128, 512], F32)))
        b_ps.append(B("ps%d" % i))
    pstate = {"i": 0, "a": 0, "b": 0, "sq": 0}

    pinned = set()

    def PS(pin=False):
        i = pstate["i"]
        while i in pinned:
            i = (i + 1) % 8
        pstate["i"] = (i + 1) % 8
        if pin:
            pinned.add(i)
        return psum[i], b_ps[i]

    def unpin(bps):
        pinned.discard(b_ps.index(bps))

    def TA():
        i = pstate["a"]
        pstate["a"] = (i + 1) % 3
        return tmpA[i], b_tmpA[i]

    def TB():
        i = pstate["b"]
        pstate["b"] = (i + 1) % 3
        return tmpB[i], b_tmpB[i]

    b_cast = {n: B("cast_" + n) for n in WB}

    def cast_all():
        order = ["w1_gu", "w1_down", "w_in", "w_br_ret", "w_br_ssm", "w_out", "w2_gu", "w2_down", "w_ple",
                 "w_ple_gate"]
        for n in order:
            rows, cols = wshapes[n]
            pieces = []
            rb = 128
            for r0 in range(0, rows, rb):
                pieces.append((WB[n][r0:r0 + rb, :], WF[n][r0:r0 + rb, :]))
            k.dma(pieces, W=[b_cast[n]], q="pool")

    def _issue_next():
        if k.dry or k.wissued >= len(k.wreq):
            return
        i = k.wissued
        slot = i % WSLOTS
        pieces = []
        names = set()
        for (n, r0, nkc, c0, ncols, dcol) in k.wreq[i]:
            src = WB[n][r0:r0 + nkc * 128, c0:c0 + ncols].rearrange("(kc p) n -> p kc n", p=128)
            dst = wring[:, slot, :].rearrange("p (kc n) -> p kc n", kc=16)[:, 0:nkc, dcol:dcol + ncols]
            pieces.append((dst, src))
            names.add(n)
        k.dma(pieces, R=[b_cast[n] for n in names], W=[b_w[slot]])
        k.wissued += 1

    def wtile(pieces):
        if k.dry:
            k.wreq.append(tuple(pieces))
            return Dummy(), b_w[0]
        assert tuple(pieces) == k.wreq[k.wpos], (pieces, k.wreq[k.wpos])
        while k.wissued < min(k.wpos + WSLOTS, len(k.wreq)):
            _issue_next()
        slot = k.wpos % WSLOTS
        k.wpos += 1
        return wring[:, slot, :].rearrange("p (kc n) -> p kc n", kc=16), b_w[slot]

    def wrel():
        if k.dry:
            return
        while k.wissued < min(k.wpos + WSLOTS, len(k.wreq)):
            _issue_next()

    def setup():
        k.dma([(ident[:], I["ident"])], W=[b_setup])
        k.op("dve", lambda e: e.tensor_copy(identb[:], ident[:]), R=[b_setup], W=[b_setup])
        k.op("pool", lambda e: e.memset(onesf[:], 1.0), W=[b_setup])
        k.op("pool", lambda e: e.memset(onesb[:], 1.0), W=[b_setup])
        k.dma([(gcols[:], I["gcols"]), (gncols[:], I["gncols"]), (cwcols[:, 0:5, :], I["cwcols"]),
               (hv[:, 0:3, :], I["hvin"]), (flagt[:], I["flag"])], W=[b_setup], q="sp")
        k.op("dve", lambda e: e.tensor_scalar(out=cwcols[:, 5, :], in0=cwcols[:, 4, :], scalar1=-1.0, scalar2=None,
                                              op0=ALU.mult), R=[b_setup], W=[b_setup])
        k.op("act", lambda e: e.activation(out=hv[:, 3, :], in_=hv[:, 1, :], func=AF.Exp), R=[b_setup], W=[b_setup])
        k.op("dve", lambda e: e.tensor_scalar(out=hv[:, 1, :], in0=hv[:, 3, :], scalar1=-1.0, scalar2=None,
                                              op0=ALU.mult), R=[b_setup], W=[b_setup])
        for h in range(RH):
            for d in range(2):
                k.op("pool", lambda e, h=h, d=d: e.memset(S_ret[:, h, d, :], 0.0), W=[b_Sret[h][d]])
        for g in range(SG):
            k.op("pool", lambda e, g=g: e.memset(S_ssm[:, g, :], 0.0), W=[b_Sssm[g]])
        for c in range(48):
            k.op("pool", lambda e, c=c: e.memset(conv_hist[:, c, :, :], 0.0), W=[b_chist[c]])

    def load_consts(which):
        k.dma([(consts[:], I["constP" if which == "P" else "constS"])], W=[b_consts])

    def cv(off, name, n=None):
        o, w = off[name]
        return consts[:, o:o + w]

    def load_x(src_ap, tok0, Tt, NSt):
        for s in range(NSt):
            for hf in range(2):
                k.dma([(xstage[:], src_ap[tok0 + s * 128: tok0 + (s + 1) * 128, hf * 1024:(hf + 1) * 1024])],
                      W=[b_xstage])
                for cb2 in range(2):
                    cb = hf * 2 + cb2
                    ps, bps = PS()
                    for j in range(4):
                        cl = cb2 * 4 + j
                        k.op("pe", lambda e, cl=cl, j=j, ps=ps: e.transpose(ps[:, j * 128:(j + 1) * 128],
                                                                            xstage[:, cl * 128:(cl + 1) * 128], ident[:]),
                             R=[b_xstage, b_setup], W=[bps])
                    if cb % 2 == 0:
                        k.op("act", lambda e, cb=cb, ps=ps, s=s: e.copy(
                            out=xT[:, cb * 4:cb * 4 + 4, s * 128:(s + 1) * 128],
                            in_=ps[:].rearrange("p (a b) -> p a b", a=4)), R=[bps], W=b_xT[cb * 4:cb * 4 + 4])
                    else:
                        k.op("dve", lambda e, cb=cb, ps=ps, s=s: e.tensor_copy(
                            xT[:, cb * 4:cb * 4 + 4, s * 128:(s + 1) * 128],
                            ps[:].rearrange("p (a b) -> p a b", a=4)), R=[bps], W=b_xT[cb * 4:cb * 4 + 4])

    def rms_stats(Tt):
        ps, bps = PS()
        for c in range(DC):
            i = pstate["sq"]
            pstate["sq"] = (i + 1) % 2
            k.op("act", lambda e, c=c, i=i: e.activation(out=sqt[i][:, 0:Tt], in_=xT[:, c, 0:Tt], func=AF.Square),
                 R=[b_xT[c]], W=[b_sqt[i]])
            k.op("pe", lambda e, c=c, i=i, ps=ps: e.matmul(ps[:, 0:Tt], onesb[:], sqt[i][:, 0:Tt], start=(c == 0),
                                                           stop=(c == DC - 1)), R=[b_sqt[i], b_setup], W=[bps])
        k.op("act", lambda e, ps=ps: e.activation(out=rstd[:, 0:Tt], in_=ps[:, 0:Tt], func=AF.Ln, bias=EPS,
                                                  scale=1.0 / D), R=[bps], W=[b_rstd])
        k.op("act", lambda e: e.activation(out=rstd[:, 0:Tt], in_=rstd[:, 0:Tt], func=AF.Exp, scale=-0.5),
             R=[b_rstd], W=[b_rstd])

    def norm_to_hT(gi, Tt):
        rms_stats(Tt)
        for c in range(DC):
            k.op("dve", lambda e, c=c: e.scalar_tensor_tensor(out=hT[:, c, 0:Tt], in0=xT[:, c, 0:Tt],
                                                              scalar=gcols[:, gi, c:c + 1], in1=rstd[:, 0:Tt],
                                                              op0=ALU.mult, op1=ALU.mult),
                 R=[b_xT[c], b_rstd, b_setup], W=[b_hT])

    def proj_fm(wname, col0, nch, rhs_fn, nkc, rbufs, row0=0):
        res = []
        for t0 in range(0, nch, 2):
            n = min(2, nch - t0)
            wt, bw = wtile([(wname, row0, nkc, col0 + t0 * 128, n * 128, 0)])
            for j in range(n):
                ps, bps = PS()
                for kc in range(nkc):
                    k.op("pe", lambda e, kc=kc, j=j, ps=ps, wt=wt: e.matmul(
                        ps[:, 0:rhs_fn(kc).shape[-1]], wt[:, kc, j * 128:(j + 1) * 128], rhs_fn(kc),
                        start=(kc == 0), stop=(kc == nkc - 1)), R=[bw] + rbufs, W=[bps])
                res.append((t0 + j, ps, bps))
            wrel()
        return res

    def ffn(wgu, wdown, gi, Tt):
        norm_to_hT(gi, Tt)
        for jp in range(FC // 2):
            gl = proj_fm(wgu, jp * 256, 2, lambda kc: hT[:, kc, 0:Tt], DC, [b_hT])
            sgs = []
            for (j, ps, bps) in gl:
                ta, bta = TA()
                k.op("act", lambda e, ps=ps, ta=ta: e.activation(out=ta[:, 0:Tt], in_=ps[:, 0:Tt], func=AF.Silu),
                     R=[bps], W=[bta])
                sgs.append((ta, bta))
            ul = proj_fm(wgu, FF + jp * 256, 2, lambda kc: hT[:, kc, 0:Tt], DC, [b_hT])
            for (j, ps, bps), (ta, bta) in zip(ul, sgs):
                hc = jp * 2 + j
                k.op("dve", lambda e, ps=ps, ta=ta, hc=hc: e.tensor_tensor(out=RR[:, hc, 0:Tt], in0=ps[:, 0:Tt],
                                                                          in1=ta[:, 0:Tt], op=ALU.mult),
                     R=[bps, bta], W=[b_RR[hc]])
        kps = [(0, 16), (16, 16), (32, 12)]
        for op_ in range(DC // 2):
            banks = [PS(), PS()]
            for pi, (k0, nk) in enumerate(kps):
                wt, bw = wtile([(wdown, k0 * 128, nk, op_ * 256, 256, 0)])
                for j in range(2):
                    ps, bps = banks[j]
                    for kc in range(nk):
                        k.op("pe", lambda e, kc=kc, j=j, ps=ps, wt=wt, k0=k0, pi=pi, nk=nk: e.matmul(
                            ps[:, 0:Tt], wt[:, kc, j * 128:(j + 1) * 128], RR[:, k0 + kc, 0:Tt],
                            start=(pi == 0 and kc == 0), stop=(pi == 2 and kc == nk - 1)),
                             R=[bw, b_RR[k0 + kc]], W=[bps])
                wrel()
            for j in range(2):
                ps, bps = banks[j]
                c = op_ * 2 + j
                k.op("dve", lambda e, ps=ps, c=c: e.scalar_tensor_tensor(out=xT[:, c, 0:Tt], in0=ps[:, 0:Tt],
                                                                         scalar=0.5, in1=xT[:, c, 0:Tt],
                                                                         op0=ALU.mult, op1=ALU.add),
                     R=[bps, b_xT[c]], W=[b_xT[c]])

    def dt_path(Tt, NSt, off, nq, pre):
        wt, bw = wtile([("w_in", 0, DC, OFF_DT, 64, 0)])
        pss = []
        for s in range(NSt):
            ps, bps = PS()
            for kc in range(DC):
                k.op("pe", lambda e, kc=kc, ps=ps, s=s, wt=wt: e.matmul(ps[:, 0:64], hT[:, kc, s * 128:(s + 1) * 128],
                                                                        wt[:, kc, 0:64], start=(kc == 0),
                                                                        stop=(kc == DC - 1)), R=[bw, b_hT], W=[bps])
            pss.append((ps, bps))
        wrel()
        for s in range(NSt):
            ps, bps = pss[s]
            d_ = dtt[:, s]
            bd = b_dtt[s]
            k.op("dve", lambda e, ps=ps, d_=d_: e.tensor_tensor(out=d_[:, 4, :], in0=ps[:, 0:64], in1=hv[:, 0, :],
                                                                op=ALU.add), R=[bps, b_setup], W=[bd])
            k.op("dve", lambda e, d_=d_: e.scalar_tensor_tensor(out=d_[:, 5, :], in0=d_[:, 4, :], scalar=-1.0,
                                                                in1=d_[:, 4, :], op0=ALU.mult, op1=ALU.max),
                 R=[bd], W=[bd])
            k.op("act", lambda e, d_=d_: e.activation(out=d_[:, 5, :], in_=d_[:, 5, :], func=AF.Exp, scale=-1.0),
                 R=[bd], W=[bd])
            k.op("act", lambda e, d_=d_: e.activation(out=d_[:, 5, :], in_=d_[:, 5, :], func=AF.Ln, bias=1.0,
                                                      scale=1.0), R=[bd], W=[bd])
            k.op("dve", lambda e, d_=d_: e.scalar_tensor_tensor(out=d_[:, 0, :], in0=d_[:, 4, :], scalar=0.0,
                                                                in1=d_[:, 5, :], op0=ALU.max, op1=ALU.add),
                 R=[bd], W=[bd])
            k.op("dve", lambda e, d_=d_: e.tensor_tensor(out=d_[:, 1, :], in0=d_[:, 0, :], in1=hv[:, 1, :],
                                                         op=ALU.mult), R=[bd, b_setup], W=[bd])
            if not pre:
                ps2, bps2 = PS()
                k.op("pe", lambda e, ps2=ps2, d_=d_: e.matmul(ps2[:, 0:64], cv(off, "triT"), d_[:, 1, :], start=True,
                                                              stop=True), R=[bd, b_consts], W=[bps2])
                k.op("act", lambda e, ps2=ps2, d_=d_: e.activation(out=d_[:, 2, :], in_=ps2[:, 0:64], func=AF.Exp),
                     R=[bps2], W=[bd])
            ps3, bps3 = PS()
            k.op("pe", lambda e, ps3=ps3, d_=d_: e.matmul(ps3[:, 0:64], cv(off, "SL"), d_[:, 1, :], start=True,
                                                          stop=True), R=[bd, b_consts], W=[bps3])
            k.op("act", lambda e, ps3=ps3, d_=d_: e.activation(out=d_[:, 3, :], in_=ps3[:, 0:64], func=AF.Exp),
                 R=[bps3], W=[bd])
            ps4, bps4 = PS()
            for q in range(nq):
                so = cv(off, "seqones")[:, q * 128:(q + 1) * 128]
                k.op("pe", lambda e, ps4=ps4, d_=d_, q=q, so=so: e.matmul(ps4[:, q * 64:(q + 1) * 64], so,
                                                                           d_[:, 1, :], start=True, stop=True),
                     R=[bd, b_consts], W=[bps4])
            k.op("act", lambda e, ps4=ps4, s=s: e.activation(
                out=decb[:, s, 0:nq, :], in_=ps4[:, 0:nq * 64].rearrange("p (q h) -> p q h", q=nq), func=AF.Exp),
                 R=[bps4], W=[b_decb[s]])

    def silu_exp(src, bsrc, out_ap, bout, ncol, bias_col=None, nbias_col=None):
        te, bte = TA()
        if nbias_col is None:
            k.op("act", lambda e: e.activation(out=te[:, 0:ncol], in_=src, func=AF.Exp, scale=-1.0), R=[bsrc], W=[bte])
        else:
            k.op("act", lambda e: e.activation(out=te[:, 0:ncol], in_=src, func=AF.Exp, scale=-1.0, bias=nbias_col),
                 R=[bsrc, b_setup], W=[bte])
        k.op("dve", lambda e: e.tensor_scalar(out=te[:, 0:ncol], in0=te[:, 0:ncol], scalar1=1.0, scalar2=None,
                                              op0=ALU.add), R=[bte], W=[bte])
        k.op("dve", lambda e: e.reciprocal(out=te[:, 0:ncol], in_=te[:, 0:ncol]), R=[bte], W=[bte])
        if bias_col is None:
            k.op("dve", lambda e: e.tensor_tensor(out=out_ap, in0=src, in1=te[:, 0:ncol], op=ALU.mult),
                 R=[bsrc, bte], W=[bout])
        else:
            k.op("dve", lambda e: e.scalar_tensor_tensor(out=out_ap, in0=src, scalar=bias_col, in1=te[:, 0:ncol],
                                                         op0=ALU.add, op1=ALU.mult), R=[bsrc, bte, b_setup], W=[bout])

    def rope(pa, ba, pb, bb, Tt, out1, out2, wbufs):
        t1, b1 = TA()
        t2, b2 = TA()
        k.op("dve", lambda e: e.tensor_tensor(out=t1[:, 0:Tt], in0=pa[:, 0:Tt], in1=ropec[:, 0:Tt], op=ALU.mult),
             R=[ba, b_rope], W=[b1])
        k.op("dve", lambda e: e.tensor_tensor(out=t2[:, 0:Tt], in0=pb[:, 0:Tt], in1=ropes[:, 0:Tt], op=ALU.mult),
             R=[bb, b_rope], W=[b2])
        k.op("dve", lambda e: e.tensor_tensor(out=out1, in0=t1[:, 0:Tt], in1=t2[:, 0:Tt], op=ALU.subtract),
             R=[b1, b2], W=wbufs)
        t3, b3 = TA()
        k.op("dve", lambda e: e.tensor_tensor(out=t3[:, 0:Tt], in0=pb[:, 0:Tt], in1=ropec[:, 0:Tt], op=ALU.mult),
             R=[bb, b_rope], W=[b3])
        k.op("dve", lambda e: e.tensor_tensor(out=t1[:, 0:Tt], in0=pa[:, 0:Tt], in1=ropes[:, 0:Tt], op=ALU.mult),
             R=[ba, b_rope], W=[b1])
        k.op("dve", lambda e: e.tensor_tensor(out=out2, in0=t3[:, 0:Tt], in1=t1[:, 0:Tt], op=ALU.add),
             R=[b3, b1], W=wbufs)

    def ret_w(h, u, Tt, NSt, off, nq, pre):
        V = RV[u]
        qT, bq = V["qT"]
        kT, bk = V["kT"]
        qdT, bqd = V["qdT"]
        kdtm, bkd = V["kdtm"]
        vtm, bv = V["vtm"]
        grgs, bgr = V["grgs"]
        brgs, bbr = V["brgs"]
        rh = lambda kc: hT[:, kc, 0:Tt]
        names = ["k"] if pre else ["q", "k"]
        for nm in names:
            wt, bw = wtile([("w_in", 0, DC, (OFF_Q if nm == "q" else OFF_K) + h * 256, 256, 0)])
            pp = []
            for dh in range(2):
                ps, bps = PS()
                for kc in range(DC):
                    k.op("pe", lambda e, kc=kc, dh=dh, ps=ps, wt=wt: e.matmul(ps[:, 0:Tt], wt[:, kc, dh * 128:(dh + 1) * 128],
                                                                              rh(kc), start=(kc == 0),
                                                                              stop=(kc == DC - 1)),
                         R=[bw, b_hT], W=[bps])
                pp.append((ps, bps))
            if nm == "q":
                rope(pp[0][0], pp[0][1], pp[1][0], pp[1][1], Tt, qdT[:, 0, 0:Tt], qdT[:, 1, 0:Tt], [bqd])
                k.op("act", lambda e: e.copy(out=qT[:, :, 0:Tt], in_=qdT[:, :, 0:Tt]), R=[bqd], W=[bq])
                if nq == 1:
                    for dh in range(2):
                        k.op("dve", lambda e, dh=dh: e.tensor_tensor(
                            out=qdT[:, dh, 0:Tt].rearrange("p (s i) -> p s i", s=NSt),
                            in0=qdT[:, dh, 0:Tt].rearrange("p (s i) -> p s i", s=NSt),
                            in1=cv(off, "qdec")[:, h * 128:(h + 1) * 128].unsqueeze(1).to_broadcast([128, NSt, 128]),
                            op=ALU.mult), R=[bqd, b_consts, bq], W=[bqd])
                else:
                    k.op("dve", lambda e: e.tensor_tensor(
                        out=qdT[:, :, 0:128], in0=qdT[:, :, 0:128],
                        in1=cv(off, "qdec")[:, h * 128:(h + 1) * 128].unsqueeze(1).to_broadcast([128, 2, 128]),
                        op=ALU.mult), R=[bqd, b_consts, bq], W=[bqd])
            else:
                rope(pp[0][0], pp[0][1], pp[1][0], pp[1][1], Tt, kT[:, 0, 0:Tt], kT[:, 1, 0:Tt], [bk])
            wrel()
            yield
        vps = [PS() for s in range(NSt)]
        for half in range(2):
            wt, bw = wtile([("w_in", 0, DC, OFF_V + h * 512 + half * 256, 256, 0)])
            for s in range(NSt):
                ps, bps = vps[s]
                for kc in range(DC):
                    k.op("pe", lambda e, kc=kc, s=s, ps=ps, wt=wt, half=half: e.matmul(
                        ps[:, half * 256:(half + 1) * 256], hT[:, kc, s * 128:(s + 1) * 128], wt[:, kc, 0:256],
                        start=(kc == 0), stop=(kc == DC - 1)), R=[bw, b_hT], W=[bps])
            wrel()
        for s in range(NSt):
            ps, bps = vps[s]
            k.op("act", lambda e, s=s, ps=ps: e.copy(out=vtm[:, s, :], in_=ps[:]), R=[bps], W=[bv])
        yield
        for t0 in ((0, 2) if not pre else ()):
            for (j, ps, bps) in proj_fm("w_in", OFF_RG + h * 512 + t0 * 128, 2, rh, DC, [b_hT]):
                ec = t0 + j
                tb, btb = TB()
                silu_exp(ps[:, 0:Tt], bps, tb[:, 0:Tt], btb, Tt)
                k.op("dve", lambda e, ec=ec, tb=tb: e.tensor_scalar(out=grgs[:, ec, 0:Tt], in0=tb[:, 0:Tt],
                                                                     scalar1=gncols[:, 0, h * 4 + ec:h * 4 + ec + 1],
                                                                     scalar2=None, op0=ALU.mult),
                     R=[btb, b_setup], W=[bgr])
                k.op("dve", lambda e, ec=ec, tb=tb: e.tensor_scalar(out=brgs[:, ec, 0:Tt], in0=tb[:, 0:Tt],
                                                                     scalar1=gncols[:, 1, h * 4 + ec:h * 4 + ec + 1],
                                                                     scalar2=None, op0=ALU.mult),
                     R=[btb, b_setup], W=[bbr])
            yield
        for s in range(NSt):
            ps, bps = PS()
            psb = ps[:].bitcast(BF16)
            for dh in range(2):
                k.op("pe", lambda e, dh=dh, s=s, psb=psb: e.transpose(psb[:, dh * 128:(dh + 1) * 128],
                                                                      kT[:, dh, s * 128:(s + 1) * 128], identb[:]),
                     R=[bk, b_setup], W=[bps])
            for q in range(nq):
                k.op("dve", lambda e, s=s, q=q, psb=psb: e.tensor_scalar(
                    out=kdtm[:, s + q, :], in0=psb[:, 0:256], scalar1=cv(off, "kdecq")[:, q * 8 + h:q * 8 + h + 1],
                    scalar2=None, op0=ALU.mult), R=[bps, b_consts], W=[bkd])
        yield

    def ret_chunk(h, u, s, Tt, off, cdec, nq, pre, smp):
        V = RV[u]
        qT, bq = V["qT"]
        kT, bk = V["kT"]
        qdT, bqd = V["qdT"]
        kdtm, bkd = V["kdtm"]
        vtm, bv = V["vtm"]
        grgs, bgr = V["grgs"]
        brgs, bbr = V["brgs"]
        sl = slice(s * 128, (s + 1) * 128)
        i = (h * 2 + s) % 2
        if not pre:
            ps, bps = PS()
            for dh in range(2):
                k.op("pe", lambda e, dh=dh, ps=ps: e.matmul(ps[:, 0:128], kT[:, dh, sl], qT[:, dh, sl],
                                                            start=(dh == 0), stop=(dh == 1)), R=[bk, bq], W=[bps])
            k.op("dve", lambda e, ps=ps: e.tensor_tensor(out=attm[i][:], in0=ps[:, 0:128],
                                                         in1=cv(off, "dmatT")[:, h * 128:(h + 1) * 128], op=ALU.mult),
                 R=[bps, b_consts], W=[b_attm[i]])
            yield
        if smp:
            _ret_chunk_smp(h, u, s, off, cdec, nq, V, i)
            yield
            return
        if not pre:
            po, bpo = PS()
            k.op("pe", lambda e: e.matmul(po[:], attm[i][:], vtm[:, s, :], start=True, stop=False),
                 R=[b_attm[i], bv], W=[bpo])
            for dh in range(2):
                k.op("pe", lambda e, dh=dh: e.matmul(po[:], qdT[:, dh, sl], S_ret[:, h, dh, :], start=False,
                                                     stop=(dh == 1)), R=[bqd, b_Sret[h][dh]], W=[bpo])
        for dh in range(2):
            ps, bps = PS()
            k.op("pe", lambda e, dh=dh, ps=ps: e.matmul(ps[:], kdtm[:, s, dh * 128:(dh + 1) * 128], vtm[:, s, :],
                                                        start=True, stop=True), R=[bkd, bv], W=[bps])
            k.op("dve", lambda e, dh=dh, ps=ps: e.scalar_tensor_tensor(out=S_ret[:, h, dh, :], in0=S_ret[:, h, dh, :],
                                                                       scalar=cdec[h], in1=ps[:], op0=ALU.mult,
                                                                       op1=ALU.add),
                 R=[bps, b_Sret[h][dh]], W=[b_Sret[h][dh]])
        if not pre:
            ret_post1(h, s, po, bpo)
            yield
            ret_post2(h, s, grgs, bgr, brgs, bbr)
        yield

    def ret_post(h, s, po, bpo, grgs, bgr, brgs, bbr):
        ret_post1(h, s, po, bpo)
        ret_post2(h, s, grgs, bgr, brgs, bbr)

    def ret_post1(h, s, po, bpo):
        i = (h * 2 + s) % 2
        st, bst = stat[i], b_stat[i]
        k.op("dve", lambda e: e.bn_stats(out=st[:, 0:6], in_=po[:]), R=[bpo], W=[bst])
        k.op("dve", lambda e: e.bn_aggr(out=st[:, 6:8], in_=st[:, 0:6]), R=[bst], W=[bst])
        k.op("act", lambda e: e.activation(out=st[:, 8:9], in_=st[:, 7:8], func=AF.Ln, bias=EPS, scale=1.0),
             R=[bst], W=[bst])
        k.op("act", lambda e: e.activation(out=st[:, 8:9], in_=st[:, 8:9], func=AF.Exp, scale=-0.5),
             R=[bst], W=[bst])
        tb, btb = onh[i], b_onh[i]
        k.op("dve", lambda e: e.tensor_scalar(out=tb[:], in0=po[:], scalar1=st[:, 6:7], scalar2=st[:, 8:9],
                                              op0=ALU.subtract, op1=ALU.mult), R=[bpo, bst], W=[btb])

    def ret_post2(h, s, grgs, bgr, brgs, bbr):
        sl = slice(s * 128, (s + 1) * 128)
        i = (h * 2 + s) % 2
        tb, btb = onh[i], b_onh[i]
        pt, bpt = PS()
        ptb = pt[:].bitcast(BF16)
        for ec in range(4):
            k.op("pe", lambda e, ec=ec: e.transpose(ptb[:, ec * 128:(ec + 1) * 128], tb[:, ec * 128:(ec + 1) * 128],
                                                    identb[:]), R=[btb, b_setup], W=[bpt])
        tb2, btb2 = TB()
        k.op("dve", lambda e: e.tensor_tensor(out=tb2[:].rearrange("p (a b) -> p a b", a=4),
                                              in0=ptb[:, 0:512].rearrange("p (a b) -> p a b", a=4),
                                              in1=grgs[:, :, sl], op=ALU.mult), R=[bpt, bgr], W=[btb2])
        k.op("dve", lambda e: e.tensor_tensor(out=RR[:, h * 4:h * 4 + 4, sl],
                                               in0=tb2[:].rearrange("p (a b) -> p a b", a=4), in1=brgs[:, :, sl],
                                               op=ALU.add), R=[btb2, bbr], W=b_RR[h * 4:h * 4 + 4])

    def _ret_chunk_smp(h, u, s, off, cdec, nq, V, i):
        qT, bq = V["qT"]
        qdT, bqd = V["qdT"]
        kdtm, bkd = V["kdtm"]
        vtm, bv = V["vtm"]
        grgs, bgr = V["grgs"]
        brgs, bbr = V["brgs"]
        po, bpo = PS(pin=True)
        k.op("pe", lambda e: e.matmul(po[:], attm[i][:], vtm[:, 0, :], start=True, stop=False),
             R=[b_attm[i], bv], W=[bpo])
        for q in range(nq):
            j = q % 2
            k.dma([(s0buf[j][:], I["sret_in"][q, h].rearrange("(dh p) e -> p dh e", p=128))], W=[b_s0buf[j]])
            k.op("dve", lambda e, q=q, j=j: e.tensor_tensor(
                out=qmk[j][:], in0=qdT[:, :, 0:128],
                in1=cv(off, "colmask")[:, q * 128:(q + 1) * 128].unsqueeze(1).to_broadcast([128, 2, 128]),
                op=ALU.mult), R=[bqd, b_consts], W=[b_qmk[j]])
            for dh in range(2):
                k.op("pe", lambda e, dh=dh, j=j, q=q: e.matmul(po[:], qmk[j][:, dh, :], s0buf[j][:, dh, :], start=False,
                                                               stop=(q == nq - 1 and dh == 1)),
                     R=[b_qmk[j], b_s0buf[j]], W=[bpo])
            for dh in range(2):
                ps, bps = PS()
                k.op("pe", lambda e, dh=dh, ps=ps, q=q: e.matmul(ps[:], kdtm[:, q, dh * 128:(dh + 1) * 128],
                                                                 vtm[:, 0, :], start=True, stop=True),
                     R=[bkd, bv], W=[bps])
                k.op("dve", lambda e, dh=dh, ps=ps, j=j: e.scalar_tensor_tensor(
                    out=s1buf[j][:, dh, :], in0=s0buf[j][:, dh, :], scalar=cdec[h], in1=ps[:], op0=ALU.mult,
                    op1=ALU.add), R=[bps, b_s0buf[j]], W=[b_s1buf[j]])
            k.dma([(O["ret_smp"][q, h].rearrange("(dh p) e -> p dh e", p=128), s1buf[j][:])], R=[b_s1buf[j]],
                  is_out=True)
        ret_post(h, s, po, bpo, grgs, bgr, brgs, bbr)
        unpin(bpo)

    def ssm_w(g, u, Tt, NSt, off, nq, pre, L, nseg):
        V = SV[u]
        zs, bzs = V["zs"]
        xpad, bxp = V["xpad"]
        bcpad, bbc = V["bcpad"]
        xcT, bxc = V["xcT"]
        BT, bBT = V["BT"]
        CT, bCT = V["CT"]
        xtm, bxt = V["xtm"]
        Btm, bBt = V["Btm"]
        xdt, bxd = V["xdt"]
        rh = lambda kc: hT[:, kc, 0:Tt]
        LP = L + 3
        if not pre:
            zps = [PS() for s in range(NSt)]
            for half in range(2):
                wt, bw = wtile([("w_in", 0, DC, OFF_Z + g * 512 + half * 256, 256, 0)])
                for s in range(NSt):
                    ps, bps = zps[s]
                    for kc in range(DC):
                        k.op("pe", lambda e, kc=kc, s=s, ps=ps, wt=wt, half=half: e.matmul(
                            ps[:, half * 256:(half + 1) * 256], hT[:, kc, s * 128:(s + 1) * 128], wt[:, kc, 0:256],
                            start=(kc == 0), stop=(kc == DC - 1)), R=[bw, b_hT], W=[bps])
                wrel()
            for s in range(NSt):
                ps, bps = zps[s]
                silu_exp(ps[:], bps, zs[:, s, :], bzs, 512)
            yield

        def preconv(ps, bps, cch, padv, bpad):
            pv = padv.rearrange("p (q l) -> p q l", q=nseg)
            k.op("act", lambda e: e.copy(out=pv[:, :, 0:3], in_=conv_hist[:, cch, 0:nseg, :]),
                 R=[b_chist[cch]], W=[bpad])
            k.op("act", lambda e: e.copy(out=pv[:, :, 3:LP], in_=ps[:, 0:Tt].rearrange("p (q l) -> p q l", q=nseg)),
                 R=[bps], W=[bpad])
            dst = conv_new if nseg > 1 else conv_hist
            bdst = b_cnew if nseg > 1 else b_chist
            k.op("dve", lambda e: e.tensor_copy(dst[:, cch, 0:nseg, :],
                                                ps[:, 0:Tt].rearrange("p (q l) -> p q l", q=nseg)[:, :, L - 3:L]),
                 R=[bps, bpad], W=[bdst[cch]])

        def conv(cch, padv, bpad, out_ap, bout):
            pv = padv.rearrange("p (q l) -> p q l", q=nseg)
            ta, bta = TA()
            tv = ta[:, 0:Tt].rearrange("p (q l) -> p q l", q=nseg)
            k.op("dve", lambda e: e.tensor_scalar(out=tv, in0=pv[:, :, 0:L], scalar1=cwcols[:, 0, cch:cch + 1],
                                                   scalar2=None, op0=ALU.mult), R=[bpad, b_setup], W=[bta])
            for w in range(1, 4):
                k.op("dve", lambda e, w=w: e.scalar_tensor_tensor(out=tv, in0=pv[:, :, w:w + L],
                                                                  scalar=cwcols[:, w, cch:cch + 1], in1=tv,
                                                                  op0=ALU.mult, op1=ALU.add),
                     R=[bpad, b_setup, bta], W=[bta])
            silu_exp(ta[:, 0:Tt], bta, out_ap, bout, Tt, bias_col=cwcols[:, 4, cch:cch + 1],
                     nbias_col=cwcols[:, 5, cch:cch + 1])

        for t0 in (0, 2):
            for (j, ps, bps) in proj_fm("w_in", OFF_X + g * 512 + t0 * 128, 2, rh, DC, [b_hT]):
                ci = t0 + j
                cch = g * 4 + ci
                preconv(ps, bps, cch, xpad[:, ci, 0:nseg * LP], bxp)
                conv(cch, xpad[:, ci, 0:nseg * LP], bxp, xcT[:, ci, 0:Tt], bxc)
            yield
        for (bi, offc, cch) in [(0, OFF_B, 32 + g), (1, OFF_C, 40 + g)]:
            (_, ps, bps), = proj_fm("w_in", offc + g * 128, 1, rh, DC, [b_hT])
            preconv(ps, bps, cch, bcpad[:, bi, 0:nseg * LP], bbc)
            if bi == 0:
                conv(cch, bcpad[:, bi, 0:nseg * LP], bbc, BT[:, 0, 0:Tt], bBT)
            elif not pre:
                conv(cch, bcpad[:, bi, 0:nseg * LP], bbc, CT[:, 0, 0:Tt], bCT)
            yield
        for s in range(NSt):
            ps, bps = PS()
            psb = ps[:].bitcast(BF16)
            for ci in range(4):
                k.op("pe", lambda e, ci=ci, s=s, psb=psb: e.transpose(psb[:, ci * 128:(ci + 1) * 128],
                                                                      xcT[:, ci, s * 128:(s + 1) * 128], identb[:]),
                     R=[bxc, b_setup], W=[bps])
            k.op("act", lambda e, s=s, psb=psb: e.copy(out=xtm[:, s, :], in_=psb[:, 0:512]), R=[bps], W=[bxt])
            ps2, bps2 = PS()
            k.op("pe", lambda e, s=s, ps2=ps2: e.transpose(ps2[:, 0:128], BT[:, 0, s * 128:(s + 1) * 128], ident[:]),
                 R=[bBT, b_setup], W=[bps2])
            k.op("act", lambda e, s=s, ps2=ps2: e.copy(out=Btm[:, s, :], in_=ps2[:, 0:128]), R=[bps2], W=[bBt])
            yield
        for s in range(NSt):
            k.op("dve", lambda e, s=s: e.tensor_tensor(
                out=xdt[:, s, :].rearrange("p (h q) -> p h q", h=8), in0=xtm[:, s, :].rearrange("p (h q) -> p h q", h=8),
                in1=dtt[:, s, 0, g * 8:(g + 1) * 8].unsqueeze(2).to_broadcast([128, 8, 64]), op=ALU.mult),
                 R=[bxt, b_dtt[s]], W=[bxd])
        yield

    def ssm_chunk(g, u, s, Tt, off, nq, pre, smp):
        V = SV[u]
        zs, bzs = V["zs"]
        BT, bBT = V["BT"]
        CT, bCT = V["CT"]
        xtm, bxt = V["xtm"]
        Btm, bBt = V["Btm"]
        xdt, bxd = V["xdt"]
        sl = slice(s * 128, (s + 1) * 128)
        i = (g * 2 + s) % 2
        gs = slice(g * 8, (g + 1) * 8)
        if not pre:
            ps, bps = PS()
            k.op("pe", lambda e: e.matmul(ps[:, 0:128], BT[:, 0, sl], CT[:, 0, sl], start=True, stop=True),
                 R=[bBT, bCT], W=[bps])
            k.op("dve", lambda e: e.tensor_tensor(out=cbm[i][:], in0=ps[:, 0:128], in1=cv(off, "maskT"), op=ALU.mult),
                 R=[bps, b_consts], W=[b_cbm[i]])
            k.op("dve", lambda e: e.tensor_tensor(
                out=rhsR[i][:], in0=dtt[:, s, 1, gs].unsqueeze(2).to_broadcast([128, 8, 128]),
                in1=cv(off, "triT").unsqueeze(1).to_broadcast([128, 8, 128]), op=ALU.mult),
                 R=[b_dtt[s], b_consts], W=[b_rhsR[i]])
            for hh in range(2):
                pg, bpg = PS()
                k.op("pe", lambda e, hh=hh, pg=pg: e.matmul(
                    pg[:], cv(off, "SL"), rhsR[i][:, hh * 4:(hh + 1) * 4, :].rearrange("p a b -> p (a b)"),
                    start=True, stop=True), R=[b_rhsR[i], b_consts], W=[bpg])
                k.op("act", lambda e, hh=hh, pg=pg: e.activation(
                    out=wTt[i][:, hh * 4:(hh + 1) * 4, :].rearrange("p a b -> p (a b)"), in_=pg[:], func=AF.Exp),
                     R=[bpg], W=[b_wTt[i]])
            k.op("dve", lambda e: e.tensor_tensor(out=wTt[i][:], in0=wTt[i][:],
                                                  in1=cbm[i][:].unsqueeze(1).to_broadcast([128, 8, 128]), op=ALU.mult),
                 R=[b_wTt[i], b_cbm[i]], W=[b_wTt[i]])
            yield
            py, bpy = PS(pin=True)
            for hd in range(8):
                k.op("pe", lambda e, hd=hd: e.matmul(py[:, hd * 64:(hd + 1) * 64], wTt[i][:, hd, :],
                                                     xdt[:, s, hd * 64:(hd + 1) * 64], start=True, stop=True),
                     R=[b_wTt[i], bxd], W=[bpy])
        if smp:
            pin, bpin = PS(pin=True)
            for q in range(nq):
                j = q % 2
                k.dma([(s1buf[j][:].rearrange("p a b -> p (a b)")[:, 0:512].rearrange("p (a b) -> p a b", a=4),
                        I["sssm_in"][q, g * 8:(g + 1) * 8].rearrange("h p n -> (h p) n").rearrange(
                            "(a r) n -> r a n", r=128))], W=[b_s1buf[j]])
                pt, bpt = PS()
                for a in range(4):
                    k.op("pe", lambda e, a=a, j=j, pt=pt: e.transpose(
                        pt[:, a * 128:(a + 1) * 128],
                        s1buf[j][:].rearrange("p a b -> p (a b)")[:, a * 128:(a + 1) * 128], ident[:]),
                         R=[b_s1buf[j], b_setup], W=[bpt])
                k.op("act", lambda e, j=j, pt=pt: e.copy(out=s0buf[j][:, 0, :], in_=pt[:]), R=[bpt], W=[b_s0buf[j]])
                k.op("dve", lambda e, j=j, q=q: e.tensor_tensor(out=qmk[j][:, 0, :], in0=CT[:, 0, 0:128],
                                                                in1=cv(off, "colmask")[:, q * 128:(q + 1) * 128],
                                                                op=ALU.mult), R=[bCT, b_consts], W=[b_qmk[j]])
                k.op("pe", lambda e, j=j, q=q: e.matmul(pin[:], qmk[j][:, 0, :], s0buf[j][:, 0, :], start=(q == 0),
                                                        stop=(q == nq - 1)), R=[b_qmk[j], b_s0buf[j]], W=[bpin])
                k.op("dve", lambda e, q=q: e.tensor_scalar(out=wendq[:, q, :], in0=dtt[:, 0, 3, :],
                                                            scalar1=cv(off, "rowmask")[:, q:q + 1], scalar2=None,
                                                            op0=ALU.mult), R=[b_dtt[0], b_consts], W=[b_wendq])
                k.op("dve", lambda e, j=j, q=q: e.tensor_tensor(
                    out=xwt[j][:].rearrange("p (h q) -> p h q", h=8), in0=xdt[:, 0, :].rearrange("p (h q) -> p h q", h=8),
                    in1=wendq[:, q, gs].unsqueeze(2).to_broadcast([128, 8, 64]), op=ALU.mult),
                     R=[bxd, b_wendq], W=[b_xwt[j]])
                pst, bpst = PS()
                k.op("pe", lambda e, j=j, pst=pst: e.matmul(pst[:], Btm[:, 0, :], xwt[j][:], start=True, stop=True),
                     R=[bBt, b_xwt[j]], W=[bpst])
                k.op("dve", lambda e, j=j, q=q: e.tensor_tensor(
                    out=s0buf[j][:, 1, :].rearrange("p (h q) -> p h q", h=8),
                    in0=s0buf[j][:, 0, :].rearrange("p (h q) -> p h q", h=8),
                    in1=decb[:, 0, q, gs].unsqueeze(2).to_broadcast([128, 8, 64]), op=ALU.mult),
                     R=[b_s0buf[j], b_decb[0]], W=[b_s0buf[j]])
                k.op("dve", lambda e, j=j, pst=pst: e.tensor_tensor(out=s0buf[j][:, 1, :], in0=s0buf[j][:, 1, :],
                                                                    in1=pst[:], op=ALU.add),
                     R=[bpst, b_s0buf[j]], W=[b_s0buf[j]])
                pt2, bpt2 = PS()
                for a in range(4):
                    k.op("pe", lambda e, a=a, j=j, pt2=pt2: e.transpose(pt2[:, a * 128:(a + 1) * 128],
                                                                        s0buf[j][:, 1, a * 128:(a + 1) * 128],
                                                                        ident[:]),
                         R=[b_s0buf[j], b_setup], W=[bpt2])
                k.op("act", lambda e, j=j, pt2=pt2: e.copy(
                    out=s1buf[j][:].rearrange("p a b -> p (a b)")[:, 512:1024], in_=pt2[:]), R=[bpt2], W=[b_s1buf[j]])
                k.dma([(O["ssm_smp"][q, g * 512:(g + 1) * 512, :].rearrange("(a r) n -> r a n", r=128),
                        s1buf[j][:].rearrange("p a b -> p (a b)")[:, 512:1024].rearrange("p (a b) -> p a b", a=4))],
                      R=[b_s1buf[j]], is_out=True)
        else:
            if not pre:
                pin, bpin = PS(pin=True)
                k.op("pe", lambda e: e.matmul(pin[:], CT[:, 0, sl], S_ssm[:, g, :], start=True, stop=True),
                     R=[bCT, b_Sssm[g]], W=[bpin])
            k.op("dve", lambda e: e.tensor_tensor(
                out=xwt[i][:].rearrange("p (h q) -> p h q", h=8), in0=xdt[:, s, :].rearrange("p (h q) -> p h q", h=8),
                in1=dtt[:, s, 3, gs].unsqueeze(2).to_broadcast([128, 8, 64]), op=ALU.mult),
                 R=[bxd, b_dtt[s]], W=[b_xwt[i]])
            pst, bpst = PS()
            k.op("pe", lambda e: e.matmul(pst[:], Btm[:, s, :], xwt[i][:], start=True, stop=True),
                 R=[bBt, b_xwt[i]], W=[bpst])
            k.op("dve", lambda e: e.tensor_tensor(
                out=S_ssm[:, g, :].rearrange("p (h q) -> p h q", h=8), in0=S_ssm[:, g, :].rearrange("p (h q) -> p h q", h=8),
                in1=decb[:, s, 0, gs].unsqueeze(2).to_broadcast([128, 8, 64]), op=ALU.mult),
                 R=[b_Sssm[g], b_decb[s]], W=[b_Sssm[g]])
            k.op("dve", lambda e: e.tensor_tensor(out=S_ssm[:, g, :], in0=S_ssm[:, g, :], in1=pst[:], op=ALU.add),
                 R=[bpst, b_Sssm[g]], W=[b_Sssm[g]])
        if pre:
            yield
            return
        ta, bta = TA()
        k.op("dve", lambda e: e.tensor_tensor(
            out=ta[:].rearrange("p (h q) -> p h q", h=8), in0=pin[:].rearrange("p (h q) -> p h q", h=8),
            in1=dtt[:, s, 2, gs].unsqueeze(2).to_broadcast([128, 8, 64]), op=ALU.mult),
             R=[bpin, b_dtt[s]], W=[bta])
        k.op("dve", lambda e: e.tensor_tensor(out=ta[:], in0=ta[:], in1=py[:], op=ALU.add), R=[bta, bpy], W=[bta])
        unpin(bpy)
        unpin(bpin)
        ta2, bta2 = TA()
        k.op("dve", lambda e: e.tensor_tensor(
            out=ta2[:].rearrange("p (h q) -> p h q", h=8), in0=xtm[:, s, :].rearrange("p (h q) -> p h q", h=8),
            in1=hv[:, 2, gs].unsqueeze(2).to_broadcast([128, 8, 64]), op=ALU.mult), R=[bxt, b_setup], W=[bta2])
        k.op("dve", lambda e: e.tensor_tensor(out=ta[:], in0=ta[:], in1=ta2[:], op=ALU.add),
             R=[bta, bta2], W=[bta])
        k.op("dve", lambda e: e.tensor_tensor(out=ta[:], in0=ta[:], in1=zs[:, s, :], op=ALU.mult),
             R=[bta, bzs], W=[bta])
        st, bst = stat[i], b_stat[i]
        k.op("dve", lambda e: e.memset(st[:, 10:11], 0.0), W=[bst])
        k.op("act", lambda e: e.activation(out=ta2[:], in_=ta[:], func=AF.Square, accum_out=st[:, 10:11]),
             R=[bta], W=[bta2, bst])
        k.op("act", lambda e: e.activation(out=st[:, 11:12], in_=st[:, 10:11], func=AF.Ln, bias=EPS,
                                           scale=1.0 / 512.0), R=[bst], W=[bst])
        k.op("act", lambda e: e.activation(out=st[:, 11:12], in_=st[:, 11:12], func=AF.Exp, scale=-0.5),
             R=[bst], W=[bst])
        tb, btb = onh[i], b_onh[i]
        k.op("dve", lambda e: e.tensor_scalar(out=tb[:], in0=ta[:], scalar1=st[:, 11:12], scalar2=None,
                                              op0=ALU.mult), R=[bta, bst], W=[btb])
        yield
        pt, bpt = PS()
        ptb = pt[:].bitcast(BF16)
        for ci in range(4):
            k.op("pe", lambda e, ci=ci: e.transpose(ptb[:, ci * 128:(ci + 1) * 128], tb[:, ci * 128:(ci + 1) * 128],
                                                    identb[:]), R=[btb, b_setup], W=[bpt])
        for ci in range(4):
            c = g * 4 + ci
            k.op("dve", lambda e, ci=ci, c=c: e.tensor_scalar(out=RR[:, c, sl], in0=ptb[:, ci * 128:(ci + 1) * 128],
                                                              scalar1=gncols[:, 2, c:c + 1], scalar2=None,
                                                              op0=ALU.mult), R=[bpt, b_setup], W=[b_RR[c]])
        yield

    def merge(which, Tt):
        wbr, offg = ("w_br_ret", OFF_GR) if which == 0 else ("w_br_ssm", OFF_GS)
        for cp in range(DC // 2):
            banks = [PS(), PS()]
            for kp in range(2):
                wt, bw = wtile([(wbr, kp * 2048, 16, cp * 256, 256, 0)])
                for j in range(2):
                    ps, bps = banks[j]
                    for kc in range(16):
                        k.op("pe", lambda e, kc=kc, j=j, ps=ps, wt=wt, kp=kp: e.matmul(
                            ps[:, 0:Tt], wt[:, kc, j * 128:(j + 1) * 128], RR[:, kp * 16 + kc, 0:Tt],
                            start=(kp == 0 and kc == 0), stop=(kp == 1 and kc == 15)),
                             R=[bw, b_RR[kp * 16 + kc]], W=[bps])
                wrel()
            gl = proj_fm("w_in", offg + cp * 256, 2, lambda kc: hT[:, kc, 0:Tt], DC, [b_hT])
            for j in range(2):
                ps, bps = banks[j]
                (_, pg, bpg) = gl[j]
                c = cp * 2 + j
                ta, bta = TA()
                k.op("act", lambda e, pg=pg, ta=ta: e.activation(out=ta[:, 0:Tt], in_=pg[:, 0:Tt], func=AF.Sigmoid),
                     R=[bpg], W=[bta])
                if which == 0:
                    k.op("dve", lambda e, ps=ps, ta=ta, c=c: e.tensor_tensor(out=RR[:, 32 + c, 0:Tt], in0=ps[:, 0:Tt],
                                                                            in1=ta[:, 0:Tt], op=ALU.mult),
                         R=[bps, bta], W=[b_RR[32 + c]])
                else:
                    k.op("dve", lambda e, ps=ps, ta=ta: e.tensor_tensor(out=ta[:, 0:Tt], in0=ps[:, 0:Tt],
                                                                       in1=ta[:, 0:Tt], op=ALU.mult),
                         R=[bps, bta], W=[bta])
                    k.op("dve", lambda e, ta=ta, c=c: e.tensor_tensor(out=RR[:, 32 + c, 0:Tt], in0=ta[:, 0:Tt],
                                                                      in1=RR[:, 32 + c, 0:Tt], op=ALU.add),
                         R=[bta, b_RR[32 + c]], W=[b_RR[32 + c]])

    def out_proj(Tt):
        for cp in range(DC // 2):
            for (j, ps, bps) in proj_fm("w_out", cp * 256, 2, lambda kc: RR[:, 32 + kc, 0:Tt], DC, b_RR[32:48]):
                c = cp * 2 + j
                k.op("dve", lambda e, ps=ps, c=c: e.tensor_tensor(out=xT[:, c, 0:Tt], in0=ps[:, 0:Tt],
                                                                  in1=xT[:, c, 0:Tt], op=ALU.add),
                     R=[bps, b_xT[c]], W=[b_xT[c]])

    def ple(p_ap, tok0, Tt, NSt):
        norm_to_hT(3, Tt)
        for s in range(NSt):
            k.dma([(pstage[:], p_ap[tok0 + s * 128: tok0 + (s + 1) * 128, :])], W=[b_pstage])
            ps, bps = PS()
            for j in range(2):
                k.op("pe", lambda e, j=j, ps=ps: e.transpose(ps[:, j * 128:(j + 1) * 128],
                                                             pstage[:, j * 128:(j + 1) * 128], ident[:]),
                     R=[b_pstage, b_setup], W=[bps])
            k.op("act", lambda e, s=s, ps=ps: e.copy(out=pT[:, :, s * 128:(s + 1) * 128],
                                                     in_=ps[:, 0:256].rearrange("p (a b) -> p a b", a=2)),
                 R=[bps], W=[b_pT])
        for cp in range(DC // 2):
            gl = proj_fm("w_ple_gate", cp * 256, 2, lambda kc: hT[:, kc, 0:Tt], DC, [b_hT])
            pl = proj_fm("w_ple", cp * 256, 2, lambda kc: pT[:, kc, 0:Tt], 2, [b_pT])
            for j in range(2):
                (_, pg, bpg) = gl[j]
                (_, pp, bpp) = pl[j]
                c = cp * 2 + j
                ta, bta = TA()
                k.op("act", lambda e, pg=pg, ta=ta: e.activation(out=ta[:, 0:Tt], in_=pg[:, 0:Tt], func=AF.Sigmoid),
                     R=[bpg], W=[bta])
                k.op("dve", lambda e, pp=pp, ta=ta: e.tensor_tensor(out=ta[:, 0:Tt], in0=pp[:, 0:Tt], in1=ta[:, 0:Tt],
                                                                   op=ALU.mult), R=[bpp, bta], W=[bta])
                k.op("dve", lambda e, ta=ta, c=c: e.tensor_tensor(out=xT[:, c, 0:Tt], in0=ta[:, 0:Tt],
                                                                  in1=xT[:, c, 0:Tt], op=ALU.add),
                     R=[bta, b_xT[c]], W=[b_xT[c]])

    def final_store(y_ap, tok0, Tt, NSt):
        rms_stats(Tt)
        for s in range(NSt):
            for hf in range(2):
                for cb2 in range(2):
                    cb = hf * 2 + cb2
                    ps, bps = PS()
                    for j in range(4):
                        c = cb * 4 + j
                        ta, bta = TA()
                        k.op("dve", lambda e, c=c, ta=ta, s=s: e.scalar_tensor_tensor(
                            out=ta[:, 0:128], in0=xT[:, c, s * 128:(s + 1) * 128], scalar=gcols[:, 4, c:c + 1],
                            in1=rstd[:, s * 128:(s + 1) * 128], op0=ALU.mult, op1=ALU.mult),
                             R=[b_xT[c], b_rstd, b_setup], W=[bta])
                        k.op("pe", lambda e, j=j, ta=ta, ps=ps: e.transpose(ps[:, j * 128:(j + 1) * 128], ta[:, 0:128],
                                                                            ident[:]), R=[bta, b_setup], W=[bps])
                    k.op("act", lambda e, cb2=cb2, ps=ps: e.copy(out=ystage[:, cb2 * 512:(cb2 + 1) * 512], in_=ps[:]),
                         R=[bps], W=[b_ystage])
                k.dma([(y_ap[tok0 + s * 128: tok0 + (s + 1) * 128, hf * 1024:(hf + 1) * 1024], ystage[:])],
                      R=[b_ystage], is_out=True)

    def load_rope(region, tok0, Tt):
        k.dma([(ropec[:, 0:Tt], I["cos_" + region][:, tok0:tok0 + Tt]),
               (ropes[:, 0:Tt], I["sin_" + region][:, tok0:tok0 + Tt])], W=[b_rope])

    def drain(g):
        for _ in g:
            pass

    def interleave(cg, wg):
        cdone = cg is None
        wdone = wg is None
        while not (cdone and wdone):
            if not wdone:
                try:
                    next(wg)
                except StopIteration:
                    wdone = True
            if not cdone:
                try:
                    next(cg)
                except StopIteration:
                    cdone = True

    def chain(gens):
        for g in gens:
            for _ in g:
                yield

    def mixer(Tt, NSt, off, cdec, nq, pre, smp, L, nseg):
        norm_to_hT(1, Tt)
        dt_path(Tt, NSt, off, nq, pre)
        drain(ret_w(0, 0, Tt, NSt, off, nq, pre))
        for h in range(RH):
            cg = chain([ret_chunk(h, h % 2, s, Tt, off, cdec, nq, pre, smp) for s in range(NSt)])
            if h + 1 < RH:
                wg = ret_w(h + 1, (h + 1) % 2, Tt, NSt, off, nq, pre)
            else:
                wg = ssm_w(0, 0, Tt, NSt, off, nq, pre, L, nseg)
            interleave(cg, wg)
        if not pre:
            tap("retT")
            merge(0, Tt)
        for g in range(SG):
            cg = chain([ssm_chunk(g, g % 2, s, Tt, off, nq, pre, smp) for s in range(NSt)])
            wg = ssm_w(g + 1, (g + 1) % 2, Tt, NSt, off, nq, pre, L, nseg) if g + 1 < SG else None
            interleave(cg, wg)
        if not pre:
            tap("ssmT")
            merge(1, Tt)

    def apply_flag():
        for h in range(RH):
            for d in range(2):
                k.op("dve", lambda e, h=h, d=d: e.tensor_scalar(out=S_ret[:, h, d, :], in0=S_ret[:, h, d, :],
                                                                 scalar1=flagt[:, 0:1], scalar2=None, op0=ALU.mult),
                     R=[b_Sret[h][d], b_setup], W=[b_Sret[h][d]])
        for g in range(SG):
            k.op("dve", lambda e, g=g: e.tensor_scalar(out=S_ssm[:, g, :], in0=S_ssm[:, g, :], scalar1=flagt[:, 0:1],
                                                        scalar2=None, op0=ALU.mult),
                 R=[b_Sssm[g], b_setup], W=[b_Sssm[g]])
        for c in range(48):
            k.op("dve", lambda e, c=c: e.tensor_scalar(out=conv_hist[:, c, 0, :], in0=conv_hist[:, c, 0, :],
                                                        scalar1=flagt[:, 0:1], scalar2=None, op0=ALU.mult),
                 R=[b_chist[c], b_setup], W=[b_chist[c]])

    def store_states_main():
        for h in range(RH):
            k.dma([(O["ret_main"][h].rearrange("(dh p) e -> p dh e", p=128), S_ret[:, h, :, :])],
                  R=[b_Sret[h][0], b_Sret[h][1]], is_out=True)
        for g in range(SG):
            j = g % 2
            pt, bpt = PS()
            for a in range(4):
                k.op("pe", lambda e, a=a, pt=pt, g=g: e.transpose(pt[:, a * 128:(a + 1) * 128],
                                                                  S_ssm[:, g, a * 128:(a + 1) * 128], ident[:]),
                     R=[b_Sssm[g], b_setup], W=[bpt])
            k.op("act", lambda e, j=j, pt=pt: e.copy(out=s1buf[j][:, 0, :], in_=pt[:]), R=[bpt], W=[b_s1buf[j]])
            k.dma([(O["ssm_main"][g * 512:(g + 1) * 512, :].rearrange("(a r) n -> r a n", r=128),
                    s1buf[j][:, 0, :].rearrange("p (a b) -> p a b", a=4))], R=[b_s1buf[j]], is_out=True)
        conv_out(conv_hist, b_chist, 1, O["conv_main"])

    def conv_out(src, bsrc, nseg, out_ap):
        n = nseg * 3
        for pc in range(6):
            for cb2 in range(2):
                ps, bps = PS()
                for j in range(4):
                    c = pc * 8 + cb2 * 4 + j
                    k.op("pe", lambda e, j=j, c=c, ps=ps: e.transpose(
                        ps[0:n, j * 128:(j + 1) * 128], src[:, c, 0:nseg, :].rearrange("p q t -> p (q t)"), ident[:]),
                         R=[bsrc[c], b_setup], W=[bps])
                k.op("act", lambda e, cb2=cb2, ps=ps: e.copy(out=ystage[0:n, cb2 * 512:(cb2 + 1) * 512],
                                                             in_=ps[0:n, :]), R=[bps], W=[b_ystage])
            k.dma([(out_ap[:, pc * 1024:(pc + 1) * 1024], ystage[0:n, :])], R=[b_ystage], is_out=True)

    def conv_in_smp():
        for pc in range(6):
            k.dma([(ystage[0:12, :], I["sconv_in"][:, pc * 1024:(pc + 1) * 1024])], W=[b_ystage])
            for cb2 in range(2):
                ps, bps = PS()
                for j in range(4):
                    cl = cb2 * 4 + j
                    k.op("pe", lambda e, j=j, cl=cl, ps=ps: e.transpose(
                        ps[:, j * 12:(j + 1) * 12], ystage[0:12, cl * 128:(cl + 1) * 128], ident[0:12, 0:12]),
                         R=[b_ystage, b_setup], W=[bps])
                for j in range(4):
                    c = pc * 8 + cb2 * 4 + j
                    k.op("act", lambda e, j=j, c=c, ps=ps: e.copy(
                        out=conv_hist[:, c, :, :].rearrange("p q t -> p (q t)"), in_=ps[:, j * 12:(j + 1) * 12]),
                         R=[bps], W=[b_chist[c]])

    tapped = set()

    def tap(name):
        if name not in DEBUG_TAPS or name in tapped:
            return
        tapped.add(name)
        if name in ("retT", "ssmT", "merged", "hidden"):
            k.dma([(O["dbg_" + name], RR[:, :, :])], R=b_RR, q="pool", is_out=True)
        else:
            k.dma([(O["dbg_" + name], xT[:, :, :])], R=b_xT, q="pool", is_out=True)

    cast_all()
    setup()
    load_consts("P")
    for t in range(NPRE):
        load_x(I["x_pre"], t * T, T, NS)
        load_rope("pre", t * T, T)
        ffn("w1_gu", "w1_down", 0, T)
        mixer(T, NS, offP, cdecP, 1, True, False, T, 1)
    apply_flag()
    for t in range(NMAIN):
        load_x(I["x_main"], t * T, T, NS)
        load_rope("main", t * T, T)
        ffn("w1_gu", "w1_down", 0, T)
        tap("x1")
        mixer(T, NS, offP, cdecP, 1, False, False, T, 1)
        tap("merged")
        out_proj(T)
        tap("x2")
        ffn("w2_gu", "w2_down", 2, T)
        tap("x3")
        ple(I["p_main"], t * T, T, NS)
        tap("x4")
        final_store(O["y_main"], t * T, T, NS)
    store_states_main()
    if with_smp:
        load_consts("S")
        conv_in_smp()
        load_x(I["x_smp"], 0, 128, 1)
        load_rope("smp", 0, 128)
        ffn("w1_gu", "w1_down", 0, 128)
        mixer(128, 1, offS, cdecS, 4, False, True, 32, 4)
        out_proj(128)
        ffn("w2_gu", "w2_down", 2, 128)
        ple(I["p_smp"], 0, 128, 1)
        final_store(O["y_smp"], 0, 128, 1)
        conv_out(conv_new, b_cnew, 4, O["conv_smp"])


_NC_CACHE = {}
LAST_RESULTS = None


def run(inputs, NPRE, NMAIN, T=256, with_smp=True, ncores=8):
    f32 = np.float32
    xp = np.asarray(inputs["x_prompt"], f32)
    seq = xp.shape[1]
    half = seq // 2
    assert half == NMAIN * T and (NPRE == NMAIN)
    key = (NPRE, NMAIN, T, with_smp)
    if key not in _NC_CACHE:
        import time as _t
        _t0 = _t.time()
        _NC_CACHE[key] = build(NPRE, NMAIN, T, with_smp)
        print("[kernel] build %.1fs" % (_t.time() - _t0), flush=True)
    nc = _NC_CACHE[key]
    constP = make_consts("P")[0]
    constS = make_consts("S")[0]
    cP = np.zeros((128, CONST_W), f32)
    cP[:, :constP.shape[1]] = constP
    cS = np.zeros((128, CONST_W), f32)
    cS[:, :constS.shape[1]] = constS
    ident = np.eye(128, dtype=f32)
    cos_a, sin_a = rope_tables(np.arange(0, half))
    cos_b, sin_b = rope_tables(np.arange(half, seq))
    cos_s, sin_s = rope_tables(np.tile(PAST + np.arange(32), 4))
    shared = {}
    for n in ("w1_gu", "w1_down", "w_in", "w_br_ret", "w_br_ssm", "w_out", "w2_gu", "w2_down", "w_ple", "w_ple_gate"):
        shared[n] = np.ascontiguousarray(np.asarray(inputs[n], f32)[0])
    def colv(v, nchunk):
        return np.ascontiguousarray(np.asarray(v, f32).reshape(nchunk, 128).T)
    gc = np.stack([colv(inputs[n][0], DC) for n in ("g_ffn1", "g_mix", "g_ffn2", "g_ple")] +
                  [colv(inputs["g_final"], DC)], axis=1)
    shared["gcols"] = np.ascontiguousarray(gc)
    shared["gncols"] = np.ascontiguousarray(np.stack([colv(inputs[n][0], 32) for n in
                                                      ("ret_gn_g", "ret_gn_b", "ssm_norm_g")], axis=1))
    cwv = np.asarray(inputs["conv_w"], f32)[0]
    shared["cwcols"] = np.ascontiguousarray(np.stack([colv(cwv[w], 48) for w in range(4)] +
                                                     [colv(inputs["conv_b"][0], 48)], axis=1))
    shared["hvin"] = np.ascontiguousarray(np.broadcast_to(
        np.stack([np.asarray(inputs[n], f32)[0] for n in ("dt_bias", "a_log", "d_skip")], axis=0)[None], (128, 3, 64)))
    shared["constP"] = cP
    shared["constS"] = cS
    shared["ident"] = ident
    in_maps = []
    for c in range(ncores):
        b, hf = c // 2, c % 2
        m = dict(shared)
        m["x_pre"] = np.ascontiguousarray(xp[b, 0:half])
        m["x_main"] = np.ascontiguousarray(xp[b, hf * half:(hf + 1) * half])
        m["p_main"] = np.ascontiguousarray(np.asarray(inputs["p_prompt"], f32)[0, b, hf * half:(hf + 1) * half])
        m["x_smp"] = np.ascontiguousarray(np.asarray(inputs["x_sample"], f32)[4 * c:4 * c + 4].reshape(128, D))
        m["p_smp"] = np.ascontiguousarray(np.asarray(inputs["p_sample"], f32)[0, 4 * c:4 * c + 4].reshape(128, 256))
        m["sret_in"] = np.ascontiguousarray(np.asarray(inputs["state_ret"], f32)[0, 4 * c:4 * c + 4])
        m["sssm_in"] = np.ascontiguousarray(np.asarray(inputs["state_ssm"], f32)[0, 4 * c:4 * c + 4])
        m["sconv_in"] = np.ascontiguousarray(np.asarray(inputs["state_conv"], f32)[0, 4 * c:4 * c + 4].reshape(12, 6144))
        m["flag"] = np.full((128, 1), float(hf), f32)
        m["cos_pre"], m["sin_pre"] = cos_a, sin_a
        m["cos_main"], m["sin_main"] = (cos_a, sin_a) if hf == 0 else (cos_b, sin_b)
        m["cos_smp"], m["sin_smp"] = cos_s, sin_s
        in_maps.append(m)
    import time as _t
    _t0 = _t.time()
    res = run_bass_kernel_spmd(nc, in_maps, core_ids=list(range(ncores)))
    print("[kernel] spmd launch wall %.1fs" % (_t.time() - _t0), flush=True)
    R = res.results
    global LAST_RESULTS
    LAST_RESULTS = R
    nb = xp.shape[0]
    y_prompt = np.zeros((nb, seq, D), f32)
    ret_p = np.zeros((1, nb, RH, 256, 512), f32)
    ssm_p = np.zeros((1, nb, 64, 64, 128), f32)
    conv_p = np.zeros((1, nb, 3, 6144), f32)
    nsb = np.asarray(inputs["x_sample"]).shape[0]
    y_sample = np.zeros((nsb, 32, D), f32)
    ret_s = np.zeros((1, nsb, RH, 256, 512), f32)
    ssm_s = np.zeros((1, nsb, 64, 64, 128), f32)
    conv_s = np.zeros((1, nsb, 3, 6144), f32)
    for c in range(ncores):
        b, hf = c // 2, c % 2
        y_prompt[b, hf * half:(hf + 1) * half] = R[c]["y_main"]
        if hf == 1:
            ret_p[0, b] = R[c]["ret_main"]
            ssm_p[0, b] = R[c]["ssm_main"].reshape(64, 64, 128)
            conv_p[0, b] = R[c]["conv_main"]
        if with_smp:
            y_sample[4 * c:4 * c + 4] = R[c]["y_smp"].reshape(4, 32, D)
            ret_s[0, 4 * c:4 * c + 4] = R[c]["ret_smp"]
            ssm_s[0, 4 * c:4 * c + 4] = R[c]["ssm_smp"].reshape(4, 64, 64, 128)
            conv_s[0, 4 * c:4 * c + 4] = R[c]["conv_smp"].reshape(4, 3, 6144)
    return (y_prompt, y_sample, ret_p, ssm_p, conv_p, ret_s, ssm_s, conv_s)


def kernel(**inputs):
    return run(inputs, 16, 16, 256, True, 8)
```

```python
import numpy as np
from contextlib import ExitStack
import concourse.bass as bass
import concourse.mybir as mybir
from concourse.bass_utils import run_bass_kernel_spmd

F32 = mybir.dt.float32
BF16 = mybir.dt.bfloat16
AF = mybir.ActivationFunctionType
ALU = mybir.AluOpType

D = 2048
DC = 16
FF = 5632
FC = 44
NIN = 26688
RH = 8
SG = 8
EPS = 1e-6
PAST = 4096
OFF_Q, OFF_K, OFF_V, OFF_RG, OFF_Z, OFF_X, OFF_B, OFF_C, OFF_DT, OFF_GR, OFF_GS = (
    0, 2048, 4096, 8192, 12288, 16384, 20480, 21504, 22528, 22592, 24640)
SAME_SYNC = True
WSLOTS = 3
WCOLS = 256


class Buf:
    __slots__ = ("name", "w", "r", "region", "lo", "hi", "dkey", "dcnt")

    def __init__(self, name, region=None, lo=0, hi=1):
        self.name = name
        self.w = None
        self.r = {}
        self.region = region
        self.lo = lo
        self.hi = hi
        self.dkey = None
        self.dcnt = 0
        if region is not None:
            region.append(self)

    def overl(self):
        if self.region is None:
            return (self,)
        return [b for b in self.region if b.lo < self.hi and self.lo < b.hi]


class K:
    def __init__(self, nc, es, dry):
        self.nc = nc
        self.es = es
        self.dry = dry
        self.eng = {"pe": nc.tensor, "act": nc.scalar, "dve": nc.vector, "pool": nc.gpsimd, "sp": nc.sync}
        self.sems = {}
        self.cnt = {}
        self.waited = {e: {} for e in self.eng}
        self.nsem = 0
        self.wreq = []
        self.wpos = 0
        self.wissued = 0
        self.out_events = []
        if not dry:
            for e in ("pe", "act", "dve", "pool"):
                self.sems[e] = es.enter_context(nc.semaphore("s_" + e))
                self.cnt[e] = 0
                self.nsem += 1

    def _wait(self, eng, key, val):
        if key == eng and (eng == "pe" or not SAME_SYNC):
            return
        wd = self.waited[eng]
        if wd.get(key, 0) >= val:
            return
        self.eng[eng].wait_ge(self.sems[key], val)
        wd[key] = val

    def _deps(self, eng, R, W):
        need = {}

        def add(ev):
            if ev is not None and need.get(ev[0], 0) < ev[1]:
                need[ev[0]] = ev[1]
        for b in R:
            for o in b.overl():
                add(o.w)
        for b in W:
            for o in b.overl():
                add(o.w)
                for k_, v_ in o.r.items():
                    add((k_, v_))
        for k_, v_ in need.items():
            self._wait(eng, k_, v_)

    def _record(self, ev, R, W):
        for b in R:
            for o in b.overl():
                if o.r.get(ev[0], 0) < ev[1]:
                    o.r[ev[0]] = ev[1]
        for b in W:
            for o in b.overl():
                o.w = ev
                o.r = {}

    def op(self, eng, fn, R=(), W=()):
        if self.dry:
            return
        self._deps(eng, R, W)
        ins = fn(self.eng[eng])
        self.cnt[eng] += 1
        ins.then_inc(self.sems[eng], 1)
        self._record((eng, self.cnt[eng]), R, W)

    def _dsem(self, b):
        if b.dkey is None:
            b.dkey = "d_" + b.name
            self.sems[b.dkey] = self.es.enter_context(self.nc.semaphore(b.dkey))
            self.nsem += 1
        return b.dkey

    def dma(self, pieces, R=(), W=(), q="sp", is_out=False):
        if self.dry:
            return
        self._deps(q, R, W)
        b = W[0] if W else R[0]
        key = self._dsem(b)
        for (o, i) in pieces:
            self.eng[q].dma_start(out=o, in_=i).then_inc(self.sems[key], 16)
            b.dcnt += 16
        ev = (key, b.dcnt)
        self._record(ev, R, W)
        if is_out:
            self.out_events.append(ev)

    def finish(self):
        if self.dry:
            return
        for ev in self.out_events:
            self._wait("sp", ev[0], ev[1])


def _gammas():
    return 1.0 - np.exp2(-5.0 - np.arange(RH, dtype=np.float64))


def make_consts(kind):
    if kind == "P":
        nq, blk, pos, clen = 1, np.zeros(128, int), np.arange(128), 128
    else:
        nq, blk, pos, clen = 4, np.arange(128) // 32, np.arange(128) % 32, 32
    g = _gammas()
    same = blk[:, None] == blk[None, :]
    ar = np.arange(128)
    triT = ((ar[:, None] <= ar[None, :]) & same).astype(np.float64)
    SL = ((ar[:, None] > ar[None, :]) & same).astype(np.float64)
    maskT = ((ar[None, :] >= ar[:, None]) & same).astype(np.float64)
    dif = np.maximum(ar[None, :] - ar[:, None], 0)
    dmatT = np.stack([np.power(g[h], dif) * maskT * (256.0 ** -0.5) for h in range(RH)], axis=1)
    qdec = np.stack([np.power(g[h], pos + 1.0) for h in range(RH)], axis=0)
    qdec_bc = np.broadcast_to(qdec[None], (128, RH, 128))
    kdec = np.stack([np.power(g[h], clen - 1.0 - pos) * (256.0 ** -0.5) for h in range(RH)], axis=1)
    kdecq = np.stack([kdec * (blk == q)[:, None] for q in range(nq)], axis=1)
    colmask = np.stack([np.broadcast_to((blk == q)[None, :], (128, 128)) for q in range(nq)], axis=1)
    rowmask = np.stack([(blk == q) for q in range(nq)], axis=1).astype(np.float64)
    seqones = np.stack([np.broadcast_to((blk == q)[:, None], (128, 128)) for q in range(nq)], axis=1)
    parts = [("triT", triT), ("SL", SL), ("maskT", maskT), ("dmatT", dmatT.reshape(128, -1)),
             ("qdec", qdec_bc.reshape(128, -1)), ("kdecq", kdecq.reshape(128, -1)),
             ("colmask", colmask.reshape(128, -1)), ("rowmask", rowmask), ("seqones", seqones.reshape(128, -1))]
    offs = {}
    o = 0
    for n, a in parts:
        offs[n] = (o, a.shape[1])
        o += a.shape[1]
    arr = np.concatenate([np.asarray(a, np.float64) for _, a in parts], axis=1).astype(np.float32)
    cdec = [float(np.power(g[h], float(clen))) for h in range(RH)]
    return arr, offs, cdec, nq


CONST_W = 3500


def rope_tables(pos):
    half = 128
    inv = (10000.0 ** (-np.arange(half, dtype=np.float32) / np.float32(half))).astype(np.float32)
    ang = pos.astype(np.float32)[None, :] * inv[:, None]
    return np.cos(ang).astype(np.float32), np.sin(ang).astype(np.float32)


DEBUG_TAPS = []
_REPORT = False


def build(NPRE, NMAIN, T=256, with_smp=True):
    NS = T // 128
    nc = bass.Bass("TRN2", target_bir_lowering=False)
    npre_tok, nmain_tok = NPRE * T, NMAIN * T

    def din(name, shape, dt=F32):
        return nc.dram_tensor(name, list(shape), dt, kind="ExternalInput").ap()

    def dout(name, shape, dt=F32):
        return nc.dram_tensor(name, list(shape), dt, kind="ExternalOutput").ap()

    def dint(name, shape, dt=BF16):
        return nc.dram_tensor(name, list(shape), dt, kind="Internal").ap()

    I = {}
    I["x_pre"] = din("x_pre", [max(npre_tok, 1), D])
    I["x_main"] = din("x_main", [nmain_tok, D])
    I["p_main"] = din("p_main", [nmain_tok, 256])
    I["x_smp"] = din("x_smp", [128, D])
    I["p_smp"] = din("p_smp", [128, 256])
    I["sret_in"] = din("sret_in", [4, RH, 256, 512])
    I["sssm_in"] = din("sssm_in", [4, 64, 64, 128])
    I["sconv_in"] = din("sconv_in", [12, 6144])
    I["flag"] = din("flag", [128, 1])
    for r, n in (("pre", max(npre_tok, 1)), ("main", nmain_tok), ("smp", 128)):
        I["cos_" + r] = din("cos_" + r, [128, n])
        I["sin_" + r] = din("sin_" + r, [128, n])
    I["constP"] = din("constP", [128, CONST_W])
    I["constS"] = din("constS", [128, CONST_W])
    I["ident"] = din("ident", [128, 128])
    wshapes = {"w1_gu": (D, 2 * FF), "w1_down": (FF, D), "w_in": (D, NIN), "w_br_ret": (4096, D),
               "w_br_ssm": (4096, D), "w_out": (D, D), "w2_gu": (D, 2 * FF), "w2_down": (FF, D),
               "w_ple": (256, D), "w_ple_gate": (D, D)}
    WF = {n: din(n, s) for n, s in wshapes.items()}
    WB = {n: dint(n + "_bf", s) for n, s in wshapes.items()}
    I["gcols"] = din("gcols", [128, 5, DC])
    I["gncols"] = din("gncols", [128, 3, 32])
    I["cwcols"] = din("cwcols", [128, 5, 48])
    I["hvin"] = din("hvin", [128, 3, 64])
    O = {}
    O["y_main"] = dout("y_main", [nmain_tok, D])
    O["y_smp"] = dout("y_smp", [128, D])
    O["ret_main"] = dout("ret_main", [RH, 256, 512])
    O["ssm_main"] = dout("ssm_main", [64 * 64, 128])
    O["conv_main"] = dout("conv_main", [3, 6144])
    O["ret_smp"] = dout("ret_smp", [4, RH, 256, 512])
    O["ssm_smp"] = dout("ssm_smp", [4, 64 * 64, 128])
    O["conv_smp"] = dout("conv_smp", [12, 6144])

    for tn in DEBUG_TAPS:
        rr = tn in ("retT", "ssmT", "merged", "hidden")
        O["dbg_" + tn] = dout("dbg_" + tn, [128, 48 if rr else DC, T], BF16 if rr else F32)
    constP, offP, cdecP, _ = make_consts("P")
    constS, offS, cdecS, _ = make_consts("S")

    wreq_holder = []
    for dry in (True, False):
        es = ExitStack()
        k = K(nc, es, dry)
        if not dry:
            k.wreq = wreq_holder[0]
        _emit(nc, es, k, I, O, WF, WB, wshapes, NPRE, NMAIN, T, NS, with_smp, offP, cdecP, offS, cdecS)
        if dry:
            wreq_holder.append(k.wreq)
            es.close()
        else:
            assert k.wpos == len(k.wreq), (k.wpos, len(k.wreq))
            k.finish()
            es.close()
    return nc


def _emit(nc, es, k, I, O, WF, WB, wshapes, NPRE, NMAIN, T, NS, with_smp, offP, cdecP, offS, cdecS):
    dry = k.dry
    TP = T + 3

    class Dummy:
        def __getitem__(self, _):
            return self

        def __getattr__(self, _):
            return lambda *a, **kw: self

    def sb(name, shape, dt=F32):
        if dry:
            return Dummy()
        return es.enter_context(nc.sbuf_tensor("sb_" + name, list(shape), dt))

    def B(name):
        return Buf(name)

    xT = sb("xT", [128, DC, T])
    b_xT = [B("xT%d" % c) for c in range(DC)]
    hT = sb("hT", [128, DC, T], BF16)
    b_hT = [B("hT%d" % c) for c in range(DC)]
    HT = "HT"
    wring = sb("wring", [128, WSLOTS, 16 * WCOLS], BF16)
    b_w = [B("w%d" % i) for i in range(WSLOTS)]
    S_ret = sb("S_ret", [128, RH, 2, 512])
    Sreg = []
    b_Sret = [[Buf("Sret%d_%d" % (h, d), Sreg, (h * 2 + d) * 2048, (h * 2 + d + 1) * 2048) for d in range(2)]
              for h in range(RH)]
    S_ssm = sb("S_ssm", [128, SG, 512])
    b_Sssm = [B("Sssm%d" % g) for g in range(SG)]
    RR = sb("RR", [128, 48, T], BF16)
    b_RR = [B("RR%d" % j) for j in range(48)]
    conv_hist = sb("conv_hist", [128, 48, 4, 3])
    b_chist = [B("chist%d" % c) for c in range(48)]
    if dry:
        conv_new = Dummy()
    else:
        conv_new = S_ret[:, 5, :, :].rearrange("p a b -> p (a b)")[:, 0:576].rearrange("p (c q t) -> p c q t", c=48, q=4)
    b_cnew = [Buf("cnew%d" % c, Sreg, 5 * 4096 + c * 48, 5 * 4096 + (c + 1) * 48) for c in range(48)]
    USZ = 6144
    U = [sb("U%d" % u, [128, USZ], BF16) for u in range(2)]
    Ureg = [[], []]

    def uview(u, name, lo_b, shape, dt):
        n = int(np.prod(shape))
        nb = n * (4 if dt == F32 else 2)
        buf = Buf("U%d_%s" % (u, name), Ureg[u], lo_b, lo_b + nb)
        if dry:
            return Dummy(), buf
        ap = U[u][:, lo_b // 2:(lo_b + nb) // 2]
        if dt == F32:
            ap = ap.bitcast(F32)
        if len(shape) == 2:
            ap = ap.rearrange("p (a b) -> p a b", a=shape[0])
        elif len(shape) == 3:
            ap = ap.rearrange("p (a b c) -> p a b c", a=shape[0], b=shape[1])
        return ap, buf

    RV = []
    SV = []
    for u in range(2):
        o = 0
        d = {}
        for name, shape, dt in (("qT", (2, T), BF16), ("kT", (2, T), BF16), ("qdT", (2, T), F32),
                                ("kdtm", (4, 256), BF16), ("vtm", (NS, 512), BF16),
                                ("grgs", (4, T), BF16), ("brgs", (4, T), BF16)):
            d[name] = uview(u, "r_" + name, o, shape, dt)
            o += int(np.prod(shape)) * (4 if dt == F32 else 2)
        assert o <= USZ * 2, o
        RV.append(d)
        d = {}
        lay = (("zs", (NS, 512), BF16, 0), ("xpad", (4, TP + 9), BF16, 2048), ("xtm", (NS, 512), BF16, 2048),
               ("bcpad", (2, TP + 9), BF16, 4224), ("Btm", (NS, 128), BF16, 4224),
               ("xcT", (4, T), BF16, 5312), ("xdt", (NS, 512), BF16, 5312),
               ("BT", (1, T), F32, 7360), ("CT", (1, T), F32, 8384))
        for name, shape, dt, o in lay:
            d[name] = uview(u, "s_" + name, o, shape, dt)
            assert o + int(np.prod(shape)) * (4 if dt == F32 else 2) <= USZ * 2
        SV.append(d)

    xstage = sb("xstage", [128, 1024])
    b_xstage = B("xstage")
    ystage = sb("ystage", [128, 1024])
    b_ystage = B("ystage")
    pstage = sb("pstage", [128, 256])
    b_pstage = B("pstage")
    pT = sb("pT", [128, 2, T], BF16)
    b_pT = B("pT")
    ropec = sb("ropec", [128, T])
    ropes = sb("ropes", [128, T])
    b_rope = B("rope")
    consts = sb("consts", [128, CONST_W])
    b_consts = B("consts")
    ident = sb("ident_sb", [128, 128])
    identb = sb("identb", [128, 128], BF16)
    onesf = sb("onesf", [128, 128])
    onesb = sb("onesb", [128, 128], BF16)
    b_setup = B("setup")
    gcols = sb("gcols", [128, 5, DC])
    gncols = sb("gncols", [128, 3, 32])
    cwcols = sb("cwcols", [128, 6, 48])
    hv = sb("hv", [128, 4, 64])
    flagt = sb("flagt", [128, 1])
    sqt = [sb("sqt%d" % i, [128, T], BF16) for i in range(2)]
    b_sqt = [B("sqt%d" % i) for i in range(2)]
    rstd = sb("rstd", [128, T])
    b_rstd = B("rstd")
    tmpA = [sb("tmpA%d" % i, [128, 512]) for i in range(4)]
    b_tmpA = [B("tmpA%d" % i) for i in range(4)]
    tmpB = [sb("tmpB%d" % i, [128, 512], BF16) for i in range(3)]
    b_tmpB = [B("tmpB%d" % i) for i in range(3)]
    dtt = sb("dtt", [128, NS, 6, 64])
    b_dtt = [B("dtt%d" % s) for s in range(NS)]
    decb = sb("decb", [128, NS, 4, 64])
    b_decb = [B("decb%d" % s) for s in range(NS)]
    wendq = sb("wendq", [128, 4, 64])
    b_wendq = B("wendq")
    attm = [sb("attm%d" % i, [128, 128], BF16) for i in range(2)]
    b_attm = [B("attm%d" % i) for i in range(2)]
    cbm = [sb("cbm%d" % i, [128, 128], BF16) for i in range(2)]
    b_cbm = [B("cbm%d" % i) for i in range(2)]
    rhsR_ = sb("rhsR", [128, 8, 128])
    rhsR = [rhsR_, rhsR_]
    b_rhsR_ = B("rhsR")
    b_rhsR = [b_rhsR_, b_rhsR_]
    wTt_ = sb("wTt", [128, 8, 128], BF16)
    wTt = [wTt_, wTt_]
    b_wTt_ = B("wTt")
    b_wTt = [b_wTt_, b_wTt_]
    xwt = [sb("xwt%d" % i, [128, 512], BF16) for i in range(2)]
    b_xwt = [B("xwt%d" % i) for i in range(2)]
    onh = [sb("onh%d" % i, [128, 512], BF16) for i in range(2)]
    b_onh = [B("onh%d" % i) for i in range(2)]
    stat = [sb("stat%d" % i, [128, 16]) for i in range(2)]
    b_stat = [B("stat%d" % i) for i in range(2)]
    if dry:
        s0buf = [Dummy(), Dummy()]
        s1buf = [Dummy(), Dummy()]
        qmk = [Dummy(), Dummy()]
    else:
        s0buf = [S_ret[:, j, :, :] for j in range(2)]
        s1buf = [S_ret[:, 2 + j, :, :] for j in range(2)]
        qmk = [S_ret[:, 4, j, 0:256].rearrange("p (a b) -> p a b", a=2) for j in range(2)]
    b_s0buf = [Buf("s0buf%d" % j, Sreg, j * 4096, (j + 1) * 4096) for j in range(2)]
    b_s1buf = [Buf("s1buf%d" % j, Sreg, (2 + j) * 4096, (3 + j) * 4096) for j in range(2)]
    b_qmk = [Buf("qmk%d" % j, Sreg, 4 * 4096 + j * 2048, 4 * 4096 + j * 2048 + 1024) for j in range(2)]

    psum = []
    b_ps = []
    for i in range(8):
        if dry:
            psum.append(Dummy())
        else:
            psum.append(es.enter_context(nc.psum_tensor("ps%d" % i, [128, 512], F32)))
        b_ps.append(B("ps%d" % i))
    pstate = {"i": 0, "a": 0, "b": 0, "sq": 0}
    if not dry and _REPORT:
        print("[kernel] sbuf bytes remaining per partition:", nc.sbuf_bytes_remaining)

    pinned = set()

    def PS(pin=False):
        i = pstate["i"]
        while i in pinned:
            i = (i + 1) % 8
        pstate["i"] = (i + 1) % 8
        if pin:
            pinned.add(i)
        return psum[i], b_ps[i]

    def unpin(bps):
        pinned.discard(b_ps.index(bps))

    def TA():
        i = pstate["a"]
        pstate["a"] = (i + 1) % 4
        return tmpA[i], b_tmpA[i]

    def TB():
        i = pstate["b"]
        pstate["b"] = (i + 1) % 3
        return tmpB[i], b_tmpB[i]

    b_cast = {n: B("cast_" + n) for n in WB}

    def cast_all():
        order = ["w1_gu", "w1_down", "w_in", "w_br_ret", "w_br_ssm", "w_out", "w2_gu", "w2_down", "w_ple",
                 "w_ple_gate"]
        for n in order:
            rows, cols = wshapes[n]
            pieces = []
            rb = 128
            for r0 in range(0, rows, rb):
                pieces.append((WB[n][r0:r0 + rb, :], WF[n][r0:r0 + rb, :]))
            k.dma(pieces, W=[b_cast[n]], q="pool")

    def _issue_next():
        if k.dry or k.wissued >= len(k.wreq):
            return
        i = k.wissued
        slot = i % WSLOTS
        pieces = []
        names = set()
        for (n, r0, nkc, c0, ncols, dcol) in k.wreq[i]:
            src = WB[n][r0:r0 + nkc * 128, c0:c0 + ncols].rearrange("(kc p) n -> p kc n", p=128)
            dst = wring[:, slot, :].rearrange("p (kc n) -> p kc n", kc=16)[:, 0:nkc, dcol:dcol + ncols]
            pieces.append((dst, src))
            names.add(n)
        k.dma(pieces, R=[b_cast[n] for n in names], W=[b_w[slot]])
        k.wissued += 1

    def wtile(pieces):
        if k.dry:
            k.wreq.append(tuple(pieces))
            return Dummy(), b_w[0]
        assert tuple(pieces) == k.wreq[k.wpos], (pieces, k.wreq[k.wpos])
        while k.wissued < min(k.wpos + WSLOTS, len(k.wreq)):
            _issue_next()
        slot = k.wpos % WSLOTS
        k.wpos += 1
        return wring[:, slot, :].rearrange("p (kc n) -> p kc n", kc=16), b_w[slot]

    def wrel():
        if k.dry:
            return
        while k.wissued < min(k.wpos + WSLOTS, len(k.wreq)):
            _issue_next()

    def setup():
        k.dma([(ident[:], I["ident"])], W=[b_setup])
        k.op("dve", lambda e: e.tensor_copy(identb[:], ident[:]), R=[b_setup], W=[b_setup])
        k.op("pool", lambda e: e.memset(onesf[:], 1.0), W=[b_setup])
        k.op("pool", lambda e: e.memset(onesb[:], 1.0), W=[b_setup])
        k.dma([(gcols[:], I["gcols"]), (gncols[:], I["gncols"]), (cwcols[:, 0:5, :], I["cwcols"]),
               (hv[:, 0:3, :], I["hvin"]), (flagt[:], I["flag"])], W=[b_setup], q="sp")
        k.op("dve", lambda e: e.tensor_scalar(out=cwcols[:, 5, :], in0=cwcols[:, 4, :], scalar1=-1.0, scalar2=None,
                                              op0=ALU.mult), R=[b_setup], W=[b_setup])
        k.op("act", lambda e: e.activation(out=hv[:, 3, :], in_=hv[:, 1, :], func=AF.Exp), R=[b_setup], W=[b_setup])
        k.op("dve", lambda e: e.tensor_scalar(out=hv[:, 1, :], in0=hv[:, 3, :], scalar1=-1.0, scalar2=None,
                                              op0=ALU.mult), R=[b_setup], W=[b_setup])
        for h in range(RH):
            for d in range(2):
                k.op("pool", lambda e, h=h, d=d: e.memset(S_ret[:, h, d, :], 0.0), W=[b_Sret[h][d]])
        for g in range(SG):
            k.op("pool", lambda e, g=g: e.memset(S_ssm[:, g, :], 0.0), W=[b_Sssm[g]])
        for c in range(48):
            k.op("pool", lambda e, c=c: e.memset(conv_hist[:, c, :, :], 0.0), W=[b_chist[c]])

    def load_consts(which):
        k.dma([(consts[:], I["constP" if which == "P" else "constS"])], W=[b_consts])

    def cv(off, name, n=None):
        o, w = off[name]
        return consts[:, o:o + w]

    def load_x(src_ap, tok0, Tt, NSt):
        for s in range(NSt):
            for hf in range(2):
                k.dma([(xstage[:], src_ap[tok0 + s * 128: tok0 + (s + 1) * 128, hf * 1024:(hf + 1) * 1024])],
                      W=[b_xstage])
                for cb2 in range(2):
                    cb = hf * 2 + cb2
                    ps, bps = PS()
                    for j in range(4):
                        cl = cb2 * 4 + j
                        k.op("pe", lambda e, cl=cl, j=j, ps=ps: e.transpose(ps[:, j * 128:(j + 1) * 128],
                                                                            xstage[:, cl * 128:(cl + 1) * 128], ident[:]),
                             R=[b_xstage, b_setup], W=[bps])
                    if cb % 2 == 0:
                        k.op("act", lambda e, cb=cb, ps=ps, s=s: e.copy(
                            out=xT[:, cb * 4:cb * 4 + 4, s * 128:(s + 1) * 128],
                            in_=ps[:].rearrange("p (a b) -> p a b", a=4)), R=[bps], W=b_xT[cb * 4:cb * 4 + 4])
                    else:
                        k.op("dve", lambda e, cb=cb, ps=ps, s=s: e.tensor_copy(
                            xT[:, cb * 4:cb * 4 + 4, s * 128:(s + 1) * 128],
                            ps[:].rearrange("p (a b) -> p a b", a=4)), R=[bps], W=b_xT[cb * 4:cb * 4 + 4])

    def rms_stats(Tt):
        ps, bps = PS()
        for c in range(DC):
            i = pstate["sq"]
            pstate["sq"] = (i + 1) % 2
            k.op("act", lambda e, c=c, i=i: e.activation(out=sqt[i][:, 0:Tt], in_=xT[:, c, 0:Tt], func=AF.Square),
                 R=[b_xT[c]], W=[b_sqt[i]])
            k.op("pe", lambda e, c=c, i=i, ps=ps: e.matmul(ps[:, 0:Tt], onesb[:], sqt[i][:, 0:Tt], start=(c == 0),
                                                           stop=(c == DC - 1)), R=[b_sqt[i], b_setup], W=[bps])
        k.op("act", lambda e, ps=ps: e.activation(out=rstd[:, 0:Tt], in_=ps[:, 0:Tt], func=AF.Ln, bias=EPS,
                                                  scale=1.0 / D), R=[bps], W=[b_rstd])
        k.op("act", lambda e: e.activation(out=rstd[:, 0:Tt], in_=rstd[:, 0:Tt], func=AF.Exp, scale=-0.5),
             R=[b_rstd], W=[b_rstd])

    def norm_to_hT(gi, Tt):
        rms_stats(Tt)
        for c in range(DC):
            k.op("dve", lambda e, c=c: e.scalar_tensor_tensor(out=hT[:, c, 0:Tt], in0=xT[:, c, 0:Tt],
                                                              scalar=gcols[:, gi, c:c + 1], in1=rstd[:, 0:Tt],
                                                              op0=ALU.mult, op1=ALU.mult),
                 R=[b_xT[c], b_rstd, b_setup], W=[b_hT[c]])

    def proj_fm(wname, col0, nch, rhs_fn, nkc, rbufs, row0=0):
        res = []
        for t0 in range(0, nch, 2):
            n = min(2, nch - t0)
            wt, bw = wtile([(wname, row0, nkc, col0 + t0 * 128, n * 128, 0)])
            for j in range(n):
                ps, bps = PS()
                for kc in range(nkc):
                    k.op("pe", lambda e, kc=kc, j=j, ps=ps, wt=wt: e.matmul(
                        ps[:, 0:rhs_fn(kc).shape[-1]], wt[:, kc, j * 128:(j + 1) * 128], rhs_fn(kc),
                        start=(kc == 0), stop=(kc == nkc - 1)), R=[bw] + ([b_hT[kc]] if rbufs is HT else rbufs),
                         W=[bps])
                res.append((t0 + j, ps, bps))
            wrel()
        return res

    def ffn(wgu, wdown, gi, Tt):
        norm_to_hT(gi, Tt)
        for jp in range(FC // 2):
            gl = proj_fm(wgu, jp * 256, 2, lambda kc: hT[:, kc, 0:Tt], DC, HT)
            sgs = []
            for (j, ps, bps) in gl:
                ta, bta = TA()
                k.op("act", lambda e, ps=ps, ta=ta: e.activation(out=ta[:, 0:Tt], in_=ps[:, 0:Tt], func=AF.Silu),
                     R=[bps], W=[bta])
                sgs.append((ta, bta))
            ul = proj_fm(wgu, FF + jp * 256, 2, lambda kc: hT[:, kc, 0:Tt], DC, HT)
            for (j, ps, bps), (ta, bta) in zip(ul, sgs):
                hc = jp * 2 + j
                k.op("dve", lambda e, ps=ps, ta=ta, hc=hc: e.tensor_tensor(out=RR[:, hc, 0:Tt], in0=ps[:, 0:Tt],
                                                                          in1=ta[:, 0:Tt], op=ALU.mult),
                     R=[bps, bta], W=[b_RR[hc]])
        kps = [(0, 16), (16, 16), (32, 12)]
        for op_ in range(DC // 2):
            banks = [PS(), PS()]
            for pi, (k0, nk) in enumerate(kps):
                wt, bw = wtile([(wdown, k0 * 128, nk, op_ * 256, 256, 0)])
                for j in range(2):
                    ps, bps = banks[j]
                    for kc in range(nk):
                        k.op("pe", lambda e, kc=kc, j=j, ps=ps, wt=wt, k0=k0, pi=pi, nk=nk: e.matmul(
                            ps[:, 0:Tt], wt[:, kc, j * 128:(j + 1) * 128], RR[:, k0 + kc, 0:Tt],
                            start=(pi == 0 and kc == 0), stop=(pi == 2 and kc == nk - 1)),
                             R=[bw, b_RR[k0 + kc]], W=[bps])
                wrel()
            for j in range(2):
                ps, bps = banks[j]
                c = op_ * 2 + j
                k.op("dve", lambda e, ps=ps, c=c: e.scalar_tensor_tensor(out=xT[:, c, 0:Tt], in0=ps[:, 0:Tt],
                                                                         scalar=0.5, in1=xT[:, c, 0:Tt],
                                                                         op0=ALU.mult, op1=ALU.add),
                     R=[bps, b_xT[c]], W=[b_xT[c]])

    def dt_path(Tt, NSt, off, nq, pre):
        wt, bw = wtile([("w_in", 0, DC, OFF_DT, 64, 0)])
        pss = []
        for s in range(NSt):
            ps, bps = PS()
            for kc in range(DC):
                k.op("pe", lambda e, kc=kc, ps=ps, s=s, wt=wt: e.matmul(ps[:, 0:64], hT[:, kc, s * 128:(s + 1) * 128],
                                                                        wt[:, kc, 0:64], start=(kc == 0),
                                                                        stop=(kc == DC - 1)), R=[bw, b_hT[kc]], W=[bps])
            pss.append((ps, bps))
        wrel()
        for s in range(NSt):
            ps, bps = pss[s]
            d_ = dtt[:, s]
            bd = b_dtt[s]
            k.op("dve", lambda e, ps=ps, d_=d_: e.tensor_tensor(out=d_[:, 4, :], in0=ps[:, 0:64], in1=hv[:, 0, :],
                                                                op=ALU.add), R=[bps, b_setup], W=[bd])
            k.op("dve", lambda e, d_=d_: e.scalar_tensor_tensor(out=d_[:, 5, :], in0=d_[:, 4, :], scalar=-1.0,
                                                                in1=d_[:, 4, :], op0=ALU.mult, op1=ALU.max),
                 R=[bd], W=[bd])
            k.op("act", lambda e, d_=d_: e.activation(out=d_[:, 5, :], in_=d_[:, 5, :], func=AF.Exp, scale=-1.0),
                 R=[bd], W=[bd])
            k.op("act", lambda e, d_=d_: e.activation(out=d_[:, 5, :], in_=d_[:, 5, :], func=AF.Ln, bias=1.0,
                                                      scale=1.0), R=[bd], W=[bd])
            k.op("dve", lambda e, d_=d_: e.scalar_tensor_tensor(out=d_[:, 0, :], in0=d_[:, 4, :], scalar=0.0,
                                                                in1=d_[:, 5, :], op0=ALU.max, op1=ALU.add),
                 R=[bd], W=[bd])
            k.op("dve", lambda e, d_=d_: e.tensor_tensor(out=d_[:, 1, :], in0=d_[:, 0, :], in1=hv[:, 1, :],
                                                         op=ALU.mult), R=[bd, b_setup], W=[bd])
            if not pre:
                ps2, bps2 = PS()
                k.op("pe", lambda e, ps2=ps2, d_=d_: e.matmul(ps2[:, 0:64], cv(off, "triT"), d_[:, 1, :], start=True,
                                                              stop=True), R=[bd, b_consts], W=[bps2])
                k.op("act", lambda e, ps2=ps2, d_=d_: e.activation(out=d_[:, 2, :], in_=ps2[:, 0:64], func=AF.Exp),
                     R=[bps2], W=[bd])
            ps3, bps3 = PS()
            k.op("pe", lambda e, ps3=ps3, d_=d_: e.matmul(ps3[:, 0:64], cv(off, "SL"), d_[:, 1, :], start=True,
                                                          stop=True), R=[bd, b_consts], W=[bps3])
            k.op("act", lambda e, ps3=ps3, d_=d_: e.activation(out=d_[:, 3, :], in_=ps3[:, 0:64], func=AF.Exp),
                 R=[bps3], W=[bd])
            ps4, bps4 = PS()
            for q in range(nq):
                so = cv(off, "seqones")[:, q * 128:(q + 1) * 128]
                k.op("pe", lambda e, ps4=ps4, d_=d_, q=q, so=so: e.matmul(ps4[:, q * 64:(q + 1) * 64], so,
                                                                           d_[:, 1, :], start=True, stop=True),
                     R=[bd, b_consts], W=[bps4])
            k.op("act", lambda e, ps4=ps4, s=s: e.activation(
                out=decb[:, s, 0:nq, :], in_=ps4[:, 0:nq * 64].rearrange("p (q h) -> p q h", q=nq), func=AF.Exp),
                 R=[bps4], W=[b_decb[s]])

    def silu_exp(src, bsrc, out_ap, bout, ncol, bias_col=None, nbias_col=None):
        te, bte = TA()
        if nbias_col is None:
            k.op("act", lambda e: e.activation(out=te[:, 0:ncol], in_=src, func=AF.Exp, scale=-1.0), R=[bsrc], W=[bte])
        else:
            k.op("act", lambda e: e.activation(out=te[:, 0:ncol], in_=src, func=AF.Exp, scale=-1.0, bias=nbias_col),
                 R=[bsrc, b_setup], W=[bte])
        k.op("act", lambda e: e.activation(out=te[:, 0:ncol], in_=te[:, 0:ncol], func=AF.Ln, bias=1.0, scale=1.0),
             R=[bte], W=[bte])
        k.op("act", lambda e: e.activation(out=te[:, 0:ncol], in_=te[:, 0:ncol], func=AF.Exp, scale=-1.0),
             R=[bte], W=[bte])
        if bias_col is None:
            k.op("dve", lambda e: e.tensor_tensor(out=out_ap, in0=src, in1=te[:, 0:ncol], op=ALU.mult),
                 R=[bsrc, bte], W=[bout])
        else:
            k.op("dve", lambda e: e.scalar_tensor_tensor(out=out_ap, in0=src, scalar=bias_col, in1=te[:, 0:ncol],
                                                         op0=ALU.add, op1=ALU.mult), R=[bsrc, bte, b_setup], W=[bout])

    def rope(pa, ba, pb, bb, Tt, out1, out2, wbufs):
        t1, b1 = TA()
        t2, b2 = TA()
        k.op("dve", lambda e: e.tensor_tensor(out=t1[:, 0:Tt], in0=pa[:, 0:Tt], in1=ropec[:, 0:Tt], op=ALU.mult),
             R=[ba, b_rope], W=[b1])
        k.op("dve", lambda e: e.tensor_tensor(out=t2[:, 0:Tt], in0=pb[:, 0:Tt], in1=ropes[:, 0:Tt], op=ALU.mult),
             R=[bb, b_rope], W=[b2])
        k.op("dve", lambda e: e.tensor_tensor(out=out1, in0=t1[:, 0:Tt], in1=t2[:, 0:Tt], op=ALU.subtract),
             R=[b1, b2], W=wbufs)
        t3, b3 = TA()
        k.op("dve", lambda e: e.tensor_tensor(out=t3[:, 0:Tt], in0=pb[:, 0:Tt], in1=ropec[:, 0:Tt], op=ALU.mult),
             R=[bb, b_rope], W=[b3])
        k.op("dve", lambda e: e.tensor_tensor(out=t1[:, 0:Tt], in0=pa[:, 0:Tt], in1=ropes[:, 0:Tt], op=ALU.mult),
             R=[ba, b_rope], W=[b1])
        k.op("dve", lambda e: e.tensor_tensor(out=out2, in0=t3[:, 0:Tt], in1=t1[:, 0:Tt], op=ALU.add),
             R=[b3, b1], W=wbufs)

    def ret_w(h, u, Tt, NSt, off, nq, pre):
        V = RV[u]
        qT, bq = V["qT"]
        kT, bk = V["kT"]
        qdT, bqd = V["qdT"]
        kdtm, bkd = V["kdtm"]
        vtm, bv = V["vtm"]
        grgs, bgr = V["grgs"]
        brgs, bbr = V["brgs"]
        rh = lambda kc: hT[:, kc, 0:Tt]
        names = ["k"] if pre else ["q", "k"]
        for nm in names:
            wt, bw = wtile([("w_in", 0, DC, (OFF_Q if nm == "q" else OFF_K) + h * 256, 256, 0)])
            pp = []
            for dh in range(2):
                ps, bps = PS()
                for kc in range(DC):
                    k.op("pe", lambda e, kc=kc, dh=dh, ps=ps, wt=wt: e.matmul(ps[:, 0:Tt], wt[:, kc, dh * 128:(dh + 1) * 128],
                                                                              rh(kc), start=(kc == 0),
                                                                              stop=(kc == DC - 1)),
                         R=[bw, b_hT[kc]], W=[bps])
                pp.append((ps, bps))
            if nm == "q":
                rope(pp[0][0], pp[0][1], pp[1][0], pp[1][1], Tt, qdT[:, 0, 0:Tt], qdT[:, 1, 0:Tt], [bqd])
                k.op("act", lambda e: e.copy(out=qT[:, :, 0:Tt], in_=qdT[:, :, 0:Tt]), R=[bqd], W=[bq])
                if nq == 1:
                    for dh in range(2):
                        k.op("dve", lambda e, dh=dh: e.tensor_tensor(
                            out=qdT[:, dh, 0:Tt].rearrange("p (s i) -> p s i", s=NSt),
                            in0=qdT[:, dh, 0:Tt].rearrange("p (s i) -> p s i", s=NSt),
                            in1=cv(off, "qdec")[:, h * 128:(h + 1) * 128].unsqueeze(1).to_broadcast([128, NSt, 128]),
                            op=ALU.mult), R=[bqd, b_consts, bq], W=[bqd])
                else:
                    k.op("dve", lambda e: e.tensor_tensor(
                        out=qdT[:, :, 0:128], in0=qdT[:, :, 0:128],
                        in1=cv(off, "qdec")[:, h * 128:(h + 1) * 128].unsqueeze(1).to_broadcast([128, 2, 128]),
                        op=ALU.mult), R=[bqd, b_consts, bq], W=[bqd])
            else:
                rope(pp[0][0], pp[0][1], pp[1][0], pp[1][1], Tt, kT[:, 0, 0:Tt], kT[:, 1, 0:Tt], [bk])
            wrel()
            yield
        vps = [PS() for s in range(NSt)]
        for half in range(2):
            wt, bw = wtile([("w_in", 0, DC, OFF_V + h * 512 + half * 256, 256, 0)])
            for s in range(NSt):
                ps, bps = vps[s]
                for kc in range(DC):
                    k.op("pe", lambda e, kc=kc, s=s, ps=ps, wt=wt, half=half: e.matmul(
                        ps[:, half * 256:(half + 1) * 256], hT[:, kc, s * 128:(s + 1) * 128], wt[:, kc, 0:256],
                        start=(kc == 0), stop=(kc == DC - 1)), R=[bw, b_hT[kc]], W=[bps])
            wrel()
        for s in range(NSt):
            ps, bps = vps[s]
            k.op("act", lambda e, s=s, ps=ps: e.copy(out=vtm[:, s, :], in_=ps[:]), R=[bps], W=[bv])
        yield
        for t0 in ((0, 2) if not pre else ()):
            for (j, ps, bps) in proj_fm("w_in", OFF_RG + h * 512 + t0 * 128, 2, rh, DC, HT):
                ec = t0 + j
                tb, btb = TB()
                silu_exp(ps[:, 0:Tt], bps, tb[:, 0:Tt], btb, Tt)
                k.op("dve", lambda e, ec=ec, tb=tb: e.tensor_scalar(out=grgs[:, ec, 0:Tt], in0=tb[:, 0:Tt],
                                                                     scalar1=gncols[:, 0, h * 4 + ec:h * 4 + ec + 1],
                                                                     scalar2=None, op0=ALU.mult),
                     R=[btb, b_setup], W=[bgr])
                k.op("dve", lambda e, ec=ec, tb=tb: e.tensor_scalar(out=brgs[:, ec, 0:Tt], in0=tb[:, 0:Tt],
                                                                     scalar1=gncols[:, 1, h * 4 + ec:h * 4 + ec + 1],
                                                                     scalar2=None, op0=ALU.mult),
                     R=[btb, b_setup], W=[bbr])
            yield
        for s in range(NSt):
            ps, bps = PS()
            psb = ps[:].bitcast(BF16)
            for dh in range(2):
                k.op("pe", lambda e, dh=dh, s=s, psb=psb: e.transpose(psb[:, dh * 128:(dh + 1) * 128],
                                                                      kT[:, dh, s * 128:(s + 1) * 128], identb[:]),
                     R=[bk, b_setup], W=[bps])
            for q in range(nq):
                k.op("dve", lambda e, s=s, q=q, psb=psb: e.tensor_scalar(
                    out=kdtm[:, s + q, :], in0=psb[:, 0:256], scalar1=cv(off, "kdecq")[:, q * 8 + h:q * 8 + h + 1],
                    scalar2=None, op0=ALU.mult), R=[bps, b_consts], W=[bkd])
        yield

    def ret_chunk(h, u, s, Tt, off, cdec, nq, pre, smp):
        V = RV[u]
        qT, bq = V["qT"]
        kT, bk = V["kT"]
        qdT, bqd = V["qdT"]
        kdtm, bkd = V["kdtm"]
        vtm, bv = V["vtm"]
        grgs, bgr = V["grgs"]
        brgs, bbr = V["brgs"]
        sl = slice(s * 128, (s + 1) * 128)
        i = (h * 2 + s) % 2
        if not pre:
            ps, bps = PS()
            for dh in range(2):
                k.op("pe", lambda e, dh=dh, ps=ps: e.matmul(ps[:, 0:128], kT[:, dh, sl], qT[:, dh, sl],
                                                            start=(dh == 0), stop=(dh == 1)), R=[bk, bq], W=[bps])
            k.op("dve", lambda e, ps=ps: e.tensor_tensor(out=attm[i][:], in0=ps[:, 0:128],
                                                         in1=cv(off, "dmatT")[:, h * 128:(h + 1) * 128], op=ALU.mult),
                 R=[bps, b_consts], W=[b_attm[i]])
            yield
        if smp:
            _ret_chunk_smp(h, u, s, off, cdec, nq, V, i)
            yield
            return
        if not pre:
            po, bpo = PS()
            k.op("pe", lambda e: e.matmul(po[:], attm[i][:], vtm[:, s, :], start=True, stop=False),
                 R=[b_attm[i], bv], W=[bpo])
            for dh in range(2):
                k.op("pe", lambda e, dh=dh: e.matmul(po[:], qdT[:, dh, sl], S_ret[:, h, dh, :], start=False,
                                                     stop=(dh == 1)), R=[bqd, b_Sret[h][dh]], W=[bpo])
        for dh in range(2):
            ps, bps = PS()
            k.op("pe", lambda e, dh=dh, ps=ps: e.matmul(ps[:], kdtm[:, s, dh * 128:(dh + 1) * 128], vtm[:, s, :],
                                                        start=True, stop=True), R=[bkd, bv], W=[bps])
            k.op("dve", lambda e, dh=dh, ps=ps: e.scalar_tensor_tensor(out=S_ret[:, h, dh, :], in0=S_ret[:, h, dh, :],
                                                                       scalar=cdec[h], in1=ps[:], op0=ALU.mult,
                                                                       op1=ALU.add),
                 R=[bps, b_Sret[h][dh]], W=[b_Sret[h][dh]])
        if not pre:
            ret_post1(h, s, po, bpo)
            yield
            ret_post2(h, s, grgs, bgr, brgs, bbr)
        yield

    def ret_post(h, s, po, bpo, grgs, bgr, brgs, bbr):
        ret_post1(h, s, po, bpo)
        ret_post2(h, s, grgs, bgr, brgs, bbr)

    def ret_post1(h, s, po, bpo):
        i = (h * 2 + s) % 2
        st, bst = stat[i], b_stat[i]
        k.op("dve", lambda e: e.bn_stats(out=st[:, 0:6], in_=po[:]), R=[bpo], W=[bst])
        k.op("dve", lambda e: e.bn_aggr(out=st[:, 6:8], in_=st[:, 0:6]), R=[bst], W=[bst])
        k.op("act", lambda e: e.activation(out=st[:, 8:9], in_=st[:, 7:8], func=AF.Ln, bias=EPS, scale=1.0),
             R=[bst], W=[bst])
        k.op("act", lambda e: e.activation(out=st[:, 8:9], in_=st[:, 8:9], func=AF.Exp, scale=-0.5),
             R=[bst], W=[bst])
        tb, btb = onh[i], b_onh[i]
        k.op("dve", lambda e: e.tensor_scalar(out=tb[:], in0=po[:], scalar1=st[:, 6:7], scalar2=st[:, 8:9],
                                              op0=ALU.subtract, op1=ALU.mult), R=[bpo, bst], W=[btb])

    def ret_post2(h, s, grgs, bgr, brgs, bbr):
        sl = slice(s * 128, (s + 1) * 128)
        i = (h * 2 + s) % 2
        tb, btb = onh[i], b_onh[i]
        pt, bpt = PS()
        ptb = pt[:].bitcast(BF16)
        for ec in range(4):
            k.op("pe", lambda e, ec=ec: e.transpose(ptb[:, ec * 128:(ec + 1) * 128], tb[:, ec * 128:(ec + 1) * 128],
                                                    identb[:]), R=[btb, b_setup], W=[bpt])
        tb2, btb2 = TB()
        k.op("dve", lambda e: e.tensor_tensor(out=tb2[:].rearrange("p (a b) -> p a b", a=4),
                                              in0=ptb[:, 0:512].rearrange("p (a b) -> p a b", a=4),
                                              in1=grgs[:, :, sl], op=ALU.mult), R=[bpt, bgr], W=[btb2])
        k.op("dve", lambda e: e.tensor_tensor(out=RR[:, h * 4:h * 4 + 4, sl],
                                               in0=tb2[:].rearrange("p (a b) -> p a b", a=4), in1=brgs[:, :, sl],
                                               op=ALU.add), R=[btb2, bbr], W=b_RR[h * 4:h * 4 + 4])

    def _ret_chunk_smp(h, u, s, off, cdec, nq, V, i):
        qT, bq = V["qT"]
        qdT, bqd = V["qdT"]
        kdtm, bkd = V["kdtm"]
        vtm, bv = V["vtm"]
        grgs, bgr = V["grgs"]
        brgs, bbr = V["brgs"]
        po, bpo = PS(pin=True)
        k.op("pe", lambda e: e.matmul(po[:], attm[i][:], vtm[:, 0, :], start=True, stop=False),
             R=[b_attm[i], bv], W=[bpo])
        for q in range(nq):
            j = q % 2
            k.dma([(s0buf[j][:], I["sret_in"][q, h].rearrange("(dh p) e -> p dh e", p=128))], W=[b_s0buf[j]])
            k.op("dve", lambda e, q=q, j=j: e.tensor_tensor(
                out=qmk[j][:], in0=qdT[:, :, 0:128],
                in1=cv(off, "colmask")[:, q * 128:(q + 1) * 128].unsqueeze(1).to_broadcast([128, 2, 128]),
                op=ALU.mult), R=[bqd, b_consts], W=[b_qmk[j]])
            for dh in range(2):
                k.op("pe", lambda e, dh=dh, j=j, q=q: e.matmul(po[:], qmk[j][:, dh, :], s0buf[j][:, dh, :], start=False,
                                                               stop=(q == nq - 1 and dh == 1)),
                     R=[b_qmk[j], b_s0buf[j]], W=[bpo])
            for dh in range(2):
                ps, bps = PS()
                k.op("pe", lambda e, dh=dh, ps=ps, q=q: e.matmul(ps[:], kdtm[:, q, dh * 128:(dh + 1) * 128],
                                                                 vtm[:, 0, :], start=True, stop=True),
                     R=[bkd, bv], W=[bps])
                k.op("dve", lambda e, dh=dh, ps=ps, j=j: e.scalar_tensor_tensor(
                    out=s1buf[j][:, dh, :], in0=s0buf[j][:, dh, :], scalar=cdec[h], in1=ps[:], op0=ALU.mult,
                    op1=ALU.add), R=[bps, b_s0buf[j]], W=[b_s1buf[j]])
            k.dma([(O["ret_smp"][q, h].rearrange("(dh p) e -> p dh e", p=128), s1buf[j][:])], R=[b_s1buf[j]],
                  is_out=True)
        ret_post(h, s, po, bpo, grgs, bgr, brgs, bbr)
        unpin(bpo)

    def ssm_w(g, u, Tt, NSt, off, nq, pre, L, nseg):
        V = SV[u]
        zs, bzs = V["zs"]
        xpad, bxp = V["xpad"]
        bcpad, bbc = V["bcpad"]
        xcT, bxc = V["xcT"]
        BT, bBT = V["BT"]
        CT, bCT = V["CT"]
        xtm, bxt = V["xtm"]
        Btm, bBt = V["Btm"]
        xdt, bxd = V["xdt"]
        rh = lambda kc: hT[:, kc, 0:Tt]
        LP = L + 3
        if not pre:
            zps = [PS() for s in range(NSt)]
            for half in range(2):
                wt, bw = wtile([("w_in", 0, DC, OFF_Z + g * 512 + half * 256, 256, 0)])
                for s in range(NSt):
                    ps, bps = zps[s]
                    for kc in range(DC):
                        k.op("pe", lambda e, kc=kc, s=s, ps=ps, wt=wt, half=half: e.matmul(
                            ps[:, half * 256:(half + 1) * 256], hT[:, kc, s * 128:(s + 1) * 128], wt[:, kc, 0:256],
                            start=(kc == 0), stop=(kc == DC - 1)), R=[bw, b_hT[kc]], W=[bps])
                wrel()
            for s in range(NSt):
                ps, bps = zps[s]
                silu_exp(ps[:], bps, zs[:, s, :], bzs, 512)
            yield

        def preconv(ps, bps, cch, padv, bpad):
            pv = padv.rearrange("p (q l) -> p q l", q=nseg)
            k.op("act", lambda e: e.copy(out=pv[:, :, 0:3], in_=conv_hist[:, cch, 0:nseg, :]),
                 R=[b_chist[cch]], W=[bpad])
            k.op("act", lambda e: e.copy(out=pv[:, :, 3:LP], in_=ps[:, 0:Tt].rearrange("p (q l) -> p q l", q=nseg)),
                 R=[bps], W=[bpad])
            dst = conv_new if nseg > 1 else conv_hist
            bdst = b_cnew if nseg > 1 else b_chist
            k.op("dve", lambda e: e.tensor_copy(dst[:, cch, 0:nseg, :],
                                                ps[:, 0:Tt].rearrange("p (q l) -> p q l", q=nseg)[:, :, L - 3:L]),
                 R=[bps, bpad], W=[bdst[cch]])

        def conv_taps(cch, padv, bpad):
            pv = padv.rearrange("p (q l) -> p q l", q=nseg)
            ta, bta = TA()
            tv = ta[:, 0:Tt].rearrange("p (q l) -> p q l", q=nseg)
            k.op("dve", lambda e: e.tensor_scalar(out=tv, in0=pv[:, :, 0:L], scalar1=cwcols[:, 0, cch:cch + 1],
                                                  scalar2=None, op0=ALU.mult), R=[bpad, b_setup], W=[bta])
            for w in range(1, 4):
                k.op("dve", lambda e, w=w: e.scalar_tensor_tensor(out=tv, in0=pv[:, :, w:w + L],
                                                                  scalar=cwcols[:, w, cch:cch + 1], in1=tv,
                                                                  op0=ALU.mult, op1=ALU.add),
                     R=[bpad, b_setup, bta], W=[bta])
            return ta, bta

        def conv_act(cch, ta, bta):
            te, bte = TA()
            k.op("act", lambda e: e.activation(out=te[:, 0:Tt], in_=ta[:, 0:Tt], func=AF.Exp, scale=-1.0,
                                               bias=cwcols[:, 5, cch:cch + 1]), R=[bta, b_setup], W=[bte])
            k.op("act", lambda e: e.activation(out=te[:, 0:Tt], in_=te[:, 0:Tt], func=AF.Ln, bias=1.0, scale=1.0),
                 R=[bte], W=[bte])
            k.op("act", lambda e: e.activation(out=te[:, 0:Tt], in_=te[:, 0:Tt], func=AF.Exp, scale=-1.0),
                 R=[bte], W=[bte])
            return te, bte

        def conv_fin(cch, ta, bta, te, bte, out_ap, bout):
            k.op("dve", lambda e: e.scalar_tensor_tensor(out=out_ap, in0=ta[:, 0:Tt], scalar=cwcols[:, 4, cch:cch + 1],
                                                         in1=te[:, 0:Tt], op0=ALU.add, op1=ALU.mult),
                 R=[bta, bte, b_setup], W=[bout])

        def conv_group(items):
            accs = []
            for (cch, padv, bpad, out_ap, bout) in items:
                accs.append(conv_taps(cch, padv, bpad) if out_ap is not None else None)
            tes = []
            for (cch, padv, bpad, out_ap, bout), acc in zip(items, accs):
                tes.append(conv_act(cch, acc[0], acc[1]) if acc is not None else None)
            for (cch, padv, bpad, out_ap, bout), acc, te in zip(items, accs, tes):
                if acc is not None:
                    conv_fin(cch, acc[0], acc[1], te[0], te[1], out_ap, bout)

        for t0 in (0, 2):
            items = []
            for (j, ps, bps) in proj_fm("w_in", OFF_X + g * 512 + t0 * 128, 2, rh, DC, HT):
                ci = t0 + j
                cch = g * 4 + ci
                preconv(ps, bps, cch, xpad[:, ci, 0:nseg * LP], bxp)
                items.append((cch, xpad[:, ci, 0:nseg * LP], bxp, xcT[:, ci, 0:Tt], bxc))
            conv_group(items)
            yield
        items = []
        for (bi, offc, cch) in [(0, OFF_B, 32 + g), (1, OFF_C, 40 + g)]:
            (_, ps, bps), = proj_fm("w_in", offc + g * 128, 1, rh, DC, HT)
            preconv(ps, bps, cch, bcpad[:, bi, 0:nseg * LP], bbc)
            if bi == 0:
                items.append((cch, bcpad[:, bi, 0:nseg * LP], bbc, BT[:, 0, 0:Tt], bBT))
            elif not pre:
                items.append((cch, bcpad[:, bi, 0:nseg * LP], bbc, CT[:, 0, 0:Tt], bCT))
        conv_group(items)
        yield
        for s in range(NSt):
            ps, bps = PS()
            psb = ps[:].bitcast(BF16)
            for ci in range(4):
                k.op("pe", lambda e, ci=ci, s=s, psb=psb: e.transpose(psb[:, ci * 128:(ci + 1) * 128],
                                                                      xcT[:, ci, s * 128:(s + 1) * 128], identb[:]),
                     R=[bxc, b_setup], W=[bps])
            k.op("act", lambda e, s=s, psb=psb: e.copy(out=xtm[:, s, :], in_=psb[:, 0:512]), R=[bps], W=[bxt])
            ps2, bps2 = PS()
            k.op("pe", lambda e, s=s, ps2=ps2: e.transpose(ps2[:, 0:128], BT[:, 0, s * 128:(s + 1) * 128], ident[:]),
                 R=[bBT, b_setup], W=[bps2])
            k.op("act", lambda e, s=s, ps2=ps2: e.copy(out=Btm[:, s, :], in_=ps2[:, 0:128]), R=[bps2], W=[bBt])
            yield
        for s in range(NSt):
            k.op("pool", lambda e, s=s: e.tensor_tensor(
                out=xdt[:, s, :].rearrange("p (h q) -> p h q", h=8), in0=xtm[:, s, :].rearrange("p (h q) -> p h q", h=8),
                in1=dtt[:, s, 0, g * 8:(g + 1) * 8].unsqueeze(2).to_broadcast([128, 8, 64]), op=ALU.mult),
                 R=[bxt, b_dtt[s]], W=[bxd])
        if not pre:
            emit_rhsR(g, 0, off)
        yield

    rhs_owner = {"o": None}

    def emit_rhsR(g, s, off):
        rhs_owner["o"] = (g, s)
        k.op("dve", lambda e: e.tensor_tensor(
            out=rhsR[0][:], in0=dtt[:, s, 1, g * 8:(g + 1) * 8].unsqueeze(2).to_broadcast([128, 8, 128]),
            in1=cv(off, "triT").unsqueeze(1).to_broadcast([128, 8, 128]), op=ALU.mult),
             R=[b_dtt[s], b_consts], W=[b_rhsR[0]])

    def ssm_chunk(g, u, s, Tt, off, nq, pre, smp):
        V = SV[u]
        zs, bzs = V["zs"]
        BT, bBT = V["BT"]
        CT, bCT = V["CT"]
        xtm, bxt = V["xtm"]
        Btm, bBt = V["Btm"]
        xdt, bxd = V["xdt"]
        sl = slice(s * 128, (s + 1) * 128)
        i = (g * 2 + s) % 2
        gs = slice(g * 8, (g + 1) * 8)
        if not pre:
            ps, bps = PS()
            k.op("pe", lambda e: e.matmul(ps[:, 0:128], BT[:, 0, sl], CT[:, 0, sl], start=True, stop=True),
                 R=[bBT, bCT], W=[bps])
            k.op("dve", lambda e: e.tensor_tensor(out=cbm[i][:], in0=ps[:, 0:128], in1=cv(off, "maskT"), op=ALU.mult),
                 R=[bps, b_consts], W=[b_cbm[i]])
            assert dry or rhs_owner["o"] == (g, s), (rhs_owner, g, s)
            for hh in range(2):
                pg, bpg = PS()
                k.op("pe", lambda e, hh=hh, pg=pg: e.matmul(
                    pg[:], cv(off, "SL"), rhsR[i][:, hh * 4:(hh + 1) * 4, :].rearrange("p a b -> p (a b)"),
                    start=True, stop=True), R=[b_rhsR[i], b_consts], W=[bpg])
                k.op("act", lambda e, hh=hh, pg=pg: e.activation(
                    out=wTt[i][:, hh * 4:(hh + 1) * 4, :].rearrange("p a b -> p (a b)"), in_=pg[:], func=AF.Exp),
                     R=[bpg], W=[b_wTt[i]])
            k.op("dve", lambda e: e.tensor_tensor(out=wTt[i][:], in0=wTt[i][:],
                                                  in1=cbm[i][:].unsqueeze(1).to_broadcast([128, 8, 128]), op=ALU.mult),
                 R=[b_wTt[i], b_cbm[i]], W=[b_wTt[i]])
            yield
            if (s + 1) * 128 < Tt:
                emit_rhsR(g, s + 1, off)
            py, bpy = PS(pin=True)
            for hd in range(8):
                k.op("pe", lambda e, hd=hd: e.matmul(py[:, hd * 64:(hd + 1) * 64], wTt[i][:, hd, :],
                                                     xdt[:, s, hd * 64:(hd + 1) * 64], start=True, stop=True),
                     R=[b_wTt[i], bxd], W=[bpy])
        if smp:
            pin, bpin = PS(pin=True)
            for q in range(nq):
                j = q % 2
                k.dma([(s1buf[j][:].rearrange("p a b -> p (a b)")[:, 0:512].rearrange("p (a b) -> p a b", a=4),
                        I["sssm_in"][q, g * 8:(g + 1) * 8].rearrange("h p n -> (h p) n").rearrange(
                            "(a r) n -> r a n", r=128))], W=[b_s1buf[j]])
                pt, bpt = PS()
                for a in range(4):
                    k.op("pe", lambda e, a=a, j=j, pt=pt: e.transpose(
                        pt[:, a * 128:(a + 1) * 128],
                        s1buf[j][:].rearrange("p a b -> p (a b)")[:, a * 128:(a + 1) * 128], ident[:]),
                         R=[b_s1buf[j], b_setup], W=[bpt])
                k.op("act", lambda e, j=j, pt=pt: e.copy(out=s0buf[j][:, 0, :], in_=pt[:]), R=[bpt], W=[b_s0buf[j]])
                k.op("dve", lambda e, j=j, q=q: e.tensor_tensor(out=qmk[j][:, 0, :], in0=CT[:, 0, 0:128],
                                                                in1=cv(off, "colmask")[:, q * 128:(q + 1) * 128],
                                                                op=ALU.mult), R=[bCT, b_consts], W=[b_qmk[j]])
                k.op("pe", lambda e, j=j, q=q: e.matmul(pin[:], qmk[j][:, 0, :], s0buf[j][:, 0, :], start=(q == 0),
                                                        stop=(q == nq - 1)), R=[b_qmk[j], b_s0buf[j]], W=[bpin])
                k.op("dve", lambda e, q=q: e.tensor_scalar(out=wendq[:, q, :], in0=dtt[:, 0, 3, :],
                                                            scalar1=cv(off, "rowmask")[:, q:q + 1], scalar2=None,
                                                            op0=ALU.mult), R=[b_dtt[0], b_consts], W=[b_wendq])
                k.op("dve", lambda e, j=j, q=q: e.tensor_tensor(
                    out=xwt[j][:].rearrange("p (h q) -> p h q", h=8), in0=xdt[:, 0, :].rearrange("p (h q) -> p h q", h=8),
                    in1=wendq[:, q, gs].unsqueeze(2).to_broadcast([128, 8, 64]), op=ALU.mult),
                     R=[bxd, b_wendq], W=[b_xwt[j]])
                pst, bpst = PS()
                k.op("pe", lambda e, j=j, pst=pst: e.matmul(pst[:], Btm[:, 0, :], xwt[j][:], start=True, stop=True),
                     R=[bBt, b_xwt[j]], W=[bpst])
                k.op("dve", lambda e, j=j, q=q: e.tensor_tensor(
                    out=s0buf[j][:, 1, :].rearrange("p (h q) -> p h q", h=8),
                    in0=s0buf[j][:, 0, :].rearrange("p (h q) -> p h q", h=8),
                    in1=decb[:, 0, q, gs].unsqueeze(2).to_broadcast([128, 8, 64]), op=ALU.mult),
                     R=[b_s0buf[j], b_decb[0]], W=[b_s0buf[j]])
                k.op("dve", lambda e, j=j, pst=pst: e.tensor_tensor(out=s0buf[j][:, 1, :], in0=s0buf[j][:, 1, :],
                                                                    in1=pst[:], op=ALU.add),
                     R=[bpst, b_s0buf[j]], W=[b_s0buf[j]])
                pt2, bpt2 = PS()
                for a in range(4):
                    k.op("pe", lambda e, a=a, j=j, pt2=pt2: e.transpose(pt2[:, a * 128:(a + 1) * 128],
                                                                        s0buf[j][:, 1, a * 128:(a + 1) * 128],
                                                                        ident[:]),
                         R=[b_s0buf[j], b_setup], W=[bpt2])
                k.op("act", lambda e, j=j, pt2=pt2: e.copy(
                    out=s1buf[j][:].rearrange("p a b -> p (a b)")[:, 512:1024], in_=pt2[:]), R=[bpt2], W=[b_s1buf[j]])
                k.dma([(O["ssm_smp"][q, g * 512:(g + 1) * 512, :].rearrange("(a r) n -> r a n", r=128),
                        s1buf[j][:].rearrange("p a b -> p (a b)")[:, 512:1024].rearrange("p (a b) -> p a b", a=4))],
                      R=[b_s1buf[j]], is_out=True)
        else:
            if not pre:
                pin, bpin = PS(pin=True)
                k.op("pe", lambda e: e.matmul(pin[:], CT[:, 0, sl], S_ssm[:, g, :], start=True, stop=True),
                     R=[bCT, b_Sssm[g]], W=[bpin])
            k.op("dve", lambda e: e.tensor_tensor(
                out=xwt[i][:].rearrange("p (h q) -> p h q", h=8), in0=xdt[:, s, :].rearrange("p (h q) -> p h q", h=8),
                in1=dtt[:, s, 3, gs].unsqueeze(2).to_broadcast([128, 8, 64]), op=ALU.mult),
                 R=[bxd, b_dtt[s]], W=[b_xwt[i]])
            pst, bpst = PS()
            k.op("pe", lambda e: e.matmul(pst[:], Btm[:, s, :], xwt[i][:], start=True, stop=True),
                 R=[bBt, b_xwt[i]], W=[bpst])
            k.op("dve", lambda e: e.tensor_tensor(
                out=S_ssm[:, g, :].rearrange("p (h q) -> p h q", h=8), in0=S_ssm[:, g, :].rearrange("p (h q) -> p h q", h=8),
                in1=decb[:, s, 0, gs].unsqueeze(2).to_broadcast([128, 8, 64]), op=ALU.mult),
                 R=[b_Sssm[g], b_decb[s]], W=[b_Sssm[g]])
            k.op("dve", lambda e: e.tensor_tensor(out=S_ssm[:, g, :], in0=S_ssm[:, g, :], in1=pst[:], op=ALU.add),
                 R=[bpst, b_Sssm[g]], W=[b_Sssm[g]])
        if pre:
            yield
            return
        ta, bta = TA()
        k.op("dve", lambda e: e.tensor_tensor(
            out=ta[:].rearrange("p (h q) -> p h q", h=8), in0=pin[:].rearrange("p (h q) -> p h q", h=8),
            in1=dtt[:, s, 2, gs].unsqueeze(2).to_broadcast([128, 8, 64]), op=ALU.mult),
             R=[bpin, b_dtt[s]], W=[bta])
        k.op("dve", lambda e: e.tensor_tensor(out=ta[:], in0=ta[:], in1=py[:], op=ALU.add), R=[bta, bpy], W=[bta])
        unpin(bpy)
        unpin(bpin)
        ta2, bta2 = TA()
        k.op("pool", lambda e: e.tensor_tensor(
            out=ta2[:].rearrange("p (h q) -> p h q", h=8), in0=xtm[:, s, :].rearrange("p (h q) -> p h q", h=8),
            in1=hv[:, 2, gs].unsqueeze(2).to_broadcast([128, 8, 64]), op=ALU.mult), R=[bxt, b_setup], W=[bta2])
        k.op("dve", lambda e: e.tensor_tensor(out=ta[:], in0=ta[:], in1=ta2[:], op=ALU.add),
             R=[bta, bta2], W=[bta])
        k.op("dve", lambda e: e.tensor_tensor(out=ta[:], in0=ta[:], in1=zs[:, s, :], op=ALU.mult),
             R=[bta, bzs], W=[bta])
        st, bst = stat[i], b_stat[i]
        k.op("dve", lambda e: e.memset(st[:, 10:11], 0.0), W=[bst])
        k.op("act", lambda e: e.activation(out=ta2[:], in_=ta[:], func=AF.Square, accum_out=st[:, 10:11]),
             R=[bta], W=[bta2, bst])
        k.op("act", lambda e: e.activation(out=st[:, 11:12], in_=st[:, 10:11], func=AF.Ln, bias=EPS,
                                           scale=1.0 / 512.0), R=[bst], W=[bst])
        k.op("act", lambda e: e.activation(out=st[:, 11:12], in_=st[:, 11:12], func=AF.Exp, scale=-0.5),
             R=[bst], W=[bst])
        tb, btb = onh[i], b_onh[i]
        k.op("dve", lambda e: e.tensor_scalar(out=tb[:], in0=ta[:], scalar1=st[:, 11:12], scalar2=None,
                                              op0=ALU.mult), R=[bta, bst], W=[btb])
        yield
        pt, bpt = PS()
        ptb = pt[:].bitcast(BF16)
        for ci in range(4):
            k.op("pe", lambda e, ci=ci: e.transpose(ptb[:, ci * 128:(ci + 1) * 128], tb[:, ci * 128:(ci + 1) * 128],
                                                    identb[:]), R=[btb, b_setup], W=[bpt])
        for ci in range(4):
            c = g * 4 + ci
            k.op("dve", lambda e, ci=ci, c=c: e.tensor_scalar(out=RR[:, c, sl], in0=ptb[:, ci * 128:(ci + 1) * 128],
                                                              scalar1=gncols[:, 2, c:c + 1], scalar2=None,
                                                              op0=ALU.mult), R=[bpt, b_setup], W=[b_RR[c]])
        yield

    def merge(which, Tt):
        wbr, offg = ("w_br_ret", OFF_GR) if which == 0 else ("w_br_ssm", OFF_GS)
        for cp in range(DC // 2):
            banks = [PS(), PS()]
            for kp in range(2):
                wt, bw = wtile([(wbr, kp * 2048, 16, cp * 256, 256, 0)])
                for j in range(2):
                    ps, bps = banks[j]
                    for kc in range(16):
                        k.op("pe", lambda e, kc=kc, j=j, ps=ps, wt=wt, kp=kp: e.matmul(
                            ps[:, 0:Tt], wt[:, kc, j * 128:(j + 1) * 128], RR[:, kp * 16 + kc, 0:Tt],
                            start=(kp == 0 and kc == 0), stop=(kp == 1 and kc == 15)),
                             R=[bw, b_RR[kp * 16 + kc]], W=[bps])
                wrel()
            gl = proj_fm("w_in", offg + cp * 256, 2, lambda kc: hT[:, kc, 0:Tt], DC, HT)
            for j in range(2):
                ps, bps = banks[j]
                (_, pg, bpg) = gl[j]
                c = cp * 2 + j
                ta, bta = TA()
                k.op("act", lambda e, pg=pg, ta=ta: e.activation(out=ta[:, 0:Tt], in_=pg[:, 0:Tt], func=AF.Sigmoid),
                     R=[bpg], W=[bta])
                if which == 0:
                    k.op("dve", lambda e, ps=ps, ta=ta, c=c: e.tensor_tensor(out=RR[:, 32 + c, 0:Tt], in0=ps[:, 0:Tt],
                                                                            in1=ta[:, 0:Tt], op=ALU.mult),
                         R=[bps, bta], W=[b_RR[32 + c]])
                else:
                    k.op("dve", lambda e, ps=ps, ta=ta: e.tensor_tensor(out=ta[:, 0:Tt], in0=ps[:, 0:Tt],
                                                                       in1=ta[:, 0:Tt], op=ALU.mult),
                         R=[bps, bta], W=[bta])
                    k.op("dve", lambda e, ta=ta, c=c: e.tensor_tensor(out=RR[:, 32 + c, 0:Tt], in0=ta[:, 0:Tt],
                                                                      in1=RR[:, 32 + c, 0:Tt], op=ALU.add),
                         R=[bta, b_RR[32 + c]], W=[b_RR[32 + c]])

    def out_proj(Tt):
        for cp in range(DC // 2):
            for (j, ps, bps) in proj_fm("w_out", cp * 256, 2, lambda kc: RR[:, 32 + kc, 0:Tt], DC, b_RR[32:48]):
                c = cp * 2 + j
                k.op("dve", lambda e, ps=ps, c=c: e.tensor_tensor(out=xT[:, c, 0:Tt], in0=ps[:, 0:Tt],
                                                                  in1=xT[:, c, 0:Tt], op=ALU.add),
                     R=[bps, b_xT[c]], W=[b_xT[c]])

    def ple(p_ap, tok0, Tt, NSt):
        norm_to_hT(3, Tt)
        for s in range(NSt):
            k.dma([(pstage[:], p_ap[tok0 + s * 128: tok0 + (s + 1) * 128, :])], W=[b_pstage])
            ps, bps = PS()
            for j in range(2):
                k.op("pe", lambda e, j=j, ps=ps: e.transpose(ps[:, j * 128:(j + 1) * 128],
                                                             pstage[:, j * 128:(j + 1) * 128], ident[:]),
                     R=[b_pstage, b_setup], W=[bps])
            k.op("act", lambda e, s=s, ps=ps: e.copy(out=pT[:, :, s * 128:(s + 1) * 128],
                                                     in_=ps[:, 0:256].rearrange("p (a b) -> p a b", a=2)),
                 R=[bps], W=[b_pT])
        for cp in range(DC // 2):
            gl = proj_fm("w_ple_gate", cp * 256, 2, lambda kc: hT[:, kc, 0:Tt], DC, HT)
            pl = proj_fm("w_ple", cp * 256, 2, lambda kc: pT[:, kc, 0:Tt], 2, [b_pT])
            for j in range(2):
                (_, pg, bpg) = gl[j]
                (_, pp, bpp) = pl[j]
                c = cp * 2 + j
                ta, bta = TA()
                k.op("act", lambda e, pg=pg, ta=ta: e.activation(out=ta[:, 0:Tt], in_=pg[:, 0:Tt], func=AF.Sigmoid),
                     R=[bpg], W=[bta])
                k.op("dve", lambda e, pp=pp, ta=ta: e.tensor_tensor(out=ta[:, 0:Tt], in0=pp[:, 0:Tt], in1=ta[:, 0:Tt],
                                                                   op=ALU.mult), R=[bpp, bta], W=[bta])
                k.op("dve", lambda e, ta=ta, c=c: e.tensor_tensor(out=xT[:, c, 0:Tt], in0=ta[:, 0:Tt],
                                                                  in1=xT[:, c, 0:Tt], op=ALU.add),
                     R=[bta, b_xT[c]], W=[b_xT[c]])

    def final_store(y_ap, tok0, Tt, NSt):
        rms_stats(Tt)
        for s in range(NSt):
            for hf in range(2):
                for cb2 in range(2):
                    cb = hf * 2 + cb2
                    ps, bps = PS()
                    for j in range(4):
                        c = cb * 4 + j
                        ta, bta = TA()
                        k.op("dve", lambda e, c=c, ta=ta, s=s: e.scalar_tensor_tensor(
                            out=ta[:, 0:128], in0=xT[:, c, s * 128:(s + 1) * 128], scalar=gcols[:, 4, c:c + 1],
                            in1=rstd[:, s * 128:(s + 1) * 128], op0=ALU.mult, op1=ALU.mult),
                             R=[b_xT[c], b_rstd, b_setup], W=[bta])
                        k.op("pe", lambda e, j=j, ta=ta, ps=ps: e.transpose(ps[:, j * 128:(j + 1) * 128], ta[:, 0:128],
                                                                            ident[:]), R=[bta, b_setup], W=[bps])
                    k.op("act", lambda e, cb2=cb2, ps=ps: e.copy(out=ystage[:, cb2 * 512:(cb2 + 1) * 512], in_=ps[:]),
                         R=[bps], W=[b_ystage])
                k.dma([(y_ap[tok0 + s * 128: tok0 + (s + 1) * 128, hf * 1024:(hf + 1) * 1024], ystage[:])],
                      R=[b_ystage], is_out=True)

    def load_rope(region, tok0, Tt):
        k.dma([(ropec[:, 0:Tt], I["cos_" + region][:, tok0:tok0 + Tt]),
               (ropes[:, 0:Tt], I["sin_" + region][:, tok0:tok0 + Tt])], W=[b_rope])

    def drain(g):
        for _ in g:
            pass

    def interleave(cg, wg):
        cdone = cg is None
        wdone = wg is None
        while not (cdone and wdone):
            if not wdone:
                try:
                    next(wg)
                except StopIteration:
                    wdone = True
            if not cdone:
                try:
                    next(cg)
                except StopIteration:
                    cdone = True

    def chain(gens):
        for g in gens:
            for _ in g:
                yield

    def mixer(Tt, NSt, off, cdec, nq, pre, smp, L, nseg):
        norm_to_hT(1, Tt)
        dt_path(Tt, NSt, off, nq, pre)
        drain(ret_w(0, 0, Tt, NSt, off, nq, pre))
        for h in range(RH):
            cg = chain([ret_chunk(h, h % 2, s, Tt, off, cdec, nq, pre, smp) for s in range(NSt)])
            if h + 1 < RH:
                wg = ret_w(h + 1, (h + 1) % 2, Tt, NSt, off, nq, pre)
            else:
                wg = ssm_w(0, 0, Tt, NSt, off, nq, pre, L, nseg)
            interleave(cg, wg)
        if not pre:
            tap("retT")
            merge(0, Tt)
        for g in range(SG):
            cg = chain([ssm_chunk(g, g % 2, s, Tt, off, nq, pre, smp) for s in range(NSt)])
            wg = ssm_w(g + 1, (g + 1) % 2, Tt, NSt, off, nq, pre, L, nseg) if g + 1 < SG else None
            interleave(cg, wg)
        if not pre:
            tap("ssmT")
            merge(1, Tt)

    def apply_flag():
        for h in range(RH):
            for d in range(2):
                k.op("dve", lambda e, h=h, d=d: e.tensor_scalar(out=S_ret[:, h, d, :], in0=S_ret[:, h, d, :],
                                                                 scalar1=flagt[:, 0:1], scalar2=None, op0=ALU.mult),
                     R=[b_Sret[h][d], b_setup], W=[b_Sret[h][d]])
        for g in range(SG):
            k.op("dve", lambda e, g=g: e.tensor_scalar(out=S_ssm[:, g, :], in0=S_ssm[:, g, :], scalar1=flagt[:, 0:1],
                                                        scalar2=None, op0=ALU.mult),
                 R=[b_Sssm[g], b_setup], W=[b_Sssm[g]])
        for c in range(48):
            k.op("dve", lambda e, c=c: e.tensor_scalar(out=conv_hist[:, c, 0, :], in0=conv_hist[:, c, 0, :],
                                                        scalar1=flagt[:, 0:1], scalar2=None, op0=ALU.mult),
                 R=[b_chist[c], b_setup], W=[b_chist[c]])

    def store_states_main():
        for h in range(RH):
            k.dma([(O["ret_main"][h].rearrange("(dh p) e -> p dh e", p=128), S_ret[:, h, :, :])],
                  R=[b_Sret[h][0], b_Sret[h][1]], is_out=True)
        for g in range(SG):
            j = g % 2
            pt, bpt = PS()
            for a in range(4):
                k.op("pe", lambda e, a=a, pt=pt, g=g: e.transpose(pt[:, a * 128:(a + 1) * 128],
                                                                  S_ssm[:, g, a * 128:(a + 1) * 128], ident[:]),
                     R=[b_Sssm[g], b_setup], W=[bpt])
            k.op("act", lambda e, j=j, pt=pt: e.copy(out=s1buf[j][:, 0, :], in_=pt[:]), R=[bpt], W=[b_s1buf[j]])
            k.dma([(O["ssm_main"][g * 512:(g + 1) * 512, :].rearrange("(a r) n -> r a n", r=128),
                    s1buf[j][:, 0, :].rearrange("p (a b) -> p a b", a=4))], R=[b_s1buf[j]], is_out=True)
        conv_out(conv_hist, b_chist, 1, O["conv_main"])

    def conv_out(src, bsrc, nseg, out_ap):
        n = nseg * 3
        for pc in range(6):
            for cb2 in range(2):
                ps, bps = PS()
                for j in range(4):
                    c = pc * 8 + cb2 * 4 + j
                    k.op("pe", lambda e, j=j, c=c, ps=ps: e.transpose(
                        ps[0:n, j * 128:(j + 1) * 128], src[:, c, 0:nseg, :].rearrange("p q t -> p (q t)"), ident[:]),
                         R=[bsrc[c], b_setup], W=[bps])
                k.op("act", lambda e, cb2=cb2, ps=ps: e.copy(out=ystage[0:n, cb2 * 512:(cb2 + 1) * 512],
                                                             in_=ps[0:n, :]), R=[bps], W=[b_ystage])
            k.dma([(out_ap[:, pc * 1024:(pc + 1) * 1024], ystage[0:n, :])], R=[b_ystage], is_out=True)

    def conv_in_smp():
        for pc in range(6):
            k.dma([(ystage[0:12, :], I["sconv_in"][:, pc * 1024:(pc + 1) * 1024])], W=[b_ystage])
            for cb2 in range(2):
                ps, bps = PS()
                for j in range(4):
                    cl = cb2 * 4 + j
                    k.op("pe", lambda e, j=j, cl=cl, ps=ps: e.transpose(
                        ps[:, j * 12:(j + 1) * 12], ystage[0:12, cl * 128:(cl + 1) * 128], ident[0:12, 0:12]),
                         R=[b_ystage, b_setup], W=[bps])
                for j in range(4):
                    c = pc * 8 + cb2 * 4 + j
                    k.op("act", lambda e, j=j, c=c, ps=ps: e.copy(
                        out=conv_hist[:, c, :, :].rearrange("p q t -> p (q t)"), in_=ps[:, j * 12:(j + 1) * 12]),
                         R=[bps], W=[b_chist[c]])

    tapped = set()

    def tap(name):
        if name not in DEBUG_TAPS or name in tapped:
            return
        tapped.add(name)
        if name in ("retT", "ssmT", "merged", "hidden"):
            k.dma([(O["dbg_" + name], RR[:, :, :])], R=b_RR, q="pool", is_out=True)
        else:
            k.dma([(O["dbg_" + name], xT[:, :, :])], R=b_xT, q="pool", is_out=True)

    cast_all()
    setup()
    load_consts("P")
    for t in range(NPRE):
        load_x(I["x_pre"], t * T, T, NS)
        load_rope("pre", t * T, T)
        ffn("w1_gu", "w1_down", 0, T)
        mixer(T, NS, offP, cdecP, 1, True, False, T, 1)
    apply_flag()
    for t in range(NMAIN):
        load_x(I["x_main"], t * T, T, NS)
        load_rope("main", t * T, T)
        ffn("w1_gu", "w1_down", 0, T)
        tap("x1")
        mixer(T, NS, offP, cdecP, 1, False, False, T, 1)
        tap("merged")
        out_proj(T)
        tap("x2")
        ffn("w2_gu", "w2_down", 2, T)
        tap("x3")
        ple(I["p_main"], t * T, T, NS)
        tap("x4")
        final_store(O["y_main"], t * T, T, NS)
    store_states_main()
    if with_smp:
        load_consts("S")
        conv_in_smp()
        load_x(I["x_smp"], 0, 128, 1)
        load_rope("smp", 0, 128)
        ffn("w1_gu", "w1_down", 0, 128)
        mixer(128, 1, offS, cdecS, 4, False, True, 32, 4)
        out_proj(128)
        ffn("w2_gu", "w2_down", 2, 128)
        ple(I["p_smp"], 0, 128, 1)
        final_store(O["y_smp"], 0, 128, 1)
        conv_out(conv_new, b_cnew, 4, O["conv_smp"])


_NC_CACHE = {}
LAST_RESULTS = None


def run(inputs, NPRE, NMAIN, T=256, with_smp=True, ncores=8):
    f32 = np.float32
    xp = np.asarray(inputs["x_prompt"], f32)
    seq = xp.shape[1]
    half = seq // 2
    assert half == NMAIN * T and (NPRE == NMAIN)
    key = (NPRE, NMAIN, T, with_smp)
    if key not in _NC_CACHE:
        import time as _t
        _t0 = _t.time()
        _NC_CACHE[key] = build(NPRE, NMAIN, T, with_smp)
        print("[kernel] build %.1fs" % (_t.time() - _t0), flush=True)
    nc = _NC_CACHE[key]
    constP = make_consts("P")[0]
    constS = make_consts("S")[0]
    cP = np.zeros((128, CONST_W), f32)
    cP[:, :constP.shape[1]] = constP
    cS = np.zeros((128, CONST_W), f32)
    cS[:, :constS.shape[1]] = constS
    ident = np.eye(128, dtype=f32)
    cos_a, sin_a = rope_tables(np.arange(0, half))
    cos_b, sin_b = rope_tables(np.arange(half, seq))
    cos_s, sin_s = rope_tables(np.tile(PAST + np.arange(32), 4))
    shared = {}
    for n in ("w1_gu", "w1_down", "w_in", "w_br_ret", "w_br_ssm", "w_out", "w2_gu", "w2_down", "w_ple", "w_ple_gate"):
        shared[n] = np.ascontiguousarray(np.asarray(inputs[n], f32)[0])
    def colv(v, nchunk):
        return np.ascontiguousarray(np.asarray(v, f32).reshape(nchunk, 128).T)
    gc = np.stack([colv(inputs[n][0], DC) for n in ("g_ffn1", "g_mix", "g_ffn2", "g_ple")] +
                  [colv(inputs["g_final"], DC)], axis=1)
    shared["gcols"] = np.ascontiguousarray(gc)
    shared["gncols"] = np.ascontiguousarray(np.stack([colv(inputs[n][0], 32) for n in
                                                      ("ret_gn_g", "ret_gn_b", "ssm_norm_g")], axis=1))
    cwv = np.asarray(inputs["conv_w"], f32)[0]
    shared["cwcols"] = np.ascontiguousarray(np.stack([colv(cwv[w], 48) for w in range(4)] +
                                                     [colv(inputs["conv_b"][0], 48)], axis=1))
    shared["hvin"] = np.ascontiguousarray(np.broadcast_to(
        np.stack([np.asarray(inputs[n], f32)[0] for n in ("dt_bias", "a_log", "d_skip")], axis=0)[None], (128, 3, 64)))
    shared["constP"] = cP
    shared["constS"] = cS
    shared["ident"] = ident
    in_maps = []
    for c in range(ncores):
        b, hf = c // 2, c % 2
        m = dict(shared)
        m["x_pre"] = np.ascontiguousarray(xp[b, 0:half])
        m["x_main"] = np.ascontiguousarray(xp[b, hf * half:(hf + 1) * half])
        m["p_main"] = np.ascontiguousarray(np.asarray(inputs["p_prompt"], f32)[0, b, hf * half:(hf + 1) * half])
        m["x_smp"] = np.ascontiguousarray(np.asarray(inputs["x_sample"], f32)[4 * c:4 * c + 4].reshape(128, D))
        m["p_smp"] = np.ascontiguousarray(np.asarray(inputs["p_sample"], f32)[0, 4 * c:4 * c + 4].reshape(128, 256))
        m["sret_in"] = np.ascontiguousarray(np.asarray(inputs["state_ret"], f32)[0, 4 * c:4 * c + 4])
        m["sssm_in"] = np.ascontiguousarray(np.asarray(inputs["state_ssm"], f32)[0, 4 * c:4 * c + 4])
        m["sconv_in"] = np.ascontiguousarray(np.asarray(inputs["state_conv"], f32)[0, 4 * c:4 * c + 4].reshape(12, 6144))
        m["flag"] = np.full((128, 1), float(hf), f32)
        m["cos_pre"], m["sin_pre"] = cos_a, sin_a
        m["cos_main"], m["sin_main"] = (cos_a, sin_a) if hf == 0 else (cos_b, sin_b)
        m["cos_smp"], m["sin_smp"] = cos_s, sin_s
        in_maps.append(m)
    import time as _t
    _t0 = _t.time()
    res = run_bass_kernel_spmd(nc, in_maps, core_ids=list(range(ncores)))
    print("[kernel] spmd launch wall %.1fs" % (_t.time() - _t0), flush=True)
    R = res.results
    global LAST_RESULTS
    LAST_RESULTS = R
    nb = xp.shape[0]
    y_prompt = np.zeros((nb, seq, D), f32)
    ret_p = np.zeros((1, nb, RH, 256, 512), f32)
    ssm_p = np.zeros((1, nb, 64, 64, 128), f32)
    conv_p = np.zeros((1, nb, 3, 6144), f32)
    nsb = np.asarray(inputs["x_sample"]).shape[0]
    y_sample = np.zeros((nsb, 32, D), f32)
    ret_s = np.zeros((1, nsb, RH, 256, 512), f32)
    ssm_s = np.zeros((1, nsb, 64, 64, 128), f32)
    conv_s = np.zeros((1, nsb, 3, 6144), f32)
    for c in range(ncores):
        b, hf = c // 2, c % 2
        y_prompt[b, hf * half:(hf + 1) * half] = R[c]["y_main"]
        if hf == 1:
            ret_p[0, b] = R[c]["ret_main"]
            ssm_p[0, b] = R[c]["ssm_main"].reshape(64, 64, 128)
            conv_p[0, b] = R[c]["conv_main"]
        if with_smp:
            y_sample[4 * c:4 * c + 4] = R[c]["y_smp"].reshape(4, 32, D)
            ret_s[0, 4 * c:4 * c + 4] = R[c]["ret_smp"]
            ssm_s[0, 4 * c:4 * c + 4] = R[c]["ssm_smp"].reshape(4, 64, 64, 128)
            conv_s[0, 4 * c:4 * c + 4] = R[c]["conv_smp"].reshape(4, 3, 6144)
    return (y_prompt, y_sample, ret_p, ssm_p, conv_p, ret_s, ssm_s, conv_s)


def kernel(**inputs):
    return run(inputs, 16, 16, 256, True, 8)
```

```python
import numpy as np
from contextlib import ExitStack
import concourse.bass as bass
import concourse.mybir as mybir
from concourse.bass_utils import run_bass_kernel_spmd

F32 = mybir.dt.float32
BF16 = mybir.dt.bfloat16
AF = mybir.ActivationFunctionType
ALU = mybir.AluOpType

D = 2048
DC = 16
FF = 5632
FC = 44
NIN = 26688
RH = 8
SG = 8
EPS = 1e-6
PAST = 4096
OFF_Q, OFF_K, OFF_V, OFF_RG, OFF_Z, OFF_X, OFF_B, OFF_C, OFF_DT, OFF_GR, OFF_GS = (
    0, 2048, 4096, 8192, 12288, 16384, 20480, 21504, 22528, 22592, 24640)
SAME_SYNC = True
WSLOTS = 3
WCOLS = 256


class Buf:
    __slots__ = ("name", "w", "r", "region", "lo", "hi", "dkey", "dcnt")

    def __init__(self, name, region=None, lo=0, hi=1):
        self.name = name
        self.w = None
        self.r = {}
        self.region = region
        self.lo = lo
        self.hi = hi
        self.dkey = None
        self.dcnt = 0
        if region is not None:
            region.append(self)

    def overl(self):
        if self.region is None:
            return (self,)
        return [b for b in self.region if b.lo < self.hi and self.lo < b.hi]


class K:
    def __init__(self, nc, es, dry):
        self.nc = nc
        self.es = es
        self.dry = dry
        self.eng = {"pe": nc.tensor, "act": nc.scalar, "dve": nc.vector, "pool": nc.gpsimd, "sp": nc.sync}
        self.sems = {}
        self.cnt = {}
        self.waited = {e: {} for e in self.eng}
        self.nsem = 0
        self.wreq = []
        self.wpos = 0
        self.wissued = 0
        self.out_events = []
        if not dry:
            for e in ("pe", "act", "dve", "pool"):
                self.sems[e] = es.enter_context(nc.semaphore("s_" + e))
                self.cnt[e] = 0
                self.nsem += 1

    def _wait(self, eng, key, val):
        if key == eng and (eng == "pe" or not SAME_SYNC):
            return
        wd = self.waited[eng]
        if wd.get(key, 0) >= val:
            return
        self.eng[eng].wait_ge(self.sems[key], val)
        wd[key] = val

    def _deps(self, eng, R, W):
        need = {}

        def add(ev):
            if ev is not None and need.get(ev[0], 0) < ev[1]:
                need[ev[0]] = ev[1]
        for b in R:
            for o in b.overl():
                add(o.w)
        for b in W:
            for o in b.overl():
                add(o.w)
                for k_, v_ in o.r.items():
                    add((k_, v_))
        for k_, v_ in need.items():
            self._wait(eng, k_, v_)

    def _record(self, ev, R, W):
        for b in R:
            for o in b.overl():
                if o.r.get(ev[0], 0) < ev[1]:
                    o.r[ev[0]] = ev[1]
        for b in W:
            for o in b.overl():
                o.w = ev
                o.r = {}

    def op(self, eng, fn, R=(), W=()):
        if self.dry:
            return
        self._deps(eng, R, W)
        ins = fn(self.eng[eng])
        self.cnt[eng] += 1
        ins.then_inc(self.sems[eng], 1)
        self._record((eng, self.cnt[eng]), R, W)

    def _dsem(self, b):
        if b.dkey is None:
            b.dkey = "d_" + b.name
            self.sems[b.dkey] = self.es.enter_context(self.nc.semaphore(b.dkey))
            self.nsem += 1
        return b.dkey

    def dma(self, pieces, R=(), W=(), q="sp", is_out=False):
        if self.dry:
            return
        self._deps(q, R, W)
        b = W[0] if W else R[0]
        key = self._dsem(b)
        for (o, i) in pieces:
            self.eng[q].dma_start(out=o, in_=i).then_inc(self.sems[key], 16)
            b.dcnt += 16
        ev = (key, b.dcnt)
        self._record(ev, R, W)
        if is_out:
            self.out_events.append(ev)

    def finish(self):
        if self.dry:
            return
        for ev in self.out_events:
            self._wait("sp", ev[0], ev[1])


def _gammas():
    return 1.0 - np.exp2(-5.0 - np.arange(RH, dtype=np.float64))


def make_consts(kind):
    if kind == "P":
        nq, blk, pos, clen = 1, np.zeros(128, int), np.arange(128), 128
    else:
        nq, blk, pos, clen = 4, np.arange(128) // 32, np.arange(128) % 32, 32
    g = _gammas()
    same = blk[:, None] == blk[None, :]
    ar = np.arange(128)
    triT = ((ar[:, None] <= ar[None, :]) & same).astype(np.float64)
    SL = ((ar[:, None] > ar[None, :]) & same).astype(np.float64)
    maskT = ((ar[None, :] >= ar[:, None]) & same).astype(np.float64)
    dif = np.maximum(ar[None, :] - ar[:, None], 0)
    dmatT = np.stack([np.power(g[h], dif) * maskT * (256.0 ** -0.5) for h in range(RH)], axis=1)
    qdec = np.stack([np.power(g[h], pos + 1.0) for h in range(RH)], axis=0)
    qdec_bc = np.broadcast_to(qdec[None], (128, RH, 128))
    kdec = np.stack([np.power(g[h], clen - 1.0 - pos) * (256.0 ** -0.5) for h in range(RH)], axis=1)
    kdecq = np.stack([kdec * (blk == q)[:, None] for q in range(nq)], axis=1)
    colmask = np.stack([np.broadcast_to((blk == q)[None, :], (128, 128)) for q in range(nq)], axis=1)
    rowmask = np.stack([(blk == q) for q in range(nq)], axis=1).astype(np.float64)
    seqones = np.stack([np.broadcast_to((blk == q)[:, None], (128, 128)) for q in range(nq)], axis=1)
    parts = [("triT", triT), ("SL", SL), ("maskT", maskT), ("dmatT", dmatT.reshape(128, -1)),
             ("qdec", qdec_bc.reshape(128, -1)), ("kdecq", kdecq.reshape(128, -1)),
             ("colmask", colmask.reshape(128, -1)), ("rowmask", rowmask), ("seqones", seqones.reshape(128, -1))]
    offs = {}
    o = 0
    for n, a in parts:
        offs[n] = (o, a.shape[1])
        o += a.shape[1]
    arr = np.concatenate([np.asarray(a, np.float64) for _, a in parts], axis=1).astype(np.float32)
    cdec = [float(np.power(g[h], float(clen))) for h in range(RH)]
    return arr, offs, cdec, nq


CONST_W = 3500


def rope_tables(pos):
    half = 128
    inv = (10000.0 ** (-np.arange(half, dtype=np.float32) / np.float32(half))).astype(np.float32)
    ang = pos.astype(np.float32)[None, :] * inv[:, None]
    return np.cos(ang).astype(np.float32), np.sin(ang).astype(np.float32)


DEBUG_TAPS = []
_REPORT = False


def build(NPRE, NMAIN, T=256, with_smp=True):
    NS = T // 128
    nc = bass.Bass("TRN2", target_bir_lowering=False)
    npre_tok, nmain_tok = NPRE * T, NMAIN * T

    def din(name, shape, dt=F32):
        return nc.dram_tensor(name, list(shape), dt, kind="ExternalInput").ap()

    def dout(name, shape, dt=F32):
        return nc.dram_tensor(name, list(shape), dt, kind="ExternalOutput").ap()

    def dint(name, shape, dt=BF16):
        return nc.dram_tensor(name, list(shape), dt, kind="Internal").ap()

    I = {}
    I["x_pre"] = din("x_pre", [max(npre_tok, 1), D])
    I["x_main"] = din("x_main", [nmain_tok, D])
    I["p_main"] = din("p_main", [nmain_tok, 256])
    I["x_smp"] = din("x_smp", [128, D])
    I["p_smp"] = din("p_smp", [128, 256])
    I["sret_in"] = din("sret_in", [4, RH, 256, 512])
    I["sssm_in"] = din("sssm_in", [4, 64, 64, 128])
    I["sconv_in"] = din("sconv_in", [12, 6144])
    I["flag"] = din("flag", [128, 1])
    for r, n in (("pre", max(npre_tok, 1)), ("main", nmain_tok), ("smp", 128)):
        I["cos_" + r] = din("cos_" + r, [128, n])
        I["sin_" + r] = din("sin_" + r, [128, n])
    I["constP"] = din("constP", [128, CONST_W])
    I["constS"] = din("constS", [128, CONST_W])
    I["ident"] = din("ident", [128, 128])
    wshapes = {"w1_gu": (D, 2 * FF), "w1_down": (FF, D), "w_in": (D, NIN), "w_br_ret": (4096, D),
               "w_br_ssm": (4096, D), "w_out": (D, D), "w2_gu": (D, 2 * FF), "w2_down": (FF, D),
               "w_ple": (256, D), "w_ple_gate": (D, D)}
    WF = {n: din(n, s) for n, s in wshapes.items()}
    WB = {n: dint(n + "_bf", s) for n, s in wshapes.items()}
    I["gcols"] = din("gcols", [128, 5, DC])
    I["gncols"] = din("gncols", [128, 3, 32])
    I["cwcols"] = din("cwcols", [128, 5, 48])
    I["hvin"] = din("hvin", [128, 3, 64])
    O = {}
    O["y_main"] = dout("y_main", [nmain_tok, D])
    O["y_smp"] = dout("y_smp", [128, D])
    O["ret_main"] = dout("ret_main", [RH, 256, 512])
    O["ssm_main"] = dout("ssm_main", [64 * 64, 128])
    O["conv_main"] = dout("conv_main", [3, 6144])
    O["ret_smp"] = dout("ret_smp", [4, RH, 256, 512])
    O["ssm_smp"] = dout("ssm_smp", [4, 64 * 64, 128])
    O["conv_smp"] = dout("conv_smp", [12, 6144])

    for tn in DEBUG_TAPS:
        rr = tn in ("retT", "ssmT", "merged", "hidden")
        O["dbg_" + tn] = dout("dbg_" + tn, [128, 48 if rr else DC, T], BF16 if rr else F32)
    constP, offP, cdecP, _ = make_consts("P")
    constS, offS, cdecS, _ = make_consts("S")

    wreq_holder = []
    for dry in (True, False):
        es = ExitStack()
        k = K(nc, es, dry)
        if not dry:
            k.wreq = wreq_holder[0]
        _emit(nc, es, k, I, O, WF, WB, wshapes, NPRE, NMAIN, T, NS, with_smp, offP, cdecP, offS, cdecS)
        if dry:
            wreq_holder.append(k.wreq)
            es.close()
        else:
            assert k.wpos == len(k.wreq), (k.wpos, len(k.wreq))
            k.finish()
            es.close()
    return nc


def _emit(nc, es, k, I, O, WF, WB, wshapes, NPRE, NMAIN, T, NS, with_smp, offP, cdecP, offS, cdecS):
    dry = k.dry
    TP = T + 3

    class Dummy:
        def __getitem__(self, _):
            return self

        def __getattr__(self, _):
            return lambda *a, **kw: self

    def sb(name, shape, dt=F32):
        if dry:
            return Dummy()
        return es.enter_context(nc.sbuf_tensor("sb_" + name, list(shape), dt))

    def B(name):
        return Buf(name)

    xT = sb("xT", [128, DC, T])
    b_xT = [B("xT%d" % c) for c in range(DC)]
    hT = sb("hT", [128, DC, T], BF16)
    b_hT = [B("hT%d" % c) for c in range(DC)]
    HT = "HT"
    wring = sb("wring", [128, WSLOTS, 16 * WCOLS], BF16)
    b_w = [B("w%d" % i) for i in range(WSLOTS)]
    S_ret = sb("S_ret", [128, RH, 2, 512])
    Sreg = []
    b_Sret = [[Buf("Sret%d_%d" % (h, d), Sreg, (h * 2 + d) * 2048, (h * 2 + d + 1) * 2048) for d in range(2)]
              for h in range(RH)]
    S_ssm = sb("S_ssm", [128, SG, 512])
    b_Sssm = [B("Sssm%d" % g) for g in range(SG)]
    RR = sb("RR", [128, 48, T], BF16)
    b_RR = [B("RR%d" % j) for j in range(48)]
    conv_hist = sb("conv_hist", [128, 48, 4, 3])
    b_chist = [B("chist%d" % c) for c in range(48)]
    if dry:
        conv_new = Dummy()
    else:
        conv_new = S_ret[:, 5, :, :].rearrange("p a b -> p (a b)")[:, 0:576].rearrange("p (c q t) -> p c q t", c=48, q=4)
    b_cnew = [Buf("cnew%d" % c, Sreg, 5 * 4096 + c * 48, 5 * 4096 + (c + 1) * 48) for c in range(48)]
    USZ = 6144
    U = [sb("U%d" % u, [128, USZ], BF16) for u in range(2)]
    Ureg = [[], []]

    def uview(u, name, lo_b, shape, dt):
        n = int(np.prod(shape))
        nb = n * (4 if dt == F32 else 2)
        buf = Buf("U%d_%s" % (u, name), Ureg[u], lo_b, lo_b + nb)
        if dry:
            return Dummy(), buf
        ap = U[u][:, lo_b // 2:(lo_b + nb) // 2]
        if dt == F32:
            ap = ap.bitcast(F32)
        if len(shape) == 2:
            ap = ap.rearrange("p (a b) -> p a b", a=shape[0])
        elif len(shape) == 3:
            ap = ap.rearrange("p (a b c) -> p a b c", a=shape[0], b=shape[1])
        return ap, buf

    RV = []
    SV = []
    for u in range(2):
        o = 0
        d = {}
        for name, shape, dt in (("qT", (2, T), BF16), ("kT", (2, T), BF16), ("qdT", (2, T), F32),
                                ("kdtm", (4, 256), BF16), ("vtm", (NS, 512), BF16),
                                ("grgs", (4, T), BF16), ("brgs", (4, T), BF16)):
            d[name] = uview(u, "r_" + name, o, shape, dt)
            o += int(np.prod(shape)) * (4 if dt == F32 else 2)
        assert o <= USZ * 2, o
        RV.append(d)
        d = {}
        lay = (("zs", (NS, 512), BF16, 0), ("xpad", (4, TP + 9), BF16, 2048), ("xtm", (NS, 512), BF16, 2048),
               ("bcpad", (2, TP + 9), BF16, 4224), ("Btm", (NS, 128), BF16, 4224),
               ("xcT", (4, T), BF16, 5312), ("xdt", (NS, 512), BF16, 5312),
               ("BT", (1, T), F32, 7360), ("CT", (1, T), F32, 8384))
        for name, shape, dt, o in lay:
            d[name] = uview(u, "s_" + name, o, shape, dt)
            assert o + int(np.prod(shape)) * (4 if dt == F32 else 2) <= USZ * 2
        SV.append(d)

    xstage = sb("xstage", [128, 1024])
    b_xstage = B("xstage")
    ystage = sb("ystage", [128, 1024])
    b_ystage = B("ystage")
    pstage = sb("pstage", [128, 256])
    b_pstage = B("pstage")
    pT = sb("pT", [128, 2, T], BF16)
    b_pT = B("pT")
    ropec = sb("ropec", [128, T])
    ropes = sb("ropes", [128, T])
    b_rope = B("rope")
    consts = sb("consts", [128, CONST_W])
    b_consts = B("consts")
    ident = sb("ident_sb", [128, 128])
    identb = sb("identb", [128, 128], BF16)
    onesf = sb("onesf", [128, 128])
    onesb = sb("onesb", [128, 128], BF16)
    b_setup = B("setup")
    gcols = sb("gcols", [128, 5, DC])
    gncols = sb("gncols", [128, 3, 32])
    cwcols = sb("cwcols", [128, 6, 48])
    hv = sb("hv", [128, 4, 64])
    flagt = sb("flagt", [128, 1])
    sqt = [sb("sqt%d" % i, [128, T], BF16) for i in range(2)]
    b_sqt = [B("sqt%d" % i) for i in range(2)]
    rstd = sb("rstd", [128, T])
    b_rstd = B("rstd")
    tmpA = [sb("tmpA%d" % i, [128, 512]) for i in range(4)]
    b_tmpA = [B("tmpA%d" % i) for i in range(4)]
    tmpB = [sb("tmpB%d" % i, [128, 512], BF16) for i in range(3)]
    b_tmpB = [B("tmpB%d" % i) for i in range(3)]
    dtt = sb("dtt", [128, NS, 6, 64])
    b_dtt = [B("dtt%d" % s) for s in range(NS)]
    decb = sb("decb", [128, NS, 4, 64])
    b_decb = [B("decb%d" % s) for s in range(NS)]
    wendq = sb("wendq", [128, 4, 64])
    b_wendq = B("wendq")
    attm = [sb("attm%d" % i, [128, 128], BF16) for i in range(2)]
    b_attm = [B("attm%d" % i) for i in range(2)]
    cbm = [sb("cbm%d" % i, [128, 128], BF16) for i in range(2)]
    b_cbm = [B("cbm%d" % i) for i in range(2)]
    rhsR_ = sb("rhsR", [128, 8, 128])
    rhsR = [rhsR_, rhsR_]
    b_rhsR_ = B("rhsR")
    b_rhsR = [b_rhsR_, b_rhsR_]
    wTt_ = sb("wTt", [128, 8, 128], BF16)
    wTt = [wTt_, wTt_]
    b_wTt_ = B("wTt")
    b_wTt = [b_wTt_, b_wTt_]
    xwt = [sb("xwt%d" % i, [128, 512], BF16) for i in range(2)]
    b_xwt = [B("xwt%d" % i) for i in range(2)]
    onh = [sb("onh%d" % i, [128, 512], BF16) for i in range(2)]
    b_onh = [B("onh%d" % i) for i in range(2)]
    stat = [sb("stat%d" % i, [128, 16]) for i in range(2)]
    b_stat = [B("stat%d" % i) for i in range(2)]
    if dry:
        s0buf = [Dummy(), Dummy()]
        s1buf = [Dummy(), Dummy()]
        qmk = [Dummy(), Dummy()]
    else:
        s0buf = [S_ret[:, j, :, :] for j in range(2)]
        s1buf = [S_ret[:, 2 + j, :, :] for j in range(2)]
        qmk = [S_ret[:, 4, j, 0:256].rearrange("p (a b) -> p a b", a=2) for j in range(2)]
    b_s0buf = [Buf("s0buf%d" % j, Sreg, j * 4096, (j + 1) * 4096) for j in range(2)]
    b_s1buf = [Buf("s1buf%d" % j, Sreg, (2 + j) * 4096, (3 + j) * 4096) for j in range(2)]
    b_qmk = [Buf("qmk%d" % j, Sreg, 4 * 4096 + j * 2048, 4 * 4096 + j * 2048 + 1024) for j in range(2)]

    psum = []
    b_ps = []
    for i in range(8):
        if dry:
            psum.append(Dummy())
        else:
            psum.append(es.enter_context(nc.psum_tensor("ps%d" % i, [128, 512], F32)))
        b_ps.append(B("ps%d" % i))
    pstate = {"i": 0, "a": 0, "b": 0, "sq": 0}
    if not dry and _REPORT:
        print("[kernel] sbuf bytes remaining per partition:", nc.sbuf_bytes_remaining)

    pinned = set()

    def PS(pin=False):
        i = pstate["i"]
        while i in pinned:
            i = (i + 1) % 8
        pstate["i"] = (i + 1) % 8
        if pin:
            pinned.add(i)
        return psum[i], b_ps[i]

    def unpin(bps):
        pinned.discard(b_ps.index(bps))

    def TA():
        i = pstate["a"]
        pstate["a"] = (i + 1) % 4
        return tmpA[i], b_tmpA[i]

    def TB():
        i = pstate["b"]
        pstate["b"] = (i + 1) % 3
        return tmpB[i], b_tmpB[i]

    b_cast = {n: B("cast_" + n) for n in WB}

    def cast_all():
        order = ["w1_gu", "w1_down", "w_in", "w_br_ret", "w_br_ssm", "w_out", "w2_gu", "w2_down", "w_ple",
                 "w_ple_gate"]
        for n in order:
            rows, cols = wshapes[n]
            pieces = []
            rb = 128
            for r0 in range(0, rows, rb):
                pieces.append((WB[n][r0:r0 + rb, :], WF[n][r0:r0 + rb, :]))
            k.dma(pieces, W=[b_cast[n]], q="pool")

    def _issue_next():
        if k.dry or k.wissued >= len(k.wreq):
            return
        i = k.wissued
        slot = i % WSLOTS
        pieces = []
        names = set()
        for pc in k.wreq[i]:
            (n, r0, nkc, c0, ncols, dcol) = pc[:6]
            wd = pc[6] if len(pc) > 6 else WCOLS
            src = WB[n][r0:r0 + nkc * 128, c0:c0 + ncols].rearrange("(kc p) n -> p kc n", p=128)
            dst = wring[:, slot, :].rearrange("p (kc n) -> p kc n", n=wd)[:, 0:nkc, dcol:dcol + ncols]
            pieces.append((dst, src))
            names.add(n)
        k.dma(pieces, R=[b_cast[n] for n in names], W=[b_w[slot]])
        k.wissued += 1

    def wtile(pieces):
        if k.dry:
            k.wreq.append(tuple(pieces))
            return Dummy(), b_w[0]
        assert tuple(pieces) == k.wreq[k.wpos], (pieces, k.wreq[k.wpos])
        while k.wissued < min(k.wpos + WSLOTS, len(k.wreq)):
            _issue_next()
        slot = k.wpos % WSLOTS
        k.wpos += 1
        wd = pieces[0][6] if len(pieces[0]) > 6 else WCOLS
        return wring[:, slot, :].rearrange("p (kc n) -> p kc n", n=wd), b_w[slot]

    def wrel():
        if k.dry:
            return
        while k.wissued < min(k.wpos + WSLOTS, len(k.wreq)):
            _issue_next()

    def setup():
        k.dma([(ident[:], I["ident"])], W=[b_setup])
        k.op("dve", lambda e: e.tensor_copy(identb[:], ident[:]), R=[b_setup], W=[b_setup])
        k.op("pool", lambda e: e.memset(onesf[:], 1.0), W=[b_setup])
        k.op("pool", lambda e: e.memset(onesb[:], 1.0), W=[b_setup])
        k.dma([(gcols[:], I["gcols"]), (gncols[:], I["gncols"]), (cwcols[:, 0:5, :], I["cwcols"]),
               (hv[:, 0:3, :], I["hvin"]), (flagt[:], I["flag"])], W=[b_setup], q="sp")
        k.op("dve", lambda e: e.tensor_scalar(out=cwcols[:, 5, :], in0=cwcols[:, 4, :], scalar1=-1.0, scalar2=None,
                                              op0=ALU.mult), R=[b_setup], W=[b_setup])
        k.op("act", lambda e: e.activation(out=hv[:, 3, :], in_=hv[:, 1, :], func=AF.Exp), R=[b_setup], W=[b_setup])
        k.op("dve", lambda e: e.tensor_scalar(out=hv[:, 1, :], in0=hv[:, 3, :], scalar1=-1.0, scalar2=None,
                                              op0=ALU.mult), R=[b_setup], W=[b_setup])
        for h in range(RH):
            for d in range(2):
                k.op("pool", lambda e, h=h, d=d: e.memset(S_ret[:, h, d, :], 0.0), W=[b_Sret[h][d]])
        for g in range(SG):
            k.op("pool", lambda e, g=g: e.memset(S_ssm[:, g, :], 0.0), W=[b_Sssm[g]])
        for c in range(48):
            k.op("pool", lambda e, c=c: e.memset(conv_hist[:, c, :, :], 0.0), W=[b_chist[c]])

    def load_consts(which):
        k.dma([(consts[:], I["constP" if which == "P" else "constS"])], W=[b_consts])

    def cv(off, name, n=None):
        o, w = off[name]
        return consts[:, o:o + w]

    def load_x(src_ap, tok0, Tt, NSt):
        for s in range(NSt):
            for hf in range(2):
                k.dma([(xstage[:], src_ap[tok0 + s * 128: tok0 + (s + 1) * 128, hf * 1024:(hf + 1) * 1024])],
                      W=[b_xstage])
                for cb2 in range(2):
                    cb = hf * 2 + cb2
                    ps, bps = PS()
                    for j in range(4):
                        cl = cb2 * 4 + j
                        k.op("pe", lambda e, cl=cl, j=j, ps=ps: e.transpose(ps[:, j * 128:(j + 1) * 128],
                                                                            xstage[:, cl * 128:(cl + 1) * 128], ident[:]),
                             R=[b_xstage, b_setup], W=[bps])
                    if cb % 2 == 0:
                        k.op("act", lambda e, cb=cb, ps=ps, s=s: e.copy(
                            out=xT[:, cb * 4:cb * 4 + 4, s * 128:(s + 1) * 128],
                            in_=ps[:].rearrange("p (a b) -> p a b", a=4)), R=[bps], W=b_xT[cb * 4:cb * 4 + 4])
                    else:
                        k.op("dve", lambda e, cb=cb, ps=ps, s=s: e.tensor_copy(
                            xT[:, cb * 4:cb * 4 + 4, s * 128:(s + 1) * 128],
                            ps[:].rearrange("p (a b) -> p a b", a=4)), R=[bps], W=b_xT[cb * 4:cb * 4 + 4])

    def rms_stats(Tt):
        ps, bps = PS()
        for c in range(DC):
            i = pstate["sq"]
            pstate["sq"] = (i + 1) % 2
            k.op("act", lambda e, c=c, i=i: e.activation(out=sqt[i][:, 0:Tt], in_=xT[:, c, 0:Tt], func=AF.Square),
                 R=[b_xT[c]], W=[b_sqt[i]])
            k.op("pe", lambda e, c=c, i=i, ps=ps: e.matmul(ps[:, 0:Tt], onesb[:], sqt[i][:, 0:Tt], start=(c == 0),
                                                           stop=(c == DC - 1)), R=[b_sqt[i], b_setup], W=[bps])
        k.op("act", lambda e, ps=ps: e.activation(out=rstd[:, 0:Tt], in_=ps[:, 0:Tt], func=AF.Ln, bias=EPS,
                                                  scale=1.0 / D), R=[bps], W=[b_rstd])
        k.op("act", lambda e: e.activation(out=rstd[:, 0:Tt], in_=rstd[:, 0:Tt], func=AF.Exp, scale=-0.5),
             R=[b_rstd], W=[b_rstd])

    def norm_to_hT(gi, Tt):
        rms_stats(Tt)
        for c in range(DC):
            k.op("dve", lambda e, c=c: e.scalar_tensor_tensor(out=hT[:, c, 0:Tt], in0=xT[:, c, 0:Tt],
                                                              scalar=gcols[:, gi, c:c + 1], in1=rstd[:, 0:Tt],
                                                              op0=ALU.mult, op1=ALU.mult),
                 R=[b_xT[c], b_rstd, b_setup], W=[b_hT[c]])

    def proj_fm(wname, col0, nch, rhs_fn, nkc, rbufs, row0=0):
        res = []
        for t0 in range(0, nch, 2):
            n = min(2, nch - t0)
            wt, bw = wtile([(wname, row0, nkc, col0 + t0 * 128, n * 128, 0)])
            for j in range(n):
                ps, bps = PS()
                for kc in range(nkc):
                    k.op("pe", lambda e, kc=kc, j=j, ps=ps, wt=wt: e.matmul(
                        ps[:, 0:rhs_fn(kc).shape[-1]], wt[:, kc, j * 128:(j + 1) * 128], rhs_fn(kc),
                        start=(kc == 0), stop=(kc == nkc - 1)), R=[bw] + ([b_hT[kc]] if rbufs is HT else rbufs),
                         W=[bps])
                res.append((t0 + j, ps, bps))
            wrel()
        return res

    def ffn(wgu, wdown, gi, Tt):
        norm_to_hT(gi, Tt)
        for jq in range(FC // 4):
            sgs = []
            for part, col0 in (("g", jq * 512), ("u", FF + jq * 512)):
                banks = [PS() for _ in range(4)]
                for kp in range(2):
                    wt, bw = wtile([(wgu, kp * 1024, 8, col0, 512, 0, 512)])
                    for j in range(4):
                        ps, bps = banks[j]
                        for kc in range(8):
                            kk = kp * 8 + kc
                            k.op("pe", lambda e, kc=kc, kk=kk, j=j, ps=ps, wt=wt, kp=kp: e.matmul(
                                ps[:, 0:Tt], wt[:, kc, j * 128:(j + 1) * 128], hT[:, kk, 0:Tt],
                                start=(kk == 0), stop=(kk == DC - 1)), R=[bw, b_hT[kk]], W=[bps])
                    wrel()
                for j in range(4):
                    ps, bps = banks[j]
                    if part == "g":
                        ta, bta = TA()
                        k.op("act", lambda e, ps=ps, ta=ta: e.activation(out=ta[:, 0:Tt], in_=ps[:, 0:Tt],
                                                                         func=AF.Silu), R=[bps], W=[bta])
                        sgs.append((ta, bta))
                    else:
                        ta, bta = sgs[j]
                        hc = jq * 4 + j
                        k.op("dve", lambda e, ps=ps, ta=ta, hc=hc: e.tensor_tensor(out=RR[:, hc, 0:Tt], in0=ps[:, 0:Tt],
                                                                                  in1=ta[:, 0:Tt], op=ALU.mult),
                             R=[bps, bta], W=[b_RR[hc]])
        kps = [(0, 8), (8, 8), (16, 8), (24, 8), (32, 8), (40, 4)]
        for og in range(DC // 4):
            banks = [PS() for _ in range(4)]
            for pi, (k0, nk) in enumerate(kps):
                wt, bw = wtile([(wdown, k0 * 128, nk, og * 512, 512, 0, 512)])
                for j in range(4):
                    ps, bps = banks[j]
                    for kc in range(nk):
                        k.op("pe", lambda e, kc=kc, j=j, ps=ps, wt=wt, k0=k0, pi=pi, nk=nk: e.matmul(
                            ps[:, 0:Tt], wt[:, kc, j * 128:(j + 1) * 128], RR[:, k0 + kc, 0:Tt],
                            start=(pi == 0 and kc == 0), stop=(pi == len(kps) - 1 and kc == nk - 1)),
                             R=[bw, b_RR[k0 + kc]], W=[bps])
                wrel()
            for j in range(4):
                ps, bps = banks[j]
                c = og * 4 + j
                k.op("dve", lambda e, ps=ps, c=c: e.scalar_tensor_tensor(out=xT[:, c, 0:Tt], in0=ps[:, 0:Tt],
                                                                         scalar=0.5, in1=xT[:, c, 0:Tt],
                                                                         op0=ALU.mult, op1=ALU.add),
                     R=[bps, b_xT[c]], W=[b_xT[c]])

    def dt_path(Tt, NSt, off, nq, pre):
        wt, bw = wtile([("w_in", 0, DC, OFF_DT, 64, 0)])
        pss = []
        for s in range(NSt):
            ps, bps = PS()
            for kc in range(DC):
                k.op("pe", lambda e, kc=kc, ps=ps, s=s, wt=wt: e.matmul(ps[:, 0:64], hT[:, kc, s * 128:(s + 1) * 128],
                                                                        wt[:, kc, 0:64], start=(kc == 0),
                                                                        stop=(kc == DC - 1)), R=[bw, b_hT[kc]], W=[bps])
            pss.append((ps, bps))
        wrel()
        for s in range(NSt):
            ps, bps = pss[s]
            d_ = dtt[:, s]
            bd = b_dtt[s]
            k.op("dve", lambda e, ps=ps, d_=d_: e.tensor_tensor(out=d_[:, 4, :], in0=ps[:, 0:64], in1=hv[:, 0, :],
                                                                op=ALU.add), R=[bps, b_setup], W=[bd])
            k.op("dve", lambda e, d_=d_: e.scalar_tensor_tensor(out=d_[:, 5, :], in0=d_[:, 4, :], scalar=-1.0,
                                                                in1=d_[:, 4, :], op0=ALU.mult, op1=ALU.max),
                 R=[bd], W=[bd])
            k.op("act", lambda e, d_=d_: e.activation(out=d_[:, 5, :], in_=d_[:, 5, :], func=AF.Exp, scale=-1.0),
                 R=[bd], W=[bd])
            k.op("act", lambda e, d_=d_: e.activation(out=d_[:, 5, :], in_=d_[:, 5, :], func=AF.Ln, bias=1.0,
                                                      scale=1.0), R=[bd], W=[bd])
            k.op("dve", lambda e, d_=d_: e.scalar_tensor_tensor(out=d_[:, 0, :], in0=d_[:, 4, :], scalar=0.0,
                                                                in1=d_[:, 5, :], op0=ALU.max, op1=ALU.add),
                 R=[bd], W=[bd])
            k.op("dve", lambda e, d_=d_: e.tensor_tensor(out=d_[:, 1, :], in0=d_[:, 0, :], in1=hv[:, 1, :],
                                                         op=ALU.mult), R=[bd, b_setup], W=[bd])
            if not pre:
                ps2, bps2 = PS()
                k.op("pe", lambda e, ps2=ps2, d_=d_: e.matmul(ps2[:, 0:64], cv(off, "triT"), d_[:, 1, :], start=True,
                                                              stop=True), R=[bd, b_consts], W=[bps2])
                k.op("act", lambda e, ps2=ps2, d_=d_: e.activation(out=d_[:, 2, :], in_=ps2[:, 0:64], func=AF.Exp),
                     R=[bps2], W=[bd])
            ps3, bps3 = PS()
            k.op("pe", lambda e, ps3=ps3, d_=d_: e.matmul(ps3[:, 0:64], cv(off, "SL"), d_[:, 1, :], start=True,
                                                          stop=True), R=[bd, b_consts], W=[bps3])
            k.op("act", lambda e, ps3=ps3, d_=d_: e.activation(out=d_[:, 3, :], in_=ps3[:, 0:64], func=AF.Exp),
                 R=[bps3], W=[bd])
            ps4, bps4 = PS()
            for q in range(nq):
                so = cv(off, "seqones")[:, q * 128:(q + 1) * 128]
                k.op("pe", lambda e, ps4=ps4, d_=d_, q=q, so=so: e.matmul(ps4[:, q * 64:(q + 1) * 64], so,
                                                                           d_[:, 1, :], start=True, stop=True),
                     R=[bd, b_consts], W=[bps4])
            k.op("act", lambda e, ps4=ps4, s=s: e.activation(
                out=decb[:, s, 0:nq, :], in_=ps4[:, 0:nq * 64].rearrange("p (q h) -> p q h", q=nq), func=AF.Exp),
                 R=[bps4], W=[b_decb[s]])

    def silu_exp(src, bsrc, out_ap, bout, ncol, bias_col=None, nbias_col=None):
        te, bte = TA()
        if nbias_col is None:
            k.op("act", lambda e: e.activation(out=te[:, 0:ncol], in_=src, func=AF.Exp, scale=-1.0), R=[bsrc], W=[bte])
        else:
            k.op("act", lambda e: e.activation(out=te[:, 0:ncol], in_=src, func=AF.Exp, scale=-1.0, bias=nbias_col),
                 R=[bsrc, b_setup], W=[bte])
        k.op("act", lambda e: e.activation(out=te[:, 0:ncol], in_=te[:, 0:ncol], func=AF.Ln, bias=1.0, scale=1.0),
             R=[bte], W=[bte])
        k.op("act", lambda e: e.activation(out=te[:, 0:ncol], in_=te[:, 0:ncol], func=AF.Exp, scale=-1.0),
             R=[bte], W=[bte])
        if bias_col is None:
            k.op("dve", lambda e: e.tensor_tensor(out=out_ap, in0=src, in1=te[:, 0:ncol], op=ALU.mult),
                 R=[bsrc, bte], W=[bout])
        else:
            k.op("dve", lambda e: e.scalar_tensor_tensor(out=out_ap, in0=src, scalar=bias_col, in1=te[:, 0:ncol],
                                                         op0=ALU.add, op1=ALU.mult), R=[bsrc, bte, b_setup], W=[bout])

    def rope(pa, ba, pb, bb, Tt, out1, out2, wbufs):
        t1, b1 = TA()
        t2, b2 = TA()
        k.op("dve", lambda e: e.tensor_tensor(out=t1[:, 0:Tt], in0=pa[:, 0:Tt], in1=ropec[:, 0:Tt], op=ALU.mult),
             R=[ba, b_rope], W=[b1])
        k.op("dve", lambda e: e.tensor_tensor(out=t2[:, 0:Tt], in0=pb[:, 0:Tt], in1=ropes[:, 0:Tt], op=ALU.mult),
             R=[bb, b_rope], W=[b2])
        k.op("dve", lambda e: e.tensor_tensor(out=out1, in0=t1[:, 0:Tt], in1=t2[:, 0:Tt], op=ALU.subtract),
             R=[b1, b2], W=wbufs)
        t3, b3 = TA()
        k.op("dve", lambda e: e.tensor_tensor(out=t3[:, 0:Tt], in0=pb[:, 0:Tt], in1=ropec[:, 0:Tt], op=ALU.mult),
             R=[bb, b_rope], W=[b3])
        k.op("dve", lambda e: e.tensor_tensor(out=t1[:, 0:Tt], in0=pa[:, 0:Tt], in1=ropes[:, 0:Tt], op=ALU.mult),
             R=[ba, b_rope], W=[b1])
        k.op("dve", lambda e: e.tensor_tensor(out=out2, in0=t3[:, 0:Tt], in1=t1[:, 0:Tt], op=ALU.add),
             R=[b3, b1], W=wbufs)

    def ret_w(h, u, Tt, NSt, off, nq, pre):
        V = RV[u]
        qT, bq = V["qT"]
        kT, bk = V["kT"]
        qdT, bqd = V["qdT"]
        kdtm, bkd = V["kdtm"]
        vtm, bv = V["vtm"]
        grgs, bgr = V["grgs"]
        brgs, bbr = V["brgs"]
        rh = lambda kc: hT[:, kc, 0:Tt]
        names = ["k"] if pre else ["q", "k"]
        for nm in names:
            wt, bw = wtile([("w_in", 0, DC, (OFF_Q if nm == "q" else OFF_K) + h * 256, 256, 0)])
            pp = []
            for dh in range(2):
                ps, bps = PS()
                for kc in range(DC):
                    k.op("pe", lambda e, kc=kc, dh=dh, ps=ps, wt=wt: e.matmul(ps[:, 0:Tt], wt[:, kc, dh * 128:(dh + 1) * 128],
                                                                              rh(kc), start=(kc == 0),
                                                                              stop=(kc == DC - 1)),
                         R=[bw, b_hT[kc]], W=[bps])
                pp.append((ps, bps))
            if nm == "q":
                rope(pp[0][0], pp[0][1], pp[1][0], pp[1][1], Tt, qdT[:, 0, 0:Tt], qdT[:, 1, 0:Tt], [bqd])
                k.op("act", lambda e: e.copy(out=qT[:, :, 0:Tt], in_=qdT[:, :, 0:Tt]), R=[bqd], W=[bq])
                if nq == 1:
                    for dh in range(2):
                        k.op("dve", lambda e, dh=dh: e.tensor_tensor(
                            out=qdT[:, dh, 0:Tt].rearrange("p (s i) -> p s i", s=NSt),
                            in0=qdT[:, dh, 0:Tt].rearrange("p (s i) -> p s i", s=NSt),
                            in1=cv(off, "qdec")[:, h * 128:(h + 1) * 128].unsqueeze(1).to_broadcast([128, NSt, 128]),
                            op=ALU.mult), R=[bqd, b_consts, bq], W=[bqd])
                else:
                    k.op("dve", lambda e: e.tensor_tensor(
                        out=qdT[:, :, 0:128], in0=qdT[:, :, 0:128],
                        in1=cv(off, "qdec")[:, h * 128:(h + 1) * 128].unsqueeze(1).to_broadcast([128, 2, 128]),
                        op=ALU.mult), R=[bqd, b_consts, bq], W=[bqd])
            else:
                rope(pp[0][0], pp[0][1], pp[1][0], pp[1][1], Tt, kT[:, 0, 0:Tt], kT[:, 1, 0:Tt], [bk])
            wrel()
            yield
        vps = [PS() for s in range(NSt)]
        for half in range(2):
            wt, bw = wtile([("w_in", 0, DC, OFF_V + h * 512 + half * 256, 256, 0)])
            for s in range(NSt):
                ps, bps = vps[s]
                for kc in range(DC):
                    k.op("pe", lambda e, kc=kc, s=s, ps=ps, wt=wt, half=half: e.matmul(
                        ps[:, half * 256:(half + 1) * 256], hT[:, kc, s * 128:(s + 1) * 128], wt[:, kc, 0:256],
                        start=(kc == 0), stop=(kc == DC - 1)), R=[bw, b_hT[kc]], W=[bps])
            wrel()
        for s in range(NSt):
            ps, bps = vps[s]
            k.op("act", lambda e, s=s, ps=ps: e.copy(out=vtm[:, s, :], in_=ps[:]), R=[bps], W=[bv])
        yield
        for t0 in ((0, 2) if not pre else ()):
            for (j, ps, bps) in proj_fm("w_in", OFF_RG + h * 512 + t0 * 128, 2, rh, DC, HT):
                ec = t0 + j
                tb, btb = TB()
                silu_exp(ps[:, 0:Tt], bps, tb[:, 0:Tt], btb, Tt)
                k.op("dve", lambda e, ec=ec, tb=tb: e.tensor_scalar(out=grgs[:, ec, 0:Tt], in0=tb[:, 0:Tt],
                                                                     scalar1=gncols[:, 0, h * 4 + ec:h * 4 + ec + 1],
                                                                     scalar2=None, op0=ALU.mult),
                     R=[btb, b_setup], W=[bgr])
                k.op("dve", lambda e, ec=ec, tb=tb: e.tensor_scalar(out=brgs[:, ec, 0:Tt], in0=tb[:, 0:Tt],
                                                                     scalar1=gncols[:, 1, h * 4 + ec:h * 4 + ec + 1],
                                                                     scalar2=None, op0=ALU.mult),
                     R=[btb, b_setup], W=[bbr])
            yield
        for s in range(NSt):
            ps, bps = PS()
            psb = ps[:].bitcast(BF16)
            for dh in range(2):
                k.op("pe", lambda e, dh=dh, s=s, psb=psb: e.transpose(psb[:, dh * 128:(dh + 1) * 128],
                                                                      kT[:, dh, s * 128:(s + 1) * 128], identb[:]),
                     R=[bk, b_setup], W=[bps])
            for q in range(nq):
                k.op("dve", lambda e, s=s, q=q, psb=psb: e.tensor_scalar(
                    out=kdtm[:, s + q, :], in0=psb[:, 0:256], scalar1=cv(off, "kdecq")[:, q * 8 + h:q * 8 + h + 1],
                    scalar2=None, op0=ALU.mult), R=[bps, b_consts], W=[bkd])
        yield

    def ret_chunk(h, u, s, Tt, off, cdec, nq, pre, smp):
        V = RV[u]
        qT, bq = V["qT"]
        kT, bk = V["kT"]
        qdT, bqd = V["qdT"]
        kdtm, bkd = V["kdtm"]
        vtm, bv = V["vtm"]
        grgs, bgr = V["grgs"]
        brgs, bbr = V["brgs"]
        sl = slice(s * 128, (s + 1) * 128)
        i = (h * 2 + s) % 2
        if not pre:
            ps, bps = PS()
            for dh in range(2):
                k.op("pe", lambda e, dh=dh, ps=ps: e.matmul(ps[:, 0:128], kT[:, dh, sl], qT[:, dh, sl],
                                                            start=(dh == 0), stop=(dh == 1)), R=[bk, bq], W=[bps])
            k.op("dve", lambda e, ps=ps: e.tensor_tensor(out=attm[i][:], in0=ps[:, 0:128],
                                                         in1=cv(off, "dmatT")[:, h * 128:(h + 1) * 128], op=ALU.mult),
                 R=[bps, b_consts], W=[b_attm[i]])
            yield
        if smp:
            _ret_chunk_smp(h, u, s, off, cdec, nq, V, i)
            yield
            return
        if not pre:
            po, bpo = PS()
            k.op("pe", lambda e: e.matmul(po[:], attm[i][:], vtm[:, s, :], start=True, stop=False),
                 R=[b_attm[i], bv], W=[bpo])
            for dh in range(2):
                k.op("pe", lambda e, dh=dh: e.matmul(po[:], qdT[:, dh, sl], S_ret[:, h, dh, :], start=False,
                                                     stop=(dh == 1)), R=[bqd, b_Sret[h][dh]], W=[bpo])
        for dh in range(2):
            ps, bps = PS()
            k.op("pe", lambda e, dh=dh, ps=ps: e.matmul(ps[:], kdtm[:, s, dh * 128:(dh + 1) * 128], vtm[:, s, :],
                                                        start=True, stop=True), R=[bkd, bv], W=[bps])
            k.op("dve", lambda e, dh=dh, ps=ps: e.scalar_tensor_tensor(out=S_ret[:, h, dh, :], in0=S_ret[:, h, dh, :],
                                                                       scalar=cdec[h], in1=ps[:], op0=ALU.mult,
                                                                       op1=ALU.add),
                 R=[bps, b_Sret[h][dh]], W=[b_Sret[h][dh]])
        if not pre:
            ret_post1(h, s, po, bpo)
            yield
            ret_post2(h, s, grgs, bgr, brgs, bbr)
        yield

    def ret_post(h, s, po, bpo, grgs, bgr, brgs, bbr):
        ret_post1(h, s, po, bpo)
        ret_post2(h, s, grgs, bgr, brgs, bbr)

    def ret_post1(h, s, po, bpo):
        i = (h * 2 + s) % 2
        st, bst = stat[i], b_stat[i]
        k.op("dve", lambda e: e.bn_stats(out=st[:, 0:6], in_=po[:]), R=[bpo], W=[bst])
        k.op("dve", lambda e: e.bn_aggr(out=st[:, 6:8], in_=st[:, 0:6]), R=[bst], W=[bst])
        k.op("act", lambda e: e.activation(out=st[:, 8:9], in_=st[:, 7:8], func=AF.Ln, bias=EPS, scale=1.0),
             R=[bst], W=[bst])
        k.op("act", lambda e: e.activation(out=st[:, 8:9], in_=st[:, 8:9], func=AF.Exp, scale=-0.5),
             R=[bst], W=[bst])
        tb, btb = onh[i], b_onh[i]
        k.op("dve", lambda e: e.tensor_scalar(out=tb[:], in0=po[:], scalar1=st[:, 6:7], scalar2=st[:, 8:9],
                                              op0=ALU.subtract, op1=ALU.mult), R=[bpo, bst], W=[btb])

    def ret_post2(h, s, grgs, bgr, brgs, bbr):
        sl = slice(s * 128, (s + 1) * 128)
        i = (h * 2 + s) % 2
        tb, btb = onh[i], b_onh[i]
        pt, bpt = PS()
        ptb = pt[:].bitcast(BF16)
        for ec in range(4):
            k.op("pe", lambda e, ec=ec: e.transpose(ptb[:, ec * 128:(ec + 1) * 128], tb[:, ec * 128:(ec + 1) * 128],
                                                    identb[:]), R=[btb, b_setup], W=[bpt])
        tb2, btb2 = TB()
        k.op("dve", lambda e: e.tensor_tensor(out=tb2[:].rearrange("p (a b) -> p a b", a=4),
                                              in0=ptb[:, 0:512].rearrange("p (a b) -> p a b", a=4),
                                              in1=grgs[:, :, sl], op=ALU.mult), R=[bpt, bgr], W=[btb2])
        k.op("dve", lambda e: e.tensor_tensor(out=RR[:, h * 4:h * 4 + 4, sl],
                                               in0=tb2[:].rearrange("p (a b) -> p a b", a=4), in1=brgs[:, :, sl],
                                               op=ALU.add), R=[btb2, bbr], W=b_RR[h * 4:h * 4 + 4])

    def _ret_chunk_smp(h, u, s, off, cdec, nq, V, i):
        qT, bq = V["qT"]
        qdT, bqd = V["qdT"]
        kdtm, bkd = V["kdtm"]
        vtm, bv = V["vtm"]
        grgs, bgr = V["grgs"]
        brgs, bbr = V["brgs"]
        po, bpo = PS(pin=True)
        k.op("pe", lambda e: e.matmul(po[:], attm[i][:], vtm[:, 0, :], start=True, stop=False),
             R=[b_attm[i], bv], W=[bpo])
        for q in range(nq):
            j = q % 2
            k.dma([(s0buf[j][:], I["sret_in"][q, h].rearrange("(dh p) e -> p dh e", p=128))], W=[b_s0buf[j]])
            k.op("dve", lambda e, q=q, j=j: e.tensor_tensor(
                out=qmk[j][:], in0=qdT[:, :, 0:128],
                in1=cv(off, "colmask")[:, q * 128:(q + 1) * 128].unsqueeze(1).to_broadcast([128, 2, 128]),
                op=ALU.mult), R=[bqd, b_consts], W=[b_qmk[j]])
            for dh in range(2):
                k.op("pe", lambda e, dh=dh, j=j, q=q: e.matmul(po[:], qmk[j][:, dh, :], s0buf[j][:, dh, :], start=False,
                                                               stop=(q == nq - 1 and dh == 1)),
                     R=[b_qmk[j], b_s0buf[j]], W=[bpo])
            for dh in range(2):
                ps, bps = PS()
                k.op("pe", lambda e, dh=dh, ps=ps, q=q: e.matmul(ps[:], kdtm[:, q, dh * 128:(dh + 1) * 128],
                                                                 vtm[:, 0, :], start=True, stop=True),
                     R=[bkd, bv], W=[bps])
                k.op("dve", lambda e, dh=dh, ps=ps, j=j: e.scalar_tensor_tensor(
                    out=s1buf[j][:, dh, :], in0=s0buf[j][:, dh, :], scalar=cdec[h], in1=ps[:], op0=ALU.mult,
                    op1=ALU.add), R=[bps, b_s0buf[j]], W=[b_s1buf[j]])
            k.dma([(O["ret_smp"][q, h].rearrange("(dh p) e -> p dh e", p=128), s1buf[j][:])], R=[b_s1buf[j]],
                  is_out=True)
        ret_post(h, s, po, bpo, grgs, bgr, brgs, bbr)
        unpin(bpo)

    def ssm_w(g, u, Tt, NSt, off, nq, pre, L, nseg, do_c=True):
        V = SV[u]
        zs, bzs = V["zs"]
        xpad, bxp = V["xpad"]
        bcpad, bbc = V["bcpad"]
        xcT, bxc = V["xcT"]
        BT, bBT = V["BT"]
        CT, bCT = V["CT"]
        xtm, bxt = V["xtm"]
        Btm, bBt = V["Btm"]
        xdt, bxd = V["xdt"]
        rh = lambda kc: hT[:, kc, 0:Tt]
        LP = L + 3
        if not pre:
            zps = [PS() for s in range(NSt)]
            for half in range(2):
                wt, bw = wtile([("w_in", 0, DC, OFF_Z + g * 512 + half * 256, 256, 0)])
                for s in range(NSt):
                    ps, bps = zps[s]
                    for kc in range(DC):
                        k.op("pe", lambda e, kc=kc, s=s, ps=ps, wt=wt, half=half: e.matmul(
                            ps[:, half * 256:(half + 1) * 256], hT[:, kc, s * 128:(s + 1) * 128], wt[:, kc, 0:256],
                            start=(kc == 0), stop=(kc == DC - 1)), R=[bw, b_hT[kc]], W=[bps])
                wrel()
            for s in range(NSt):
                ps, bps = zps[s]
                silu_exp(ps[:], bps, zs[:, s, :], bzs, 512)
            yield

        def preconv(ps, bps, cch, padv, bpad):
            pv = padv.rearrange("p (q l) -> p q l", q=nseg)
            k.op("act", lambda e: e.copy(out=pv[:, :, 0:3], in_=conv_hist[:, cch, 0:nseg, :]),
                 R=[b_chist[cch]], W=[bpad])
            k.op("act", lambda e: e.copy(out=pv[:, :, 3:LP], in_=ps[:, 0:Tt].rearrange("p (q l) -> p q l", q=nseg)),
                 R=[bps], W=[bpad])
            dst = conv_new if nseg > 1 else conv_hist
            bdst = b_cnew if nseg > 1 else b_chist
            k.op("act", lambda e: e.copy(out=dst[:, cch, 0:nseg, :],
                                         in_=ps[:, 0:Tt].rearrange("p (q l) -> p q l", q=nseg)[:, :, L - 3:L]),
                 R=[bps, bpad], W=[bdst[cch]])

        def conv_taps(cch, padv, bpad):
            pv = padv.rearrange("p (q l) -> p q l", q=nseg)
            ta, bta = TA()
            tv = ta[:, 0:Tt].rearrange("p (q l) -> p q l", q=nseg)
            k.op("act", lambda e: e.activation(out=tv, in_=pv[:, :, 0:L], func=AF.Identity,
                                               scale=cwcols[:, 0, cch:cch + 1], bias=cwcols[:, 4, cch:cch + 1]),
                 R=[bpad, b_setup], W=[bta])
            for w in range(1, 4):
                k.op("dve", lambda e, w=w: e.scalar_tensor_tensor(out=tv, in0=pv[:, :, w:w + L],
                                                                  scalar=cwcols[:, w, cch:cch + 1], in1=tv,
                                                                  op0=ALU.mult, op1=ALU.add),
                     R=[bpad, b_setup, bta], W=[bta])
            return ta, bta

        def conv_act(cch, ta, bta):
            te, bte = TA()
            k.op("act", lambda e: e.activation(out=te[:, 0:Tt], in_=ta[:, 0:Tt], func=AF.Exp, scale=-1.0),
                 R=[bta], W=[bte])
            k.op("act", lambda e: e.activation(out=te[:, 0:Tt], in_=te[:, 0:Tt], func=AF.Ln, bias=1.0, scale=1.0),
                 R=[bte], W=[bte])
            k.op("act", lambda e: e.activation(out=te[:, 0:Tt], in_=te[:, 0:Tt], func=AF.Exp, scale=-1.0),
                 R=[bte], W=[bte])
            return te, bte

        def conv_fin(cch, ta, bta, te, bte, out_ap, bout):
            k.op("dve", lambda e: e.tensor_tensor(out=out_ap, in0=ta[:, 0:Tt], in1=te[:, 0:Tt], op=ALU.mult),
                 R=[bta, bte], W=[bout])

        def conv_group(items):
            accs = []
            for (cch, padv, bpad, out_ap, bout) in items:
                accs.append(conv_taps(cch, padv, bpad) if out_ap is not None else None)
            tes = []
            for (cch, padv, bpad, out_ap, bout), acc in zip(items, accs):
                tes.append(conv_act(cch, acc[0], acc[1]) if acc is not None else None)
            for (cch, padv, bpad, out_ap, bout), acc, te in zip(items, accs, tes):
                if acc is not None:
                    conv_fin(cch, acc[0], acc[1], te[0], te[1], out_ap, bout)

        for t0 in (0, 2):
            items = []
            for (j, ps, bps) in proj_fm("w_in", OFF_X + g * 512 + t0 * 128, 2, rh, DC, HT):
                ci = t0 + j
                cch = g * 4 + ci
                preconv(ps, bps, cch, xpad[:, ci, 0:nseg * LP], bxp)
                items.append((cch, xpad[:, ci, 0:nseg * LP], bxp, xcT[:, ci, 0:Tt], bxc))
            conv_group(items)
            yield
        items = []
        for (bi, offc, cch) in [(0, OFF_B, 32 + g)] + ([(1, OFF_C, 40 + g)] if do_c else []):
            (_, ps, bps), = proj_fm("w_in", offc + g * 128, 1, rh, DC, HT)
            preconv(ps, bps, cch, bcpad[:, bi, 0:nseg * LP], bbc)
            if bi == 0:
                items.append((cch, bcpad[:, bi, 0:nseg * LP], bbc, BT[:, 0, 0:Tt], bBT))
            elif not pre:
                items.append((cch, bcpad[:, bi, 0:nseg * LP], bbc, CT[:, 0, 0:Tt], bCT))
        conv_group(items)
        yield
        for s in range(NSt):
            ps, bps = PS()
            psb = ps[:].bitcast(BF16)
            for ci in range(4):
                k.op("pe", lambda e, ci=ci, s=s, psb=psb: e.transpose(psb[:, ci * 128:(ci + 1) * 128],
                                                                      xcT[:, ci, s * 128:(s + 1) * 128], identb[:]),
                     R=[bxc, b_setup], W=[bps])
            k.op("act", lambda e, s=s, psb=psb: e.copy(out=xtm[:, s, :], in_=psb[:, 0:512]), R=[bps], W=[bxt])
            ps2, bps2 = PS()
            k.op("pe", lambda e, s=s, ps2=ps2: e.transpose(ps2[:, 0:128], BT[:, 0, s * 128:(s + 1) * 128], ident[:]),
                 R=[bBT, b_setup], W=[bps2])
            k.op("act", lambda e, s=s, ps2=ps2: e.copy(out=Btm[:, s, :], in_=ps2[:, 0:128]), R=[bps2], W=[bBt])
            yield
        for s in range(NSt):
            k.op("pool", lambda e, s=s: e.tensor_tensor(
                out=xdt[:, s, :].rearrange("p (h q) -> p h q", h=8), in0=xtm[:, s, :].rearrange("p (h q) -> p h q", h=8),
                in1=dtt[:, s, 0, g * 8:(g + 1) * 8].unsqueeze(2).to_broadcast([128, 8, 64]), op=ALU.mult),
                 R=[bxt, b_dtt[s]], W=[bxd])
        if not pre:
            emit_rhsR(g, 0, off)
        yield

    rhs_owner = {"o": None}

    def emit_rhsR(g, s, off):
        rhs_owner["o"] = (g, s)
        k.op("dve", lambda e: e.tensor_tensor(
            out=rhsR[0][:], in0=dtt[:, s, 1, g * 8:(g + 1) * 8].unsqueeze(2).to_broadcast([128, 8, 128]),
            in1=cv(off, "triT").unsqueeze(1).to_broadcast([128, 8, 128]), op=ALU.mult),
             R=[b_dtt[s], b_consts], W=[b_rhsR[0]])

    def ssm_chunk(g, u, s, Tt, off, nq, pre, smp):
        V = SV[u]
        zs, bzs = V["zs"]
        BT, bBT = V["BT"]
        CT, bCT = V["CT"]
        xtm, bxt = V["xtm"]
        Btm, bBt = V["Btm"]
        xdt, bxd = V["xdt"]
        sl = slice(s * 128, (s + 1) * 128)
        i = (g * 2 + s) % 2
        gs = slice(g * 8, (g + 1) * 8)
        if not pre:
            ps, bps = PS()
            k.op("pe", lambda e: e.matmul(ps[:, 0:128], BT[:, 0, sl], CT[:, 0, sl], start=True, stop=True),
                 R=[bBT, bCT], W=[bps])
            k.op("dve", lambda e: e.tensor_tensor(out=cbm[i][:], in0=ps[:, 0:128], in1=cv(off, "maskT"), op=ALU.mult),
                 R=[bps, b_consts], W=[b_cbm[i]])
            assert dry or rhs_owner["o"] == (g, s), (rhs_owner, g, s)
            for hh in range(2):
                pg, bpg = PS()
                k.op("pe", lambda e, hh=hh, pg=pg: e.matmul(
                    pg[:], cv(off, "SL"), rhsR[i][:, hh * 4:(hh + 1) * 4, :].rearrange("p a b -> p (a b)"),
                    start=True, stop=True), R=[b_rhsR[i], b_consts], W=[bpg])
                k.op("act", lambda e, hh=hh, pg=pg: e.activation(
                    out=wTt[i][:, hh * 4:(hh + 1) * 4, :].rearrange("p a b -> p (a b)"), in_=pg[:], func=AF.Exp),
                     R=[bpg], W=[b_wTt[i]])
            k.op("dve", lambda e: e.tensor_tensor(out=wTt[i][:], in0=wTt[i][:],
                                                  in1=cbm[i][:].unsqueeze(1).to_broadcast([128, 8, 128]), op=ALU.mult),
                 R=[b_wTt[i], b_cbm[i]], W=[b_wTt[i]])
            yield
            if (s + 1) * 128 < Tt:
                emit_rhsR(g, s + 1, off)
            py, bpy = PS(pin=True)
            for hd in range(8):
                k.op("pe", lambda e, hd=hd: e.matmul(py[:, hd * 64:(hd + 1) * 64], wTt[i][:, hd, :],
                                                     xdt[:, s, hd * 64:(hd + 1) * 64], start=True, stop=True),
                     R=[b_wTt[i], bxd], W=[bpy])
        if smp:
            pin, bpin = PS(pin=True)
            for q in range(nq):
                j = q % 2
                k.dma([(s1buf[j][:].rearrange("p a b -> p (a b)")[:, 0:512].rearrange("p (a b) -> p a b", a=4),
                        I["sssm_in"][q, g * 8:(g + 1) * 8].rearrange("h p n -> (h p) n").rearrange(
                            "(a r) n -> r a n", r=128))], W=[b_s1buf[j]])
                pt, bpt = PS()
                for a in range(4):
                    k.op("pe", lambda e, a=a, j=j, pt=pt: e.transpose(
                        pt[:, a * 128:(a + 1) * 128],
                        s1buf[j][:].rearrange("p a b -> p (a b)")[:, a * 128:(a + 1) * 128], ident[:]),
                         R=[b_s1buf[j], b_setup], W=[bpt])
                k.op("act", lambda e, j=j, pt=pt: e.copy(out=s0buf[j][:, 0, :], in_=pt[:]), R=[bpt], W=[b_s0buf[j]])
                k.op("dve", lambda e, j=j, q=q: e.tensor_tensor(out=qmk[j][:, 0, :], in0=CT[:, 0, 0:128],
                                                                in1=cv(off, "colmask")[:, q * 128:(q + 1) * 128],
                                                                op=ALU.mult), R=[bCT, b_consts], W=[b_qmk[j]])
                k.op("pe", lambda e, j=j, q=q: e.matmul(pin[:], qmk[j][:, 0, :], s0buf[j][:, 0, :], start=(q == 0),
                                                        stop=(q == nq - 1)), R=[b_qmk[j], b_s0buf[j]], W=[bpin])
                k.op("dve", lambda e, q=q: e.tensor_scalar(out=wendq[:, q, :], in0=dtt[:, 0, 3, :],
                                                            scalar1=cv(off, "rowmask")[:, q:q + 1], scalar2=None,
                                                            op0=ALU.mult), R=[b_dtt[0], b_consts], W=[b_wendq])
                k.op("dve", lambda e, j=j, q=q: e.tensor_tensor(
                    out=xwt[j][:].rearrange("p (h q) -> p h q", h=8), in0=xdt[:, 0, :].rearrange("p (h q) -> p h q", h=8),
                    in1=wendq[:, q, gs].unsqueeze(2).to_broadcast([128, 8, 64]), op=ALU.mult),
                     R=[bxd, b_wendq], W=[b_xwt[j]])
                pst, bpst = PS()
                k.op("pe", lambda e, j=j, pst=pst: e.matmul(pst[:], Btm[:, 0, :], xwt[j][:], start=True, stop=True),
                     R=[bBt, b_xwt[j]], W=[bpst])
                k.op("dve", lambda e, j=j, q=q: e.tensor_tensor(
                    out=s0buf[j][:, 1, :].rearrange("p (h q) -> p h q", h=8),
                    in0=s0buf[j][:, 0, :].rearrange("p (h q) -> p h q", h=8),
                    in1=decb[:, 0, q, gs].unsqueeze(2).to_broadcast([128, 8, 64]), op=ALU.mult),
                     R=[b_s0buf[j], b_decb[0]], W=[b_s0buf[j]])
                k.op("dve", lambda e, j=j, pst=pst: e.tensor_tensor(out=s0buf[j][:, 1, :], in0=s0buf[j][:, 1, :],
                                                                    in1=pst[:], op=ALU.add),
                     R=[bpst, b_s0buf[j]], W=[b_s0buf[j]])
                pt2, bpt2 = PS()
                for a in range(4):
                    k.op("pe", lambda e, a=a, j=j, pt2=pt2: e.transpose(pt2[:, a * 128:(a + 1) * 128],
                                                                        s0buf[j][:, 1, a * 128:(a + 1) * 128],
                                                                        ident[:]),
                         R=[b_s0buf[j], b_setup], W=[bpt2])
                k.op("act", lambda e, j=j, pt2=pt2: e.copy(
                    out=s1buf[j][:].rearrange("p a b -> p (a b)")[:, 512:1024], in_=pt2[:]), R=[bpt2], W=[b_s1buf[j]])
                k.dma([(O["ssm_smp"][q, g * 512:(g + 1) * 512, :].rearrange("(a r) n -> r a n", r=128),
                        s1buf[j][:].rearrange("p a b -> p (a b)")[:, 512:1024].rearrange("p (a b) -> p a b", a=4))],
                      R=[b_s1buf[j]], is_out=True)
        else:
            if not pre:
                pin, bpin = PS(pin=True)
                k.op("pe", lambda e: e.matmul(pin[:], CT[:, 0, sl], S_ssm[:, g, :], start=True, stop=True),
                     R=[bCT, b_Sssm[g]], W=[bpin])
            k.op("dve", lambda e: e.tensor_tensor(
                out=xwt[i][:].rearrange("p (h q) -> p h q", h=8), in0=xdt[:, s, :].rearrange("p (h q) -> p h q", h=8),
                in1=dtt[:, s, 3, gs].unsqueeze(2).to_broadcast([128, 8, 64]), op=ALU.mult),
                 R=[bxd, b_dtt[s]], W=[b_xwt[i]])
            pst, bpst = PS()
            k.op("pe", lambda e: e.matmul(pst[:], Btm[:, s, :], xwt[i][:], start=True, stop=True),
                 R=[bBt, b_xwt[i]], W=[bpst])
            k.op("dve", lambda e: e.tensor_tensor(
                out=S_ssm[:, g, :].rearrange("p (h q) -> p h q", h=8), in0=S_ssm[:, g, :].rearrange("p (h q) -> p h q", h=8),
                in1=decb[:, s, 0, gs].unsqueeze(2).to_broadcast([128, 8, 64]), op=ALU.mult),
                 R=[b_Sssm[g], b_decb[s]], W=[b_Sssm[g]])
            k.op("dve", lambda e: e.tensor_tensor(out=S_ssm[:, g, :], in0=S_ssm[:, g, :], in1=pst[:], op=ALU.add),
                 R=[bpst, b_Sssm[g]], W=[b_Sssm[g]])
        if pre:
            yield
            return
        ta, bta = TA()
        k.op("dve", lambda e: e.tensor_tensor(
            out=ta[:].rearrange("p (h q) -> p h q", h=8), in0=pin[:].rearrange("p (h q) -> p h q", h=8),
            in1=dtt[:, s, 2, gs].unsqueeze(2).to_broadcast([128, 8, 64]), op=ALU.mult),
             R=[bpin, b_dtt[s]], W=[bta])
        k.op("dve", lambda e: e.tensor_tensor(out=ta[:], in0=ta[:], in1=py[:], op=ALU.add), R=[bta, bpy], W=[bta])
        unpin(bpy)
        unpin(bpin)
        ta2, bta2 = TA()
        k.op("pool", lambda e: e.tensor_tensor(
            out=ta2[:].rearrange("p (h q) -> p h q", h=8), in0=xtm[:, s, :].rearrange("p (h q) -> p h q", h=8),
            in1=hv[:, 2, gs].unsqueeze(2).to_broadcast([128, 8, 64]), op=ALU.mult), R=[bxt, b_setup], W=[bta2])
        k.op("dve", lambda e: e.tensor_tensor(out=ta[:], in0=ta[:], in1=ta2[:], op=ALU.add),
             R=[bta, bta2], W=[bta])
        k.op("dve", lambda e: e.tensor_tensor(out=ta[:], in0=ta[:], in1=zs[:, s, :], op=ALU.mult),
             R=[bta, bzs], W=[bta])
        st, bst = stat[i], b_stat[i]
        k.op("dve", lambda e: e.memset(st[:, 10:11], 0.0), W=[bst])
        k.op("act", lambda e: e.activation(out=ta2[:], in_=ta[:], func=AF.Square, accum_out=st[:, 10:11]),
             R=[bta], W=[bta2, bst])
        k.op("act", lambda e: e.activation(out=st[:, 11:12], in_=st[:, 10:11], func=AF.Ln, bias=EPS,
                                           scale=1.0 / 512.0), R=[bst], W=[bst])
        k.op("act", lambda e: e.activation(out=st[:, 11:12], in_=st[:, 11:12], func=AF.Exp, scale=-0.5),
             R=[bst], W=[bst])
        tb, btb = onh[i], b_onh[i]
        k.op("act", lambda e: e.activation(out=tb[:], in_=ta[:], func=AF.Copy, scale=st[:, 11:12]),
             R=[bta, bst], W=[btb])
        yield
        pt, bpt = PS()
        ptb = pt[:].bitcast(BF16)
        for ci in range(4):
            k.op("pe", lambda e, ci=ci: e.transpose(ptb[:, ci * 128:(ci + 1) * 128], tb[:, ci * 128:(ci + 1) * 128],
                                                    identb[:]), R=[btb, b_setup], W=[bpt])
        for ci in range(4):
            c = g * 4 + ci
            k.op("act", lambda e, ci=ci, c=c: e.activation(out=RR[:, c, sl], in_=ptb[:, ci * 128:(ci + 1) * 128],
                                                           func=AF.Copy, scale=gncols[:, 2, c:c + 1]),
                 R=[bpt, b_setup], W=[b_RR[c]])
        yield

    def merge(which, Tt):
        wbr, offg = ("w_br_ret", OFF_GR) if which == 0 else ("w_br_ssm", OFF_GS)
        for cp in range(DC // 2):
            banks = [PS(), PS()]
            for kp in range(2):
                wt, bw = wtile([(wbr, kp * 2048, 16, cp * 256, 256, 0)])
                for j in range(2):
                    ps, bps = banks[j]
                    for kc in range(16):
                        k.op("pe", lambda e, kc=kc, j=j, ps=ps, wt=wt, kp=kp: e.matmul(
                            ps[:, 0:Tt], wt[:, kc, j * 128:(j + 1) * 128], RR[:, kp * 16 + kc, 0:Tt],
                            start=(kp == 0 and kc == 0), stop=(kp == 1 and kc == 15)),
                             R=[bw, b_RR[kp * 16 + kc]], W=[bps])
                wrel()
            gl = proj_fm("w_in", offg + cp * 256, 2, lambda kc: hT[:, kc, 0:Tt], DC, HT)
            for j in range(2):
                ps, bps = banks[j]
                (_, pg, bpg) = gl[j]
                c = cp * 2 + j
                ta, bta = TA()
                k.op("act", lambda e, pg=pg, ta=ta: e.activation(out=ta[:, 0:Tt], in_=pg[:, 0:Tt], func=AF.Sigmoid),
                     R=[bpg], W=[bta])
                if which == 0:
                    k.op("dve", lambda e, ps=ps, ta=ta, c=c: e.tensor_tensor(out=RR[:, 32 + c, 0:Tt], in0=ps[:, 0:Tt],
                                                                            in1=ta[:, 0:Tt], op=ALU.mult),
                         R=[bps, bta], W=[b_RR[32 + c]])
                else:
                    k.op("dve", lambda e, ps=ps, ta=ta: e.tensor_tensor(out=ta[:, 0:Tt], in0=ps[:, 0:Tt],
                                                                       in1=ta[:, 0:Tt], op=ALU.mult),
                         R=[bps, bta], W=[bta])
                    k.op("dve", lambda e, ta=ta, c=c: e.tensor_tensor(out=RR[:, 32 + c, 0:Tt], in0=ta[:, 0:Tt],
                                                                      in1=RR[:, 32 + c, 0:Tt], op=ALU.add),
                         R=[bta, b_RR[32 + c]], W=[b_RR[32 + c]])

    def out_proj(Tt):
        for cp in range(DC // 2):
            for (j, ps, bps) in proj_fm("w_out", cp * 256, 2, lambda kc: RR[:, 32 + kc, 0:Tt], DC, b_RR[32:48]):
                c = cp * 2 + j
                k.op("dve", lambda e, ps=ps, c=c: e.tensor_tensor(out=xT[:, c, 0:Tt], in0=ps[:, 0:Tt],
                                                                  in1=xT[:, c, 0:Tt], op=ALU.add),
                     R=[bps, b_xT[c]], W=[b_xT[c]])

    def ple(p_ap, tok0, Tt, NSt):
        norm_to_hT(3, Tt)
        for s in range(NSt):
            k.dma([(pstage[:], p_ap[tok0 + s * 128: tok0 + (s + 1) * 128, :])], W=[b_pstage])
            ps, bps = PS()
            for j in range(2):
                k.op("pe", lambda e, j=j, ps=ps: e.transpose(ps[:, j * 128:(j + 1) * 128],
                                                             pstage[:, j * 128:(j + 1) * 128], ident[:]),
                     R=[b_pstage, b_setup], W=[bps])
            k.op("act", lambda e, s=s, ps=ps: e.copy(out=pT[:, :, s * 128:(s + 1) * 128],
                                                     in_=ps[:, 0:256].rearrange("p (a b) -> p a b", a=2)),
                 R=[bps], W=[b_pT])
        for cp in range(DC // 2):
            gl = proj_fm("w_ple_gate", cp * 256, 2, lambda kc: hT[:, kc, 0:Tt], DC, HT)
            pl = proj_fm("w_ple", cp * 256, 2, lambda kc: pT[:, kc, 0:Tt], 2, [b_pT])
            for j in range(2):
                (_, pg, bpg) = gl[j]
                (_, pp, bpp) = pl[j]
                c = cp * 2 + j
                ta, bta = TA()
                k.op("act", lambda e, pg=pg, ta=ta: e.activation(out=ta[:, 0:Tt], in_=pg[:, 0:Tt], func=AF.Sigmoid),
                     R=[bpg], W=[bta])
                k.op("dve", lambda e, pp=pp, ta=ta: e.tensor_tensor(out=ta[:, 0:Tt], in0=pp[:, 0:Tt], in1=ta[:, 0:Tt],
                                                                   op=ALU.mult), R=[bpp, bta], W=[bta])
                k.op("dve", lambda e, ta=ta, c=c: e.tensor_tensor(out=xT[:, c, 0:Tt], in0=ta[:, 0:Tt],
                                                                  in1=xT[:, c, 0:Tt], op=ALU.add),
                     R=[bta, b_xT[c]], W=[b_xT[c]])

    def final_store(y_ap, tok0, Tt, NSt):
        rms_stats(Tt)
        for s in range(NSt):
            for hf in range(2):
                for cb2 in range(2):
                    cb = hf * 2 + cb2
                    ps, bps = PS()
                    for j in range(4):
                        c = cb * 4 + j
                        ta, bta = TA()
                        k.op("dve", lambda e, c=c, ta=ta, s=s: e.scalar_tensor_tensor(
                            out=ta[:, 0:128], in0=xT[:, c, s * 128:(s + 1) * 128], scalar=gcols[:, 4, c:c + 1],
                            in1=rstd[:, s * 128:(s + 1) * 128], op0=ALU.mult, op1=ALU.mult),
                             R=[b_xT[c], b_rstd, b_setup], W=[bta])
                        k.op("pe", lambda e, j=j, ta=ta, ps=ps: e.transpose(ps[:, j * 128:(j + 1) * 128], ta[:, 0:128],
                                                                            ident[:]), R=[bta, b_setup], W=[bps])
                    k.op("act", lambda e, cb2=cb2, ps=ps: e.copy(out=ystage[:, cb2 * 512:(cb2 + 1) * 512], in_=ps[:]),
                         R=[bps], W=[b_ystage])
                k.dma([(y_ap[tok0 + s * 128: tok0 + (s + 1) * 128, hf * 1024:(hf + 1) * 1024], ystage[:])],
                      R=[b_ystage], is_out=True)

    def load_rope(region, tok0, Tt):
        k.dma([(ropec[:, 0:Tt], I["cos_" + region][:, tok0:tok0 + Tt]),
               (ropes[:, 0:Tt], I["sin_" + region][:, tok0:tok0 + Tt])], W=[b_rope])

    def drain(g):
        for _ in g:
            pass

    def interleave(cg, wg):
        cdone = cg is None
        wdone = wg is None
        while not (cdone and wdone):
            if not wdone:
                try:
                    next(wg)
                except StopIteration:
                    wdone = True
            if not cdone:
                try:
                    next(cg)
                except StopIteration:
                    cdone = True

    def chain(gens):
        for g in gens:
            for _ in g:
                yield

    def mixer(Tt, NSt, off, cdec, nq, pre, smp, L, nseg, do_c=True):
        norm_to_hT(1, Tt)
        dt_path(Tt, NSt, off, nq, pre)
        drain(ret_w(0, 0, Tt, NSt, off, nq, pre))
        for h in range(RH):
            cg = chain([ret_chunk(h, h % 2, s, Tt, off, cdec, nq, pre, smp) for s in range(NSt)])
            if h + 1 < RH:
                wg = ret_w(h + 1, (h + 1) % 2, Tt, NSt, off, nq, pre)
            else:
                wg = ssm_w(0, 0, Tt, NSt, off, nq, pre, L, nseg, do_c)
            interleave(cg, wg)
        if not pre:
            tap("retT")
            merge(0, Tt)
        for g in range(SG):
            cg = chain([ssm_chunk(g, g % 2, s, Tt, off, nq, pre, smp) for s in range(NSt)])
            wg = ssm_w(g + 1, (g + 1) % 2, Tt, NSt, off, nq, pre, L, nseg, do_c) if g + 1 < SG else None
            interleave(cg, wg)
        if not pre:
            tap("ssmT")
            merge(1, Tt)

    def apply_flag():
        for h in range(RH):
            for d in range(2):
                k.op("dve", lambda e, h=h, d=d: e.tensor_scalar(out=S_ret[:, h, d, :], in0=S_ret[:, h, d, :],
                                                                 scalar1=flagt[:, 0:1], scalar2=None, op0=ALU.mult),
                     R=[b_Sret[h][d], b_setup], W=[b_Sret[h][d]])
        for g in range(SG):
            k.op("dve", lambda e, g=g: e.tensor_scalar(out=S_ssm[:, g, :], in0=S_ssm[:, g, :], scalar1=flagt[:, 0:1],
                                                        scalar2=None, op0=ALU.mult),
                 R=[b_Sssm[g], b_setup], W=[b_Sssm[g]])
        for c in range(48):
            k.op("dve", lambda e, c=c: e.tensor_scalar(out=conv_hist[:, c, 0, :], in0=conv_hist[:, c, 0, :],
                                                        scalar1=flagt[:, 0:1], scalar2=None, op0=ALU.mult),
                 R=[b_chist[c], b_setup], W=[b_chist[c]])

    def store_states_main():
        for h in range(RH):
            k.dma([(O["ret_main"][h].rearrange("(dh p) e -> p dh e", p=128), S_ret[:, h, :, :])],
                  R=[b_Sret[h][0], b_Sret[h][1]], is_out=True)
        for g in range(SG):
            j = g % 2
            pt, bpt = PS()
            for a in range(4):
                k.op("pe", lambda e, a=a, pt=pt, g=g: e.transpose(pt[:, a * 128:(a + 1) * 128],
                                                                  S_ssm[:, g, a * 128:(a + 1) * 128], ident[:]),
                     R=[b_Sssm[g], b_setup], W=[bpt])
            k.op("act", lambda e, j=j, pt=pt: e.copy(out=s1buf[j][:, 0, :], in_=pt[:]), R=[bpt], W=[b_s1buf[j]])
            k.dma([(O["ssm_main"][g * 512:(g + 1) * 512, :].rearrange("(a r) n -> r a n", r=128),
                    s1buf[j][:, 0, :].rearrange("p (a b) -> p a b", a=4))], R=[b_s1buf[j]], is_out=True)
        conv_out(conv_hist, b_chist, 1, O["conv_main"])

    def conv_out(src, bsrc, nseg, out_ap):
        n = nseg * 3
        for pc in range(6):
            for cb2 in range(2):
                ps, bps = PS()
                for j in range(4):
                    c = pc * 8 + cb2 * 4 + j
                    k.op("pe", lambda e, j=j, c=c, ps=ps: e.transpose(
                        ps[0:n, j * 128:(j + 1) * 128], src[:, c, 0:nseg, :].rearrange("p q t -> p (q t)"), ident[:]),
                         R=[bsrc[c], b_setup], W=[bps])
                k.op("act", lambda e, cb2=cb2, ps=ps: e.copy(out=ystage[0:n, cb2 * 512:(cb2 + 1) * 512],
                                                             in_=ps[0:n, :]), R=[bps], W=[b_ystage])
            k.dma([(out_ap[:, pc * 1024:(pc + 1) * 1024], ystage[0:n, :])], R=[b_ystage], is_out=True)

    def conv_in_smp():
        for pc in range(6):
            k.dma([(ystage[0:12, :], I["sconv_in"][:, pc * 1024:(pc + 1) * 1024])], W=[b_ystage])
            for cb2 in range(2):
                ps, bps = PS()
                for j in range(4):
                    cl = cb2 * 4 + j
                    k.op("pe", lambda e, j=j, cl=cl, ps=ps: e.transpose(
                        ps[:, j * 12:(j + 1) * 12], ystage[0:12, cl * 128:(cl + 1) * 128], ident[0:12, 0:12]),
                         R=[b_ystage, b_setup], W=[bps])
                for j in range(4):
                    c = pc * 8 + cb2 * 4 + j
                    k.op("act", lambda e, j=j, c=c, ps=ps: e.copy(
                        out=conv_hist[:, c, :, :].rearrange("p q t -> p (q t)"), in_=ps[:, j * 12:(j + 1) * 12]),
                         R=[bps], W=[b_chist[c]])

    tapped = set()

    def tap(name):
        if name not in DEBUG_TAPS or name in tapped:
            return
        tapped.add(name)
        if name in ("retT", "ssmT", "merged", "hidden"):
            k.dma([(O["dbg_" + name], RR[:, :, :])], R=b_RR, q="pool", is_out=True)
        else:
            k.dma([(O["dbg_" + name], xT[:, :, :])], R=b_xT, q="pool", is_out=True)

    cast_all()
    setup()
    load_consts("P")
    for t in range(NPRE):
        load_x(I["x_pre"], t * T, T, NS)
        load_rope("pre", t * T, T)
        ffn("w1_gu", "w1_down", 0, T)
        mixer(T, NS, offP, cdecP, 1, True, False, T, 1, do_c=(t == NPRE - 1))
    apply_flag()
    for t in range(NMAIN):
        load_x(I["x_main"], t * T, T, NS)
        load_rope("main", t * T, T)
        ffn("w1_gu", "w1_down", 0, T)
        tap("x1")
        mixer(T, NS, offP, cdecP, 1, False, False, T, 1)
        tap("merged")
        out_proj(T)
        tap("x2")
        ffn("w2_gu", "w2_down", 2, T)
        tap("x3")
        ple(I["p_main"], t * T, T, NS)
        tap("x4")
        final_store(O["y_main"], t * T, T, NS)
    store_states_main()
    if with_smp:
        load_consts("S")
        conv_in_smp()
        load_x(I["x_smp"], 0, 128, 1)
        load_rope("smp", 0, 128)
        ffn("w1_gu", "w1_down", 0, 128)
        mixer(128, 1, offS, cdecS, 4, False, True, 32, 4)
        out_proj(128)
        ffn("w2_gu", "w2_down", 2, 128)
        ple(I["p_smp"], 0, 128, 1)
        final_store(O["y_smp"], 0, 128, 1)
        conv_out(conv_new, b_cnew, 4, O["conv_smp"])


_NC_CACHE = {}
LAST_RESULTS = None


def run(inputs, NPRE, NMAIN, T=256, with_smp=True, ncores=8):
    f32 = np.float32
    xp = np.asarray(inputs["x_prompt"], f32)
    seq = xp.shape[1]
    half = seq // 2
    assert half == NMAIN * T and (NPRE == NMAIN)
    key = (NPRE, NMAIN, T, with_smp)
    if key not in _NC_CACHE:
        import time as _t
        _t0 = _t.time()
        _NC_CACHE[key] = build(NPRE, NMAIN, T, with_smp)
        print("[kernel] build %.1fs" % (_t.time() - _t0), flush=True)
    nc = _NC_CACHE[key]
    constP = make_consts("P")[0]
    constS = make_consts("S")[0]
    cP = np.zeros((128, CONST_W), f32)
    cP[:, :constP.shape[1]] = constP
    cS = np.zeros((128, CONST_W), f32)
    cS[:, :constS.shape[1]] = constS
    ident = np.eye(128, dtype=f32)
    cos_a, sin_a = rope_tables(np.arange(0, half))
    cos_b, sin_b = rope_tables(np.arange(half, seq))
    cos_s, sin_s = rope_tables(np.tile(PAST + np.arange(32), 4))
    shared = {}
    for n in ("w1_gu", "w1_down", "w_in", "w_br_ret", "w_br_ssm", "w_out", "w2_gu", "w2_down", "w_ple", "w_ple_gate"):
        shared[n] = np.ascontiguousarray(np.asarray(inputs[n], f32)[0])
    def colv(v, nchunk):
        return np.ascontiguousarray(np.asarray(v, f32).reshape(nchunk, 128).T)
    gc = np.stack([colv(inputs[n][0], DC) for n in ("g_ffn1", "g_mix", "g_ffn2", "g_ple")] +
                  [colv(inputs["g_final"], DC)], axis=1)
    shared["gcols"] = np.ascontiguousarray(gc)
    shared["gncols"] = np.ascontiguousarray(np.stack([colv(inputs[n][0], 32) for n in
                                                      ("ret_gn_g", "ret_gn_b", "ssm_norm_g")], axis=1))
    cwv = np.asarray(inputs["conv_w"], f32)[0]
    shared["cwcols"] = np.ascontiguousarray(np.stack([colv(cwv[w], 48) for w in range(4)] +
                                                     [colv(inputs["conv_b"][0], 48)], axis=1))
    shared["hvin"] = np.ascontiguousarray(np.broadcast_to(
        np.stack([np.asarray(inputs[n], f32)[0] for n in ("dt_bias", "a_log", "d_skip")], axis=0)[None], (128, 3, 64)))
    shared["constP"] = cP
    shared["constS"] = cS
    shared["ident"] = ident
    in_maps = []
    for c in range(ncores):
        b, hf = c // 2, c % 2
        m = dict(shared)
        m["x_pre"] = np.ascontiguousarray(xp[b, 0:half])
        m["x_main"] = np.ascontiguousarray(xp[b, hf * half:(hf + 1) * half])
        m["p_main"] = np.ascontiguousarray(np.asarray(inputs["p_prompt"], f32)[0, b, hf * half:(hf + 1) * half])
        m["x_smp"] = np.ascontiguousarray(np.asarray(inputs["x_sample"], f32)[4 * c:4 * c + 4].reshape(128, D))
        m["p_smp"] = np.ascontiguousarray(np.asarray(inputs["p_sample"], f32)[0, 4 * c:4 * c + 4].reshape(128, 256))
        m["sret_in"] = np.ascontiguousarray(np.asarray(inputs["state_ret"], f32)[0, 4 * c:4 * c + 4])
        m["sssm_in"] = np.ascontiguousarray(np.asarray(inputs["state_ssm"], f32)[0, 4 * c:4 * c + 4])
        m["sconv_in"] = np.ascontiguousarray(np.asarray(inputs["state_conv"], f32)[0, 4 * c:4 * c + 4].reshape(12, 6144))
        m["flag"] = np.full((128, 1), float(hf), f32)
        m["cos_pre"], m["sin_pre"] = cos_a, sin_a
        m["cos_main"], m["sin_main"] = (cos_a, sin_a) if hf == 0 else (cos_b, sin_b)
        m["cos_smp"], m["sin_smp"] = cos_s, sin_s
        in_maps.append(m)
    import time as _t
    _t0 = _t.time()
    res = run_bass_kernel_spmd(nc, in_maps, core_ids=list(range(ncores)))
    print("[kernel] spmd launch wall %.1fs" % (_t.time() - _t0), flush=True)
    R = res.results
    global LAST_RESULTS
    LAST_RESULTS = R
    nb = xp.shape[0]
    y_prompt = np.zeros((nb, seq, D), f32)
    ret_p = np.zeros((1, nb, RH, 256, 512), f32)
    ssm_p = np.zeros((1, nb, 64, 64, 128), f32)
    conv_p = np.zeros((1, nb, 3, 6144), f32)
    nsb = np.asarray(inputs["x_sample"]).shape[0]
    y_sample = np.zeros((nsb, 32, D), f32)
    ret_s = np.zeros((1, nsb, RH, 256, 512), f32)
    ssm_s = np.zeros((1, nsb, 64, 64, 128), f32)
    conv_s = np.zeros((1, nsb, 3, 6144), f32)
    for c in range(ncores):
        b, hf = c // 2, c % 2
        y_prompt[b, hf * half:(hf + 1) * half] = R[c]["y_main"]
        if hf == 1:
            ret_p[0, b] = R[c]["ret_main"]
            ssm_p[0, b] = R[c]["ssm_main"].reshape(64, 64, 128)
            conv_p[0, b] = R[c]["conv_main"]
        if with_smp:
            y_sample[4 * c:4 * c + 4] = R[c]["y_smp"].reshape(4, 32, D)
            ret_s[0, 4 * c:4 * c + 4] = R[c]["ret_smp"]
            ssm_s[0, 4 * c:4 * c + 4] = R[c]["ssm_smp"].reshape(4, 64, 64, 128)
            conv_s[0, 4 * c:4 * c + 4] = R[c]["conv_smp"].reshape(4, 3, 6144)
    return (y_prompt, y_sample, ret_p, ssm_p, conv_p, ret_s, ssm_s, conv_s)


def kernel(**inputs):
    return run(inputs, 16, 16, 256, True, 8)
```
